# Optimizing a Trainium2 kernel written in Bass

```python
import math
import jax, jax.numpy as jnp
from jax import lax
import numpy as np

D_MODEL = 1024
BATCH = 32
SEQ = 2048
DEPTH = 4

N_MIXERS = 4
N_S5 = (DEPTH + 3) // 4
N_GLA = (DEPTH + 2) // 4
N_DIFF = (DEPTH + 1) // 4
N_SSD = DEPTH // 4

EXPAND = 2
D_INNER = EXPAND * D_MODEL
EPS = 1e-6

MEM_LEN = 256
X_HEADS = 4
X_HEAD_DIM = 128
D_X = X_HEADS * X_HEAD_DIM
D_GATE = D_INNER + D_X

S5_GROUP = 16
S5_GROUPS = D_INNER // S5_GROUP
S5_STATE = 64
S5_CHUNK = 64

GLA_HEADS = 4
GLA_DK = D_MODEL // 2 // GLA_HEADS
GLA_DV = D_INNER // GLA_HEADS
GLA_RANK = 16
GLA_TAU = 16.0
GLA_CHUNK = 32

DIFF_HEADS = 16
DIFF_HALF = D_INNER // DIFF_HEADS // 2
DIFF_VDIM = 2 * DIFF_HALF
Q_BLOCK = 128
ROPE_THETA = 10000.0
MAX_POS_OFFSET = 4096

SSD_HEAD_DIM = 64
SSD_HEADS = D_INNER // SSD_HEAD_DIM
SSD_GROUPS = 8
SSD_HPG = SSD_HEADS // SSD_GROUPS
SSD_STATE = 128
SSD_CONV = 4
SSD_CONV_CH = D_INNER + 2 * SSD_GROUPS * SSD_STATE
SSD_CHUNK = 64

S5_IN = D_INNER + D_GATE + D_X
GLA_IN = 2 * GLA_HEADS * GLA_DK + D_INNER + GLA_RANK + D_GATE + D_X
DIFF_IN = 3 * D_INNER + D_GATE + D_X
SSD_IN = SSD_CONV_CH + SSD_HEADS + D_GATE + D_X

kernel_name = "hybrid_s5_gla_diff_ssd_memory"

F32 = jnp.float32


def rms_norm(x, g):
    xf = x.astype(F32)
    y = xf * lax.rsqrt(jnp.mean(xf * xf, axis=-1, keepdims=True) + EPS)
    return (y * g.astype(F32)).astype(x.dtype)


def split_cols(t, sizes):
    idx = np.cumsum(np.array(sizes))[:-1].tolist()
    return jnp.split(t, idx, axis=-1)


def rope_tables(positions, dim):
    inv = ROPE_THETA ** (-jnp.arange(0, dim, 2, dtype=F32) / dim)
    ang = positions.astype(F32)[..., None] * inv
    return jnp.cos(ang)[:, :, None, None, :], jnp.sin(ang)[:, :, None, None, :]


def rope(x, cos, sin):
    half = x.shape[-1] // 2
    x1, x2 = x[..., :half], x[..., half:]
    return jnp.concatenate([x1 * cos - x2 * sin, x2 * cos + x1 * sin], axis=-1).astype(x.dtype)


def _complex_affine_combine(e1, e2):
    a1r, a1i, b1r, b1i = e1
    a2r, a2i, b2r, b2i = e2
    ar = a2r * a1r - a2i * a1i
    ai = a2r * a1i + a2i * a1r
    br = a2r * b1r - a2i * b1i + b2r
    bi = a2r * b1i + a2i * b1r + b2i
    return ar, ai, br, bi


def s5_mixer(u, z, lam_re, lam_im, log_step, b_re, b_im, c_re, c_im, d_skip, w_glu):
    bsz, seq, _ = u.shape
    nc = seq // S5_CHUNK
    lam_re = lam_re.astype(F32)
    lam_im = lam_im.astype(F32)
    step = jnp.exp(log_step.astype(F32))[:, None]
    mag = jnp.exp(lam_re * step)
    lb_re = mag * jnp.cos(lam_im * step)
    lb_im = mag * jnp.sin(lam_im * step)
    den = lam_re * lam_re + lam_im * lam_im
    nr = lb_re - 1.0
    co_re = (nr * lam_re + lb_im * lam_im) / den
    co_im = (lb_im * lam_re - nr * lam_im) / den
    b_re = b_re.astype(F32)
    b_im = b_im.astype(F32)
    bb_re = co_re[..., None] * b_re - co_im[..., None] * b_im
    bb_im = co_re[..., None] * b_im + co_im[..., None] * b_re
    c_re = c_re.astype(F32)
    c_im = c_im.astype(F32)
    a_re = jnp.broadcast_to(lb_re, (S5_CHUNK, 1) + lb_re.shape)
    a_im = jnp.broadcast_to(lb_im, (S5_CHUNK, 1) + lb_im.shape)
    ug = u.astype(F32).reshape(bsz, nc, S5_CHUNK, S5_GROUPS, S5_GROUP).transpose(1, 2, 0, 3, 4)

    def chunk_step(carry, u_c):
        h_re, h_im = carry
        dr = jnp.einsum('tbgh,gph->tbgp', u_c, bb_re)
        di = jnp.einsum('tbgh,gph->tbgp', u_c, bb_im)
        pr, pim, sr, si = lax.associative_scan(_complex_affine_combine, (a_re, a_im, dr, di), axis=0)
        sr = sr + pr * h_re - pim * h_im
        si = si + pr * h_im + pim * h_re
        y = jnp.einsum('tbgp,ghp->tbgh', sr, c_re) - jnp.einsum('tbgp,ghp->tbgh', si, c_im)
        return (sr[-1], si[-1]), y

    h0 = jnp.zeros((bsz, S5_GROUPS, S5_STATE), F32)
    _, ys = lax.scan(chunk_step, (h0, h0), ug)
    y = ys.transpose(2, 0, 1, 3, 4).reshape(bsz, seq, D_INNER) + d_skip.astype(F32) * u.astype(F32)
    g = jax.nn.gelu(y)
    out = g * jax.nn.sigmoid(g @ w_glu.astype(F32))
    return (out * jax.nn.silu(z.astype(F32))).astype(u.dtype)


def gla_mixer(q, k, v, gk_low, z, w_gk2, b_gk2, norm_g):
    bsz, seq, _ = v.shape
    nc = seq // GLA_CHUNK
    log_a = jax.nn.log_sigmoid(gk_low.astype(F32) @ w_gk2.astype(F32) + b_gk2.astype(F32)) / GLA_TAU

    def heads(t, d):
        return t.astype(F32).reshape(bsz, nc, GLA_CHUNK, GLA_HEADS, d).transpose(1, 0, 3, 2, 4)

    qc = heads(q, GLA_DK) * GLA_DK ** -0.5
    kc = heads(k, GLA_DK)
    vc = heads(v, GLA_DV)
    gc = heads(log_a, GLA_DK)
    tri = jnp.tril(jnp.ones((GLA_CHUNK, GLA_CHUNK), dtype=bool))

    def step(state, inp):
        qi, ki, vi, gi = inp
        b = jnp.cumsum(gi, axis=2)
        o_inter = jnp.einsum('bhtk,bhkv->bhtv', qi * jnp.exp(b), state)
        rel = jnp.where(tri[:, :, None], b[:, :, :, None, :] - b[:, :, None, :, :], -jnp.inf)
        att = jnp.einsum('bhtk,bhsk,bhtsk->bhts', qi, ki, jnp.exp(rel))
        o_intra = jnp.einsum('bhts,bhsv->bhtv', att, vi)
        b_last = b[:, :, -1]
        new_state = state * jnp.exp(b_last)[..., None] + jnp.einsum(
            'bhsk,bhsv->bhkv', ki * jnp.exp(b_last[:, :, None] - b), vi)
        return new_state, o_inter + o_intra

    s0 = jnp.zeros((bsz, GLA_HEADS, GLA_DK, GLA_DV), F32)
    _, o = lax.scan(step, s0, (qc, kc, vc, gc))
    o = o.transpose(1, 0, 3, 2, 4).reshape(bsz, seq, GLA_HEADS, GLA_DV)
    o = rms_norm(o, norm_g).reshape(bsz, seq, D_INNER)
    return (o * jax.nn.silu(z.astype(F32))).astype(v.dtype)


def diff_attention(q, k, v, z, cos, sin, q_g, k_g, lq1, lk1, lq2, lk2, subln_g, lam_init):
    bsz, seq, _ = q.shape
    shp = (bsz, seq, DIFF_HEADS, 2, DIFF_HALF)
    q = rope(rms_norm(q.reshape(shp), q_g), cos, sin) * (DIFF_HALF ** -0.5)
    k = rope(rms_norm(k.reshape(shp), k_g), cos, sin)
    v = v.reshape(bsz, seq, DIFF_HEADS, DIFF_VDIM)
    lam = (jnp.exp(jnp.sum(lq1.astype(F32) * lk1.astype(F32)))
           - jnp.exp(jnp.sum(lq2.astype(F32) * lk2.astype(F32))) + lam_init)
    nb = seq // Q_BLOCK
    qb = q.reshape((bsz, nb, Q_BLOCK) + shp[2:]).swapaxes(0, 1)
    kpos = jnp.arange(seq)

    def block(args):
        qi, bi = args
        s = jnp.einsum('bqhcd,bkhcd->bhcqk', qi, k).astype(F32)
        qpos = bi * Q_BLOCK + jnp.arange(Q_BLOCK)
        s = jnp.where(kpos[None, :] <= qpos[:, None], s, -jnp.inf)
        p = jax.nn.softmax(s, axis=-1)
        w = p[:, :, 0] - lam * p[:, :, 1]
        return jnp.einsum('bhqk,bkhv->bqhv', w.astype(v.dtype), v)

    o = lax.map(block, (qb, jnp.arange(nb))).swapaxes(0, 1).reshape(bsz, seq, DIFF_HEADS, DIFF_VDIM)
    o = rms_norm(o, subln_g).astype(F32) * (1.0 - lam_init)
    return (o.reshape(bsz, seq, D_INNER) * jax.nn.silu(z.astype(F32))).astype(z.dtype)


def causal_depthwise_conv(x, w, b):
    y = lax.conv_general_dilated(x, w[:, None, :], window_strides=(1,), padding=[(SSD_CONV - 1, 0)],
                                 dimension_numbers=('NWC', 'WIO', 'NWC'), feature_group_count=x.shape[-1])
    return y + b


def ssd_mixer(xbc, dt_raw, z, conv_w, conv_b, dt_bias, a_log, d_skip, norm_g):
    bsz, seq, _ = xbc.shape
    nc = seq // SSD_CHUNK
    xbc = jax.nn.silu(causal_depthwise_conv(xbc.astype(F32), conv_w.astype(F32), conv_b.astype(F32)))
    xs, bm, cm = split_cols(xbc, (D_INNER, SSD_GROUPS * SSD_STATE, SSD_GROUPS * SSD_STATE))
    dt = jax.nn.softplus(dt_raw.astype(F32) + dt_bias.astype(F32))
    a = -jnp.exp(a_log.astype(F32)).reshape(SSD_GROUPS, SSD_HPG)

    def chunks(t, tail):
        return t.reshape((bsz, nc, SSD_CHUNK) + tail).swapaxes(0, 1)

    xc = chunks(xs, (SSD_GROUPS, SSD_HPG, SSD_HEAD_DIM))
    dtc = chunks(dt, (SSD_GROUPS, SSD_HPG))
    bc = chunks(bm, (SSD_GROUPS, SSD_STATE))
    cc = chunks(cm, (SSD_GROUPS, SSD_STATE))
    tri = jnp.tril(jnp.ones((SSD_CHUNK, SSD_CHUNK), dtype=bool))

    def step(state, inp):
        xi, dti, bi, ci = inp
        cum = jnp.cumsum(dti * a, axis=1)
        seg = jnp.where(tri[None, :, :, None, None], cum[:, :, None] - cum[:, None, :], -jnp.inf)
        cb = jnp.einsum('btgn,bsgn->btsg', ci, bi)
        y_intra = jnp.einsum('btsg,btsgr,bsgrp->btgrp', cb, jnp.exp(seg), xi * dti[..., None])
        y_inter = jnp.einsum('btgn,bgrpn->btgrp', ci, state) * jnp.exp(cum)[..., None]
        last = cum[:, -1]
        w_s = jnp.exp(last[:, None] - cum) * dti
        new_state = state * jnp.exp(last)[..., None, None] + jnp.einsum('bsgn,bsgr,bsgrp->bgrpn', bi, w_s, xi)
        return new_state, y_intra + y_inter

    s0 = jnp.zeros((bsz, SSD_GROUPS, SSD_HPG, SSD_HEAD_DIM, SSD_STATE), F32)
    _, ys = lax.scan(step, s0, (xc, dtc, bc, cc))
    y = ys.swapaxes(0, 1).reshape(bsz, seq, SSD_HEADS, SSD_HEAD_DIM)
    y = y + d_skip.astype(F32)[:, None] * xs.reshape(bsz, seq, SSD_HEADS, SSD_HEAD_DIM)
    y = y.reshape(bsz, seq, D_INNER) * jax.nn.silu(z.astype(F32))
    return rms_norm(y, norm_g).astype(z.dtype)


def memory_cross_attention(qx, mem_n, w_kv, q_g, k_g):
    bsz, seq, _ = qx.shape
    k, v = jnp.split(mem_n @ w_kv, 2, axis=-1)
    q = rms_norm(qx.reshape(bsz, seq, X_HEADS, X_HEAD_DIM), q_g)
    k = rms_norm(k.reshape(bsz, -1, X_HEADS, X_HEAD_DIM), k_g)
    v = v.reshape(bsz, -1, X_HEADS, X_HEAD_DIM)
    s = jnp.einsum('bshd,bmhd->bhsm', q, k).astype(F32) * (X_HEAD_DIM ** -0.5)
    p = jax.nn.softmax(s, axis=-1).astype(v.dtype)
    return jnp.einsum('bhsm,bmhd->bshd', p, v).reshape(bsz, seq, D_X)


def setup_inputs(seed: int = 0) -> dict:
    key = jax.random.key(seed)
    ks = iter(jax.random.split(key, 64))

    def nrm(shape, scale):
        return jax.random.normal(next(ks), shape, F32) * scale

    def gain(shape):
        return 1.0 + nrm(shape, 0.02)

    def unif(shape, lo, hi):
        return jax.random.uniform(next(ks), shape, F32, lo, hi)

    x = nrm((BATCH, SEQ, D_MODEL), 1.0)
    mem = nrm((BATCH, MEM_LEN, D_MODEL), 1.0)
    offs = jax.random.randint(next(ks), (BATCH, 1), 0, MAX_POS_OFFSET, dtype=jnp.int32)
    positions = (offs + jnp.arange(SEQ, dtype=jnp.int32)[None, :]).astype(jnp.int32)
    din = D_MODEL ** -0.5

    norm_g = gain((DEPTH, D_MODEL))
    w_out = nrm((DEPTH, D_GATE, D_MODEL), D_GATE ** -0.5)
    mem_norm_g = gain((D_MODEL,))
    w_mem_kv = nrm((DEPTH, D_MODEL, 2 * D_X), din)
    xq_g = gain((DEPTH, X_HEAD_DIM))
    xk_g = gain((DEPTH, X_HEAD_DIM))

    s5_w_in = nrm((N_S5, D_MODEL, S5_IN), din)
    s5_lam_re = -0.5 + nrm((N_S5, S5_GROUPS, S5_STATE), 0.01)
    s5_lam_im = math.pi * jnp.arange(S5_STATE, dtype=F32) + nrm((N_S5, S5_GROUPS, S5_STATE), 0.01)
    s5_log_step = unif((N_S5, S5_GROUPS), math.log(1e-3), math.log(1e-1))
    s5_b_re = nrm((N_S5, S5_GROUPS, S5_STATE, S5_GROUP), (2 * S5_GROUP) ** -0.5)
    s5_b_im = nrm((N_S5, S5_GROUPS, S5_STATE, S5_GROUP), (2 * S5_GROUP) ** -0.5)
    s5_c_re = nrm((N_S5, S5_GROUPS, S5_GROUP, S5_STATE), (2 * S5_STATE) ** -0.5)
    s5_c_im = nrm((N_S5, S5_GROUPS, S5_GROUP, S5_STATE), (2 * S5_STATE) ** -0.5)
    s5_d = nrm((N_S5, D_INNER), 1.0)
    s5_w_glu = nrm((N_S5, D_INNER, D_INNER), D_INNER ** -0.5)

    gla_w_in = nrm((N_GLA, D_MODEL, GLA_IN), din)
    gla_w_gk2 = nrm((N_GLA, GLA_RANK, GLA_HEADS * GLA_DK), GLA_RANK ** -0.5)
    gla_b_gk2 = nrm((N_GLA, GLA_HEADS * GLA_DK), 0.1)
    gla_norm_g = gain((N_GLA, GLA_DV))

    diff_w_in = nrm((N_DIFF, D_MODEL, DIFF_IN), din)
    diff_q_g = gain((N_DIFF, DIFF_HALF))
    diff_k_g = gain((N_DIFF, DIFF_HALF))
    diff_lq1 = nrm((N_DIFF, DIFF_HALF), 0.1)
    diff_lk1 = nrm((N_DIFF, DIFF_HALF), 0.1)
    diff_lq2 = nrm((N_DIFF, DIFF_HALF), 0.1)
    diff_lk2 = nrm((N_DIFF, DIFF_HALF), 0.1)
    diff_subln_g = gain((N_DIFF, DIFF_VDIM))

    ssd_w_in = nrm((N_SSD, D_MODEL, SSD_IN), din)
    ssd_conv_w = nrm((N_SSD, SSD_CONV, SSD_CONV_CH), SSD_CONV ** -0.5)
    ssd_conv_b = nrm((N_SSD, SSD_CONV_CH), 0.02)
    dt0 = jnp.exp(unif((N_SSD, SSD_HEADS), math.log(1e-3), math.log(1e-1)))
    ssd_dt_bias = dt0 + jnp.log(-jnp.expm1(-dt0))
    ssd_a_log = jnp.log(unif((N_SSD, SSD_HEADS), 1.0, 16.0))
    ssd_d = 1.0 + nrm((N_SSD, SSD_HEADS), 0.1)
    ssd_norm_g = gain((N_SSD, D_INNER))

    return {
        'x': x, 'mem': mem, 'positions': positions,
        'norm_g': norm_g, 'w_out': w_out, 'mem_norm_g': mem_norm_g, 'w_mem_kv': w_mem_kv,
        'xq_g': xq_g, 'xk_g': xk_g,
        's5_w_in': s5_w_in, 's5_lam_re': s5_lam_re, 's5_lam_im': s5_lam_im, 's5_log_step': s5_log_step,
        's5_b_re': s5_b_re, 's5_b_im': s5_b_im, 's5_c_re': s5_c_re, 's5_c_im': s5_c_im,
        's5_d': s5_d, 's5_w_glu': s5_w_glu,
        'gla_w_in': gla_w_in, 'gla_w_gk2': gla_w_gk2, 'gla_b_gk2': gla_b_gk2, 'gla_norm_g': gla_norm_g,
        'diff_w_in': diff_w_in, 'diff_q_g': diff_q_g, 'diff_k_g': diff_k_g,
        'diff_lq1': diff_lq1, 'diff_lk1': diff_lk1, 'diff_lq2': diff_lq2, 'diff_lk2': diff_lk2,
        'diff_subln_g': diff_subln_g,
        'ssd_w_in': ssd_w_in, 'ssd_conv_w': ssd_conv_w, 'ssd_conv_b': ssd_conv_b,
        'ssd_dt_bias': ssd_dt_bias, 'ssd_a_log': ssd_a_log, 'ssd_d': ssd_d, 'ssd_norm_g': ssd_norm_g,
    }


def reference(x, mem, positions, norm_g, w_out, mem_norm_g, w_mem_kv, xq_g, xk_g,
              s5_w_in, s5_lam_re, s5_lam_im, s5_log_step, s5_b_re, s5_b_im, s5_c_re, s5_c_im,
              s5_d, s5_w_glu,
              gla_w_in, gla_w_gk2, gla_b_gk2, gla_norm_g,
              diff_w_in, diff_q_g, diff_k_g, diff_lq1, diff_lk1, diff_lq2, diff_lk2, diff_subln_g,
              ssd_w_in, ssd_conv_w, ssd_conv_b, ssd_dt_bias, ssd_a_log, ssd_d, ssd_norm_g):
    mem_n = rms_norm(mem, mem_norm_g)
    cos, sin = rope_tables(positions, DIFF_HALF)
    tail = (D_INNER, D_X, D_X)
    for i in range(DEPTH):
        kind, j = i % N_MIXERS, i // N_MIXERS
        h = rms_norm(x, norm_g[i])
        if kind == 0:
            u, z_mix, z_x, qx = split_cols(h @ s5_w_in[j], (D_INNER,) + tail)
            branch = s5_mixer(u, z_mix, s5_lam_re[j], s5_lam_im[j], s5_log_step[j], s5_b_re[j], s5_b_im[j],
                              s5_c_re[j], s5_c_im[j], s5_d[j], s5_w_glu[j])
        elif kind == 1:
            q, k, v, gk, z_mix, z_x, qx = split_cols(
                h @ gla_w_in[j], (GLA_HEADS * GLA_DK, GLA_HEADS * GLA_DK, D_INNER, GLA_RANK) + tail)
            branch = gla_mixer(q, k, v, gk, z_mix, gla_w_gk2[j], gla_b_gk2[j], gla_norm_g[j])
        elif kind == 2:
            q, k, v, z_mix, z_x, qx = split_cols(h @ diff_w_in[j], (D_INNER, D_INNER, D_INNER) + tail)
            lam_init = 0.8 - 0.6 * math.exp(-0.3 * i)
            branch = diff_attention(q, k, v, z_mix, cos, sin, diff_q_g[j], diff_k_g[j], diff_lq1[j], diff_lk1[j],
                                    diff_lq2[j], diff_lk2[j], diff_subln_g[j], lam_init)
        else:
            xbc, dt_raw, z_mix, z_x, qx = split_cols(h @ ssd_w_in[j], (SSD_CONV_CH, SSD_HEADS) + tail)
            branch = ssd_mixer(xbc, dt_raw, z_mix, ssd_conv_w[j], ssd_conv_b[j], ssd_dt_bias[j],
                               ssd_a_log[j], ssd_d[j], ssd_norm_g[j])
        mem_out = memory_cross_attention(qx, mem_n, w_mem_kv[i], xq_g[i], xk_g[i]) * jax.nn.silu(z_x)
        x = x + jnp.concatenate([branch, mem_out], axis=-1) @ w_out[i]
    return x
```

```python
import numpy as np, math
from contextlib import ExitStack
import concourse.bass as bass
import concourse.mybir as mybir
from concourse.bass_utils import run_bass_kernel_spmd

F32 = mybir.dt.float32
BF16 = mybir.dt.bfloat16
I32 = mybir.dt.int32
AF = mybir.ActivationFunctionType
ALU = mybir.AluOpType
AX = mybir.AxisListType


class Sched:
    STREAMS = ("pe", "act", "dve", "pool", "sp")
    NSLOT = 8
    MAXV = 30000

    def __init__(self, nc, es):
        self.nc = nc
        self.es = es
        self.ops = []
        self.lastw = {}
        self.readers = {}
        self.n_ps = 0

    def sb(self, name, shape, dt):
        return self.es.enter_context(self.nc.sbuf_tensor("sb_" + name, list(shape), dt))

    def psum(self, name, shape, dt):
        return self.es.enter_context(self.nc.psum_tensor(name, list(shape), dt))

    def capture(self):
        self._cap = []

    def end_capture(self):
        lst, self._cap = self._cap, None
        return lst

    def replay_interleaved(self, lists):
        its = [list(l) for l in lists]
        pos = [0] * len(its)
        left = sum(len(l) for l in its)
        while left:
            for k, l in enumerate(its):
                if pos[k] < len(l):
                    self.add(*l[pos[k]])
                    pos[k] += 1
                    left -= 1

    def add(self, stream, fn, r=(), w=(), kind="cmp"):
        if getattr(self, "_cap", None) is not None:
            self._cap.append((stream, fn, tuple(r), tuple(w), kind))
            return -1
        i = len(self.ops)
        deps = set()
        px = [k for k in r if isinstance(k, str) and k[:2] == "ps" and k[2:].isdigit()]
        if px:
            r = [k for k in r if k not in px]
            w = list(w) + [k for k in px if k not in w]
        for k in r:
            j = self.lastw.get(k)
            if j is not None:
                deps.add(j)
        for k in w:
            j = self.lastw.get(k)
            if j is not None:
                deps.add(j)
            rd = self.readers.get(k)
            if rd:
                deps.update(rd.values())
        for k in r:
            rd = self.readers.setdefault(k, {})
            rk = stream if kind == "cmp" else ("dma", i)
            rd[rk] = i
        for k in w:
            self.lastw[k] = i
            self.readers[k] = {}
        self.ops.append((stream, kind, fn, deps))
        return i

    def dma(self, out, in_, r=(), w=(), q="sp", **kw):
        return self.add(q, lambda e: e.dma_start(out=out, in_=in_, **kw), r, w, kind="dma")

    def mm(self, out, lhsT, rhs, start=True, stop=True, r=(), w=(), **kw):
        return self.add("pe", lambda e: e.matmul(out, lhsT, rhs, start=start, stop=stop, **kw), r, w)

    def tr(self, out, in_, ident, r=(), w=()):
        return self.add("pe", lambda e: e.transpose(out, in_, ident), r, w)

    def act(self, out, in_, func, r=(), w=(), eng="act", **kw):
        return self.add(eng, lambda e: e.activation(out=out, in_=in_, func=func, **kw), r, w)

    def tt(self, out, in0, in1, op, r=(), w=(), eng="dve"):
        return self.add(eng, lambda e: e.tensor_tensor(out=out, in0=in0, in1=in1, op=op), r, w)

    def ts(self, out, in0, s1, op0, s2=None, op1=None, r=(), w=(), eng="dve", **kw):
        if op1 is None:
            return self.add(eng, lambda e: e.tensor_scalar(out=out, in0=in0, scalar1=s1, scalar2=None, op0=op0, **kw), r, w)
        return self.add(eng, lambda e: e.tensor_scalar(out=out, in0=in0, scalar1=s1, scalar2=s2, op0=op0, op1=op1, **kw), r, w)

    def stt(self, out, in0, scalar, in1, op0, op1, r=(), w=(), eng="dve"):
        return self.add(eng, lambda e: e.scalar_tensor_tensor(out=out, in0=in0, scalar=scalar, in1=in1, op0=op0, op1=op1), r, w)

    def cp(self, out, in_, r=(), w=(), eng="dve"):
        return self.add(eng, lambda e: e.tensor_copy(out=out, in_=in_), r, w)

    def memset(self, ap, val, w=(), eng="pool"):
        return self.add(eng, lambda e: e.memset(ap, val), (), w)

    def barrier(self):
        n = len(self.ops)
        last = {}
        dmas = set()
        start = getattr(self, "_bar_from", 0)
        for i in range(start, n):
            st, kind = self.ops[i][0], self.ops[i][1]
            if self.ops[i][2] is None:
                continue
            if kind == "dma":
                dmas.add(i)
            else:
                last[st] = i
        deps = set(last.values()) | dmas
        for st in self.STREAMS:
            self.ops.append((st, "cmp", None, set(deps)))
        self._bar_from = len(self.ops)
        self.lastw = {}
        self.readers = {}

    def emit(self):
        nc = self.nc
        ops = self.ops
        n = len(ops)
        need = [False] * n
        for i, (st, kind, fn, deps) in enumerate(ops):
            for j in deps:
                sj, kj = ops[j][0], ops[j][1]
                if kj == "dma" or kind == "dma" or sj != st or fn is None or st != "pe":
                    need[j] = True
        sems = {}

        def newsem(name):
            return self.es.enter_context(nc.semaphore(name))

        sig = [None] * n
        guard = [None] * n
        cnt = {s: 0 for s in self.STREAMS}
        cur = {}
        epoch = {s: 0 for s in self.STREAMS}
        dcount = {s: 0 for s in self.STREAMS}
        dslots = {}
        for i, (st, kind, fn, deps) in enumerate(ops):
            if kind == "dma":
                if st not in dslots:
                    dslots[st] = [newsem(f"d_{st}_{k}") for k in range(self.NSLOT)]
                k = dcount[st]
                dcount[st] += 1
                slot = k % self.NSLOT
                gen = k // self.NSLOT
                sig[i] = (dslots[st][slot], 16 * (gen + 1))
                guard[i] = (dslots[st][slot], 16 * gen)
            elif need[i]:
                if st not in cur or cnt[st] >= self.MAXV:
                    cur[st] = newsem(f"c_{st}_{epoch[st]}")
                    epoch[st] += 1
                    cnt[st] = 0
                cnt[st] += 1
                sig[i] = (cur[st], cnt[st])
        self.dslots = dslots
        self.dcount = dcount
        by_stream = {s: [] for s in self.STREAMS}
        for i, op in enumerate(ops):
            by_stream[op[0]].append(i)

        def run(stname, e):
            waited = {}

            def wait(sem, val):
                if val <= 0:
                    return
                key = id(sem)
                if waited.get(key, 0) < val:
                    e.wait_ge(sem, val)
                    waited[key] = val

            for i in by_stream[stname]:
                st, kind, fn, deps = ops[i]
                for j in sorted(deps):
                    sj, kj = ops[j][0], ops[j][1]
                    if kj == "cmp" and kind == "cmp" and sj == st and st == "pe" and fn is not None:
                        continue
                    wait(*sig[j])
                if kind == "dma":
                    wait(*guard[i])
                if fn is None:
                    continue
                ins = fn(e)
                if kind == "dma":
                    ins.then_inc(sig[i][0], 16)
                elif need[i]:
                    ins.then_inc(sig[i][0], 1)
            if stname == "sp":
                for q, sl in dslots.items():
                    tot = dcount[q]
                    for s in range(self.NSLOT):
                        ngen = (tot - s + self.NSLOT - 1) // self.NSLOT if tot > s else 0
                        wait(sl[s], 16 * ngen)

        with nc.Block() as block:
            @block.sync
            def _(e):
                run("sp", e)

            @block.tensor
            def _(e):
                run("pe", e)

            @block.scalar
            def _(e):
                run("act", e)

            @block.vector
            def _(e):
                run("dve", e)

            @block.gpsimd
            def _(e):
                run("pool", e)

D = 1024
SEQ = 2048
NT = 16
EPS = 1e-6
TWO_PI = 2.0 * math.pi
MAGIC = 12582912.0
CW1 = 6.28125
CW2 = 0.0019350051879882812
CW3 = TWO_PI - CW1 - CW2


class Ctx:
    pass


class Arena:
    def __init__(self, c):
        self.t = c.arena
        self.n = c.arena_n
        self.off = 0

    def f32(self, shape):
        n = 1
        for d in shape:
            n *= d
        a = self.off
        self.off += n
        assert self.off <= self.n, ("arena overflow", self.off, self.n)
        ap = self.t[:, a:a + n]
        return self._shape(ap, shape)

    def bf16(self, shape):
        n = 1
        for d in shape:
            n *= d
        nw = (n + 1) // 2
        a = self.off
        self.off += nw
        assert self.off <= self.n, ("arena overflow", self.off, self.n)
        ap = self.t[:, a:a + nw].bitcast(BF16)[:, 0:n]
        return self._shape(ap, shape)

    @staticmethod
    def _shape(ap, shape):
        if len(shape) == 1:
            return ap
        if len(shape) == 2:
            return ap.rearrange("p (a b) -> p a b", a=shape[0])
        if len(shape) == 3:
            return ap.rearrange("p (a b c) -> p a b c", a=shape[0], b=shape[1])
        if len(shape) == 4:
            return ap.rearrange("p (a b c d) -> p a b c d", a=shape[0], b=shape[1], c=shape[2])
        raise ValueError(shape)


def cs(c, name):
    a, b = c.cmap[name]
    return c.cst[:, a:b]


def V(c, name, i=None, n=1):
    a, b = c.vmap[name]
    if i is None:
        return c.vec[:, a:b]
    return c.vec[:, a + i:a + i + n]


def setup_common(S_, c, nc):
    c.ps = [S_.psum(f"ps{i}", [128, 512], F32) for i in range(8)]
    c.psn = 0
    c.wn = 0
    c.cst = S_.sb("cst", [128, c.ncst], F32)
    c.vec = S_.sb("vec", [128, c.nvec], F32)
    c.sm = S_.sb("sm", [128, 64], F32)
    S_.dma(c.cst[:], c.d_cst[:, :], w=["cst"])
    S_.dma(c.vec[:], c.d_vec[:, :], w=["vec"])
    c.identf = cs(c, "ident")
    names = ["ident", "U", "ones", "blk64", "rotT", "Lst"]
    c.cbf = S_.sb("cbf", [128, len(names) * 128], BF16)
    c.b = {}
    for i, nm in enumerate(names):
        S_.cp(c.cbf[:, i * 128:(i + 1) * 128], cs(c, nm), r=["cst"], w=["cbf"])
        c.b[nm] = c.cbf[:, i * 128:(i + 1) * 128]
    c.epsc = cs(c, "eps")
    c.memT = S_.sb("memT", [128, c.NSQ, 8, 256], BF16)
    c.KnT = S_.sb("KnT", [128, c.NSQ, 4, 256], BF16)
    c.Vm = S_.sb("Vm", [128, c.NSQ, 2, 512], BF16)
    rem = nc.sbuf_bytes_remaining
    c.arena_n = (rem - 2048) // 4
    c.arena = S_.sb("arena", [128, c.arena_n], F32)


def nextps(c):
    b = c.psn % getattr(c, "nrot", 6)
    c.psn = (c.psn + 1) % getattr(c, "nrot", 6)
    return b


def rstd_from_ss(S_, c, ss_ap, out_ap, n, rkeys, wkey, tmp):
    S_.act(tmp, ss_ap, AF.Sqrt, r=list(rkeys) + ["cst"], w=[wkey + "_t"], scale=1.0 / n, bias=c.epsc)
    S_.add("dve", lambda e: e.reciprocal(out=out_ap, in_=tmp), r=[wkey + "_t"], w=[wkey])


def norm_T(S_, c, src_ap, srckey, dst, dstkey, gbase, tcol, xt2, xs2, junk):
    i = c.nt_i
    c.nt_i ^= 1
    xt, xs = xt2[:, i, :], xs2[:, i, :]
    k, ks = f"xt{i}", f"xs{i}"
    S_.dma(xt, src_ap, r=srckey, w=[k])
    S_.act(junk, xt, AF.Square, r=[k], w=["junk", "ssa"], accum_out=c.sm[:, 0:1])
    rstd_from_ss(S_, c, c.sm[:, 0:1], c.sm[:, 2:3], 1024, ["ssa"], "rsa", c.sm[:, 1:2])
    S_.act(xs, xt, AF.Copy, r=[k, "rsa"], w=[ks], scale=c.sm[:, 2:3])
    for half in range(2):
        b = nextps(c)
        for j in range(4):
            kc = half * 4 + j
            S_.tr(c.ps[b][:, j * 128:(j + 1) * 128], xs[:, kc * 128:(kc + 1) * 128], c.identf,
                  r=[ks, "cst"], w=[f"ps{b}"])
        a0 = gbase + half * 4
        g = c.vec[:, a0:a0 + 4].unsqueeze(2).to_broadcast([128, 4, 128])
        S_.tt(dst[:, half * 4:half * 4 + 4, tcol:tcol + 128],
              c.ps[b][:, :].rearrange("p (j t) -> p j t", j=4), g, ALU.mult,
              r=[f"ps{b}", "vec"], w=[dstkey])


def phase_a(S_, c, A, xin, s, l):
    hT = A.bf16([8, SEQ])
    xt2 = A.f32([2, 1024])
    xs2 = A.f32([2, 1024])
    junk = A.f32([1024])
    c.nt_i = 0
    for tt in range(NT):
        norm_T(S_, c, xin[s, tt * 128:(tt + 1) * 128, :], [("X", id(xin), s, tt)], hT, "hT",
               c.vmap["norm_g"][0] + l * 8, tt * 128, xt2, xs2, junk)
    return hT


def wtiles(c, A, n=3, kc=16):
    c.wbuf = [A.bf16([kc, 128]) for _ in range(n)]
    c.wn = 0


def load_w(S_, c, dram_blk, kc):
    i = c.wn
    c.wn = (c.wn + 1) % len(c.wbuf)
    wt = c.wbuf[i]
    S_.dma(wt[:, 0:kc, :], dram_blk, r=[], w=[f"w{i}"], q="pool")
    return wt, f"w{i}"


def proj_fm(S_, c, wt, wk, hT, tb, nk=8, hk="hT", mcols=128, ntok=512):
    b = nextps(c)
    for kc in range(nk):
        S_.mm(c.ps[b][0:mcols, 0:ntok], wt[:, kc, 0:mcols], hT[:, kc, tb * ntok:(tb + 1) * ntok],
              start=(kc == 0), stop=(kc == nk - 1), r=[wk, hk], w=[f"ps{b}"])
    return b


def mem_prologue(S_, c):
    A = Arena(c)
    xt2 = A.f32([2, 1024])
    xs2 = A.f32([2, 1024])
    junk = A.f32([1024])
    c.nt_i = 0
    for s in range(c.NSQ):
        for mt in range(2):
            norm_T(S_, c, c.d_mem[s, mt * 128:(mt + 1) * 128, :], [], c.memT[:, s], "memT",
                   c.vmap["mem_norm_g"][0], mt * 128, xt2, xs2, junk)
    S_.barrier()


def xattn_layer_prologue(S_, c, l):
    A = Arena(c)
    wkv = A.bf16([8, 1024])
    kf = A.f32([256])
    sqb = A.bf16([256])
    rsb = A.f32([256])
    tmp = A.f32([256])
    for hf in range(2):
        S_.dma(wkv[:, :, hf * 512:(hf + 1) * 512], c.d_wkv[l][:, :, hf * 512:(hf + 1) * 512], r=[], w=["wkv"], q="pool")
    import os
    XS = int(os.environ.get("XSTOP", "9"))
    for s in range(c.NSQ):
        for h in range(4):
            if XS < 1:
                break
            b = nextps(c)
            for kc in range(8 if os.environ.get("XVAR") != "b" else 0):
                S_.mm(c.ps[b][:, 0:256], wkv[:, kc, h * 128:(h + 1) * 128], c.memT[:, s, kc, :],
                      start=(kc == 0), stop=(kc == 7), r=["wkv", "memT"], w=[f"ps{b}"])
            if os.environ.get("XVAR") != "a":
                S_.cp(kf, c.ps[b][:, 0:256], r=[f"ps{b}"], w=["kf"])
            if XS < 2:
                continue
            S_.act(sqb, c.ps[b][:, 0:256], AF.Square, r=[f"ps{b}"], w=["sqb"])
            b2 = nextps(c)
            S_.mm(c.ps[b2][:, 0:256], c.b["ones"], sqb, r=["cbf", "sqb"], w=[f"ps{b2}"])
            if XS < 3:
                continue
            rstd_from_ss(S_, c, c.ps[b2][:, 0:256], rsb, 128, [f"ps{b2}"], "rsb", tmp)
            if XS < 4:
                continue
            S_.stt(c.KnT[:, s, h, :], kf, V(c, "xk_g", l), rsb, ALU.mult, ALU.mult,
                   r=["kf", "rsb", "vec"], w=["KnT"])
        for mt in range(2):
            if XS < 5:
                break
            b = nextps(c)
            for kc in range(8):
                S_.mm(c.ps[b][:, :], c.memT[:, s, kc, mt * 128:(mt + 1) * 128], wkv[:, kc, 512:1024],
                      start=(kc == 0), stop=(kc == 7), r=["wkv", "memT"], w=[f"ps{b}"])
            S_.cp(c.Vm[:, s, mt, :], c.ps[b][:, :], r=[f"ps{b}"], w=["Vm"])
    S_.barrier()


def xattn_seq(S_, c, A, l, s, hT, win, cb_zx, cb_q, G):
    sc = 128.0 ** -0.5
    kf = A.f32([512])
    sqb = A.bf16([512])
    rsb = A.f32([512])
    tmp = A.f32([512])
    qn = A.bf16([512])
    sz = A.f32([512])
    pT = A.bf16([2, 512])
    gst = A.bf16([SEQ])
    for h in range(4):
        wq, wqk = load_w(S_, c, win[cb_q + h], 8)
        wz, wzk = load_w(S_, c, win[cb_zx + h], 8)
        for tb in range(4):
            bq = proj_fm(S_, c, wq, wqk, hT, tb)
            S_.cp(kf, c.ps[bq][:, :], r=[f"ps{bq}"], w=["x_kf"])
            S_.act(sqb, c.ps[bq][:, :], AF.Square, r=[f"ps{bq}"], w=["x_sqb"])
            b2 = nextps(c)
            S_.mm(c.ps[b2][:, :], c.b["ones"], sqb, r=["cbf", "x_sqb"], w=[f"ps{b2}"])
            rstd_from_ss(S_, c, c.ps[b2][:, :], rsb, 128, [f"ps{b2}"], "x_rsb", tmp)
            S_.stt(qn, kf, V(c, "xq_g", l), rsb, ALU.mult, ALU.mult, r=["x_kf", "x_rsb", "vec"], w=["x_qn"])
            bz = proj_fm(S_, c, wz, wzk, hT, tb)
            S_.act(sz, c.ps[bz][:, :], AF.Silu, r=[f"ps{bz}"], w=["x_sz"])
            for mt in range(2):
                bs = nextps(c)
                S_.mm(c.ps[bs][:, :], c.KnT[:, s, h, mt * 128:(mt + 1) * 128], qn, r=["KnT", "x_qn"], w=[f"ps{bs}"])
                S_.act(pT[:, mt, :], c.ps[bs][:, :], AF.Exp, r=[f"ps{bs}"], w=[f"x_pT{mt}"], scale=sc)
            S_.mm(c.ps[6][:, :], c.Vm[:, s, 0, h * 128:(h + 1) * 128], pT[:, 0, :], start=True, stop=False,
                  r=["Vm", "x_pT0"], w=["ps6"])
            S_.mm(c.ps[6][:, :], c.Vm[:, s, 1, h * 128:(h + 1) * 128], pT[:, 1, :], start=False, stop=True,
                  r=["Vm", "x_pT1"], w=["ps6"])
            S_.mm(c.ps[7][:, :], c.b["ones"], pT[:, 0, :], start=True, stop=False, r=["cbf", "x_pT0"], w=["ps7"])
            S_.mm(c.ps[7][:, :], c.b["ones"], pT[:, 1, :], start=False, stop=True, r=["cbf", "x_pT1"], w=["ps7"])
            S_.add("dve", lambda e, rsb=rsb: e.reciprocal(out=rsb, in_=c.ps[7][:, :]), r=["ps7"], w=["x_rsb"])
            S_.tt(kf, c.ps[6][:, :], rsb, ALU.mult, r=["ps6", "x_rsb"], w=["x_kf"])
            S_.tt(gst[:, tb * 512:(tb + 1) * 512], kf, sz, ALU.mult, r=["x_kf", "x_sz"], w=["x_gst"])
        ch = 16 + h
        S_.dma(G[s, ch * 128:(ch + 1) * 128, :], gst, r=["x_gst"], w=[("G", s, ch)])


def phase_c(S_, c, l, s, xin, xout, G):
    A = Arena(c)
    wo = A.bf16([20, 1024])
    gt2 = A.bf16([2, 20, 512])
    xt2 = A.f32([2, 1024])
    xo2 = A.f32([2, 1024])
    for hf in range(2):
        S_.dma(wo[:, :, hf * 512:(hf + 1) * 512], c.d_wout[l][:, :, hf * 512:(hf + 1) * 512], r=[], w=["wo"], q="pool")
    for tb in range(4):
        gi = tb % 2
        gtb, gk = gt2[:, gi], f"gt{gi}"
        S_.dma(gtb, G[s, :, tb * 512:(tb + 1) * 512].rearrange("(k p) t -> p k t", p=128),
               r=[("G", s, ch) for ch in range(20)], w=[gk])
        for t4 in range(4):
            tt = tb * 4 + t4
            i = tt % 2
            xt, k = xt2[:, i, :], f"cxt{i}"
            S_.dma(xt, xin[s, tt * 128:(tt + 1) * 128, :], r=[("X", id(xin), s, tt)], w=[k])
            xo, ko = xo2[:, i, :], f"cxo{i}"
            for hf in range(2):
                b = nextps(c)
                for kc in range(20):
                    S_.mm(c.ps[b][:, :], gtb[:, kc, t4 * 128:(t4 + 1) * 128], wo[:, kc, hf * 512:(hf + 1) * 512],
                          start=(kc == 0), stop=(kc == 19), r=[gk, "wo"], w=[f"ps{b}"])
                S_.tt(xo[:, hf * 512:(hf + 1) * 512], c.ps[b][:, :], xt[:, hf * 512:(hf + 1) * 512], ALU.add,
                      r=[f"ps{b}", k], w=[ko])
            S_.dma(xout[s, tt * 128:(tt + 1) * 128, :], xo, r=[ko], w=[("X", id(xout), s, tt)])
    S_.barrier()

S5_TB = 32


def lambda_bar(S_, c, A, lamr, lami, lstep, n, pfx):
    step = A.f32([n]); xi = A.f32([n]); xr = A.f32([n]); mag = A.f32([n])
    t = A.f32([n]); k = A.f32([n]); y = A.f32([n])
    sn = A.f32([n]); csn = A.f32([n]); lbr = A.f32([n]); lbi = A.f32([n])
    K = lambda s: pfx + s
    S_.act(step, lstep, AF.Exp, r=[K("in")], w=[K("step")])
    S_.tt(xi, lami, step, ALU.mult, r=[K("in"), K("step")], w=[K("xi")])
    S_.tt(xr, lamr, step, ALU.mult, r=[K("in"), K("step")], w=[K("xr")])
    S_.act(mag, xr, AF.Exp, r=[K("xr")], w=[K("mag")])
    for which, shift, dst in (("s", 0.0, sn), ("c", 0.5 * math.pi, csn)):
        S_.ts(t, xi, shift, ALU.add, 1.0 / TWO_PI, ALU.mult, r=[K("xi")], w=[K("t")])
        S_.ts(k, t, MAGIC, ALU.add, -MAGIC, ALU.add, r=[K("t")], w=[K("k")])
        S_.stt(y, k, -TWO_PI, xi, ALU.mult, ALU.add, r=[K("k"), K("xi")], w=[K("y")])
        S_.ts(y, y, shift, ALU.add, -3.14159, ALU.max, r=[K("y")], w=[K("y")])
        S_.ts(y, y, 3.14159, ALU.min, r=[K("y")], w=[K("y")])
        S_.act(dst, y, AF.Sin, r=[K("y")], w=[K(which)])
    S_.tt(lbr, mag, csn, ALU.mult, r=[K("mag"), K("c")], w=[K("lbr")])
    S_.tt(lbi, mag, sn, ALU.mult, r=[K("mag"), K("s")], w=[K("lbi")])
    return lbr, lbi, K("lbr"), K("lbi")


def s5_layer(S_, c, l, xin, xout):
    NSQ = c.NSQ
    TB = S5_TB
    win = c.d_win[l]
    CB_U, CB_ZM, CB_ZX, CB_Q = 0, 16, 32, 36
    U, GS, G = c.d_U, c.d_GS, c.d_G
    for s in range(NSQ):
        A = Arena(c)
        hT = phase_a(S_, c, A, xin, s, l)
        wtiles(c, A, 3, 8)
        ust = A.bf16([2, SEQ])
        for cb in range(16):
            wt, wk = load_w(S_, c, win[CB_U + cb], 8)
            i = cb % 2
            for tb in range(4):
                b = proj_fm(S_, c, wt, wk, hT, tb)
                if tb % 2 == 0:
                    S_.act(ust[:, i, tb * 512:(tb + 1) * 512], c.ps[b][:, :], AF.Copy, r=[f"ps{b}"], w=[f"ust{i}"])
                else:
                    S_.cp(ust[:, i, tb * 512:(tb + 1) * 512], c.ps[b][:, :], r=[f"ps{b}"], w=[f"ust{i}"])
            S_.dma(U[s, cb * 128:(cb + 1) * 128, :], ust[:, i, :], r=[f"ust{i}"], w=[("U", s, cb)])
        S_.barrier()
    if c.stop < 4:
        return
    A = Arena(c)
    sA = A.f32([3, 64])
    S_.dma(sA, c.d_s5A[:, :, :], w=["A_in"])
    Cc = A.bf16([2, 64, 64])
    S_.dma(Cc, c.d_s5C[:, :, :, :], w=["Cc"], q="pool")
    Bpad = A.bf16([16, 2, 2, 128])
    lbrA, lbiA, kra, kia = lambda_bar(S_, c, A, sA[:, 0, :], sA[:, 1, :], sA[:, 2, :], 64, "A_")
    mark = A.off
    sB = A.f32([5, 1024])
    S_.dma(sB, c.d_s5B[:, :, :], w=["B_in"])
    lbrB, lbiB, krb, kib = lambda_bar(S_, c, A, sB[:, 0, :], sB[:, 1, :], sB[:, 2, :], 1024, "B_")
    n = 1024
    nr = A.f32([n]); den = A.f32([n]); t1 = A.f32([n]); t2 = A.f32([n]); cor = A.f32([n]); coi = A.f32([n])
    lamr, lami, bre, bim = sB[:, 0, :], sB[:, 1, :], sB[:, 3, :], sB[:, 4, :]
    S_.ts(nr, lbrB, -1.0, ALU.add, r=[krb], w=["nr"])
    S_.tt(den, lamr, lamr, ALU.mult, r=["B_in"], w=["den"])
    S_.tt(t1, lami, lami, ALU.mult, r=["B_in"], w=["t1"])
    S_.tt(den, den, t1, ALU.add, r=["den", "t1"], w=["den"])
    S_.add("dve", lambda e: e.reciprocal(out=den, in_=den), r=["den"], w=["den"])
    S_.tt(t1, nr, lamr, ALU.mult, r=["nr", "B_in"], w=["t1"])
    S_.tt(t2, lbiB, lami, ALU.mult, r=[kib, "B_in"], w=["t2"])
    S_.tt(t1, t1, t2, ALU.add, r=["t1", "t2"], w=["t1"])
    S_.tt(cor, t1, den, ALU.mult, r=["t1", "den"], w=["cor"])
    S_.tt(t1, lbiB, lamr, ALU.mult, r=[kib, "B_in"], w=["t1"])
    S_.tt(t2, nr, lami, ALU.mult, r=["nr", "B_in"], w=["t2"])
    S_.tt(t1, t1, t2, ALU.subtract, r=["t1", "t2"], w=["t1"])
    S_.tt(coi, t1, den, ALU.mult, r=["t1", "den"], w=["coi"])
    mj = cs(c, "maskq")
    for ri, (a0, a1, op) in enumerate(((bre, bim, ALU.subtract), (bim, bre, ALU.add))):
        S_.tt(t1, cor, a0, ALU.mult, r=["cor", "B_in"], w=["t1"])
        S_.tt(t2, coi, a1, ALU.mult, r=["coi", "B_in"], w=["t2"])
        S_.tt(t1, t1, t2, op, r=["t1", "t2"], w=["t1"])
        for e_ in range(2):
            for j in range(2):
                S_.ts(Bpad[:, :, ri, e_, j * 64:(j + 1) * 64], t1.rearrange("p (k q) -> p k q", k=16),
                      mj[:, 2 * e_ + j:2 * e_ + j + 1], ALU.mult, r=["t1", "cst"], w=["Bpad"])
    S_.barrier()
    if c.stop < 5:
        return
    A.off = mark
    NTK = NSQ * TB
    Dt = A.f32([TB, 2, NSQ, 64])
    Hc = A.f32([2, NSQ, 64])
    Hb = A.bf16([2, 64, NSQ, TB])
    UB = 64
    ublk = A.bf16([16, NSQ, UB])
    udk = A.f32([16, NSQ, TB])
    yd = udk
    gst = A.bf16([16, NSQ, UB])
    tm = [A.f32([NSQ, 64]) for _ in range(4)]
    lr = lbrA.unsqueeze(1).to_broadcast([128, NSQ, 64])
    li = lbiA.unsqueeze(1).to_broadcast([128, NSQ, 64])
    dsk = V(c, "s5_d").unsqueeze(2).unsqueeze(3).to_broadcast([128, 16, NSQ, TB])
    S_.memset(Hc, 0.0, w=["Hc"], eng="dve")
    for ub in range(SEQ // UB if c.stop > 5 else 1):
        for s in range(NSQ):
            S_.dma(ublk[:, :, s, :], U[s, :, ub * UB:(ub + 1) * UB].rearrange("(k p) t -> p k t", p=128),
                   r=[("U", s, cb) for cb in range(16)], w=["ublk"])
        for tq in range(UB // TB):
            tsl = slice(tq * TB, (tq + 1) * TB)
            S_.tt(udk, ublk[:, :, :, tsl], dsk, ALU.mult, r=["ublk", "vec"], w=["udk"], eng="pool")
            for ri in range(2):
                for hq in range(2):
                    for g2 in range(8):
                        b = nextps(c)
                        for c2 in range(2):
                            ch = 2 * g2 + c2
                            for e_ in range(2):
                                col = (c2 * 2 + e_) * NTK
                                S_.mm(c.ps[b][:, col:col + NTK], Bpad[64 * hq:64 * hq + 64, ch, ri, e_, :],
                                      ublk[64 * hq:64 * hq + 64, ch, :, tsl], r=["Bpad", "ublk"], w=[f"ps{b}"])
                        for c2 in range(2):
                            ch = 2 * g2 + c2
                            p0 = 4 * ch + 2 * hq
                            S_.act(Dt[:, :, ri, :, p0:p0 + 2].rearrange("p t s q -> p q s t"),
                                   c.ps[b][:, c2 * 2 * NTK:(c2 + 1) * 2 * NTK].rearrange("p (q s t) -> p q s t", q=2, s=NSQ),
                                   AF.Copy, r=[f"ps{b}"], w=["D"])
            for t in range(TB):
                Pr = Hc[:, 0] if t == 0 else Dt[:, t - 1, 0]
                Pi = Hc[:, 1] if t == 0 else Dt[:, t - 1, 1]
                rk = ["D", "Hc", kra, kia]
                S_.tt(tm[0], Pr, lr, ALU.mult, r=rk, w=["tm"])
                S_.tt(tm[1], Pi, li, ALU.mult, r=rk, w=["tm"])
                S_.tt(tm[0], tm[0], tm[1], ALU.subtract, r=["tm"], w=["tm"])
                S_.tt(tm[2], Pi, lr, ALU.mult, r=rk, w=["tm"])
                S_.tt(tm[3], Pr, li, ALU.mult, r=rk, w=["tm"])
                S_.tt(tm[2], tm[2], tm[3], ALU.add, r=["tm"], w=["tm"])
                S_.tt(Dt[:, t, 0], Dt[:, t, 0], tm[0], ALU.add, r=["D", "tm"], w=["D"])
                S_.tt(Dt[:, t, 1], Dt[:, t, 1], tm[2], ALU.add, r=["D", "tm"], w=["D"])
            S_.cp(Hc, Dt[:, TB - 1], r=["D"], w=["Hc"])
            for ri in range(2):
                S_.act(Hb[:, ri], Dt[:, :, ri, :, :].rearrange("p t s q -> p q s t"), AF.Copy,
                       r=["D"], w=["Hb"], scale=(1.0 if ri == 0 else -1.0))
            for cg in range(4):
                b = nextps(c)
                for cc in range(4):
                    ch = cg * 4 + cc
                    for q in range(4):
                        pair = ch * 4 + q
                        hq, e_ = q // 2, q % 2
                        for ri in range(2):
                            S_.mm(c.ps[b][64 * hq:64 * hq + 64, cc * NTK:(cc + 1) * NTK], Cc[:, ri, pair, :],
                                  Hb[:, ri, pair], start=(e_ == 0 and ri == 0), stop=(e_ == 1 and ri == 1),
                                  r=["Cc", "Hb"], w=[f"ps{b}"])
                S_.tt(yd[:, cg * 4:cg * 4 + 4], c.ps[b][:, 0:4 * NTK].rearrange("p (k s t) -> p k s t", k=4, s=NSQ),
                      udk[:, cg * 4:cg * 4 + 4], ALU.add, r=[f"ps{b}", "udk"], w=["udk"])
            S_.act(gst[:, :, :, tsl], yd, AF.Gelu_apprx_tanh, r=["udk"], w=["gst"])
        for s in range(NSQ):
            S_.dma(GS[s, :, ub * UB:(ub + 1) * UB].rearrange("(k p) t -> p k t", p=128), gst[:, :, s, :],
                   r=["gst"], w=[("GS", s, k_) for k_ in range(16)])
    S_.barrier()
    if c.stop < 7:
        return
    for s in range(NSQ):
        A = Arena(c)
        hT = phase_a(S_, c, A, xin, s, l)
        A2off = A.off
        wtiles(c, A, 3, 16)
        gT = A.bf16([16, SEQ])
        for hf in range(2):
            S_.dma(gT[:, hf * 8:(hf + 1) * 8, :], GS[s, hf * 1024:(hf + 1) * 1024, :].rearrange("(k p) t -> p k t", p=128),
                   r=[("GS", s, k_) for k_ in range(16)], w=["gT"])
        sg = A.f32([512]); sz = A.f32([512]); tg = A.f32([512]); gout = A.bf16([2, SEQ])
        for cb in range(16):
            wg, wgk = load_w(S_, c, c.d_wglu[cb], 16)
            wz, wzk = load_w(S_, c, win[CB_ZM + cb], 8)
            i = cb % 2
            for tb in range(4):
                bg = proj_fm(S_, c, wg, wgk, gT, tb, nk=16, hk="gT")
                S_.act(sg, c.ps[bg][:, :], AF.Sigmoid, r=[f"ps{bg}"], w=["sg"])
                bz = proj_fm(S_, c, wz, wzk, hT, tb)
                S_.act(sz, c.ps[bz][:, :], AF.Silu, r=[f"ps{bz}"], w=["sz"])
                S_.tt(tg, gT[:, cb, tb * 512:(tb + 1) * 512], sg, ALU.mult, r=["gT", "sg"], w=["tg"])
                S_.tt(gout[:, i, tb * 512:(tb + 1) * 512], tg, sz, ALU.mult, r=["tg", "sz"], w=[f"gout{i}"])
            S_.dma(G[s, cb * 128:(cb + 1) * 128, :], gout[:, i, :], r=[f"gout{i}"], w=[("G", s, cb)])
        xattn_seq(S_, c, A, l, s, hT, win, CB_ZX, CB_Q, G)
        S_.barrier()
        phase_c(S_, c, l, s, xin, xout, G)

def gla_layer(S_, c, l, xin, xout):
    win = c.d_win[l]
    CB_Q, CB_K, CB_V, CB_GK, CB_ZM, CB_ZX, CB_QM = 0, 4, 8, 24, 25, 41, 45
    G = c.d_G
    Uf, Lf = cs(c, "U"), cs(c, "Lst")
    for s in range(c.NSQ):
        A = Arena(c)
        hT = phase_a(S_, c, A, xin, s, l)
        wtiles(c, A, 3, 8)
        wv = A.bf16([8, 512])
        gkT = A.bf16([SEQ])
        wg2 = A.bf16([512])
        qT = A.f32([SEQ]); kT = A.f32([SEQ])
        vtok = A.bf16([16, 512])
        szT = A.bf16([4, SEQ])
        gout = A.bf16([4, SEQ])
        St = A.f32([512]); Sb = A.bf16([512])
        e1 = A.f32([128]); la = A.f32([128]); eb = A.f32([128]); enb = A.f32([128]); ebl = A.f32([128])
        qt = A.bf16([128]); kt = A.bf16([128]); khat = A.bf16([128]); attm = A.bf16([128]); on = A.bf16([512])
        junk = A.f32([512])
        S_.memset(gkT[0:32, :], 1.0, w=["gkT"], eng="pool")
        S_.dma(wg2[0:17, :], c.d_wgk2[:, :], w=["wg2"], q="pool")
        wt, wk = load_w(S_, c, win[CB_GK], 8)
        for tb in range(4):
            b = proj_fm(S_, c, wt, wk, hT, tb, mcols=16)
            S_.cp(gkT[0:16, tb * 512:(tb + 1) * 512], c.ps[b][0:16, :], r=[f"ps{b}"], w=["gkT"])
        import os
        GS_ = int(os.environ.get("GSTOP", "99"))
        for h in range(4 if GS_ > 0 else 0):
            for nm, cb, dst in (("q", CB_Q + h, qT), ("k", CB_K + h, kT)):
                wt, wk = load_w(S_, c, win[cb], 8)
                for tb in range(4):
                    b = proj_fm(S_, c, wt, wk, hT, tb)
                    S_.act(dst[:, tb * 512:(tb + 1) * 512], c.ps[b][:, :], AF.Copy, r=[f"ps{b}"], w=[nm + "T"])
            for j in range(4):
                S_.dma(wv[:, :, j * 128:(j + 1) * 128], win[CB_V + 4 * h + j], w=["wv"], q="pool")
            for tt in range(NT):
                b = nextps(c)
                for kc in range(8):
                    S_.mm(c.ps[b][:, :], hT[:, kc, tt * 128:(tt + 1) * 128], wv[:, kc, :], start=(kc == 0), stop=(kc == 7),
                          r=["hT", "wv"], w=[f"ps{b}"])
                if tt % 2 == 0:
                    S_.cp(vtok[:, tt, :], c.ps[b][:, :], r=[f"ps{b}"], w=["vtok"])
                else:
                    S_.act(vtok[:, tt, :], c.ps[b][:, :], AF.Copy, r=[f"ps{b}"], w=["vtok"])
            for j in range(4):
                wt, wk = load_w(S_, c, win[CB_ZM + 4 * h + j], 8)
                for tb in range(4):
                    b = proj_fm(S_, c, wt, wk, hT, tb)
                    S_.act(szT[:, j, tb * 512:(tb + 1) * 512], c.ps[b][:, :], AF.Silu, r=[f"ps{b}"], w=["szT"])
            S_.memset(St, 0.0, w=["St"], eng="pool")
            S_.memset(Sb, 0.0, w=["Sb"], eng="pool")
            for ck in range(NT if GS_ > 1 else 0):
                tsl = slice(ck * 128, (ck + 1) * 128)
                b = nextps(c)
                S_.mm(c.ps[b][:, 0:128], gkT[0:17, tsl], wg2[0:17, h * 128:(h + 1) * 128], r=["gkT", "wg2"], w=[f"ps{b}"])
                S_.act(e1, c.ps[b][:, 0:128], AF.Exp, r=[f"ps{b}"], w=["e1"], scale=-1.0)
                S_.act(e1, e1, AF.Ln, r=["e1"], w=["e1"], bias=1.0)
                S_.ts(la, e1, -1.0 / 16.0, ALU.mult, r=["e1"], w=["la"])
                if GS_ < 3:
                    continue
                b1 = nextps(c)
                S_.mm(c.ps[b1][:, 0:128], la, Uf, r=["la", "cst"], w=[f"ps{b1}"])
                S_.act(eb, c.ps[b1][:, 0:128], AF.Exp, r=[f"ps{b1}"], w=["eb"])
                S_.act(enb, c.ps[b1][:, 0:128], AF.Exp, r=[f"ps{b1}"], w=["enb"], scale=-1.0)
                b2 = nextps(c)
                S_.mm(c.ps[b2][:, 0:128], Lf, la, r=["la", "cst"], w=[f"ps{b2}"])
                S_.act(ebl, c.ps[b2][:, 0:128], AF.Exp, r=[f"ps{b2}"], w=["ebl"])
                if GS_ < 4:
                    continue
                S_.stt(qt, qT[:, tsl], 128.0 ** -0.5, eb, ALU.mult, ALU.mult, r=["qT", "eb"], w=["qt"])
                S_.tt(kt, kT[:, tsl], enb, ALU.mult, r=["kT", "enb"], w=["kt"])
                b3 = nextps(c)
                S_.tr(c.ps[b3][:, 0:128], kT[:, tsl], c.identf, r=["kT", "cst"], w=[f"ps{b3}"])
                S_.tt(khat, c.ps[b3][:, 0:128], ebl, ALU.mult, r=[f"ps{b3}", "ebl"], w=["khat"])
                b4 = nextps(c)
                S_.mm(c.ps[b4][:, 0:128], kt, qt, r=["kt", "qt"], w=[f"ps{b4}"])
                S_.tt(attm, c.ps[b4][:, 0:128], Uf, ALU.mult, r=[f"ps{b4}", "cst"], w=["attm"])
                if GS_ < 5:
                    continue
                S_.mm(c.ps[6][:, :], attm, vtok[:, ck, :], start=True, stop=False, r=["attm", "vtok"], w=["ps6"])
                S_.mm(c.ps[6][:, :], qt, Sb, start=False, stop=True, r=["qt", "Sb"], w=["ps6"])
                S_.mm(c.ps[7][:, :], khat, vtok[:, ck, :], r=["khat", "vtok"], w=["ps7"])
                S_.stt(St, St, eb[:, 127:128], c.ps[7][:, :], ALU.mult, ALU.add, r=["St", "eb", "ps7"], w=["St"])
                S_.cp(Sb, St, r=["St"], w=["Sb"], eng="pool")
                if GS_ < 6:
                    continue
                S_.act(junk, c.ps[6][:, :], AF.Square, r=["ps6"], w=["junk", "g_ss"], accum_out=c.sm[:, 8:9])
                rstd_from_ss(S_, c, c.sm[:, 8:9], c.sm[:, 10:11], 512, ["g_ss"], "g_rs", c.sm[:, 9:10])
                S_.act(on, c.ps[6][:, :], AF.Copy, r=["ps6", "g_rs"], w=["on"], scale=c.sm[:, 10:11])
                b5 = nextps(c)
                pb = c.ps[b5][:, :].bitcast(BF16)
                for j in range(4):
                    S_.tr(pb[:, j * 128:(j + 1) * 128], on[:, j * 128:(j + 1) * 128], c.b["ident"], r=["on", "cbf"], w=[f"ps{b5}"])
                for j in range(4):
                    S_.stt(gout[:, j, tsl], pb[:, j * 128:(j + 1) * 128], V(c, "gla_norm_g", j), szT[:, j, tsl],
                           ALU.mult, ALU.mult, r=[f"ps{b5}", "vec", "szT"], w=["gout"])
            for j in range(4):
                chn = h * 4 + j
                S_.dma(G[s, chn * 128:(chn + 1) * 128, :], gout[:, j, :], r=["gout"], w=[("G", s, chn)])
        xattn_seq(S_, c, A, l, s, hT, win, CB_ZX, CB_QM, G)
        S_.barrier()
        phase_c(S_, c, l, s, xin, xout, G)

def sincos_tables(S_, c, A, ang, n, sinT, cosT, pfx):
    t = A.f32([n]); k = A.f32([n]); y = A.f32([n])
    for which, shift, dst in (("s", 0.0, sinT), ("c", 0.5 * math.pi, cosT)):
        S_.ts(t, ang, shift, ALU.add, 1.0 / TWO_PI, ALU.mult, r=[pfx + "ang"], w=[pfx + "t"])
        S_.ts(k, t, MAGIC, ALU.add, -MAGIC, ALU.add, r=[pfx + "t"], w=[pfx + "k"])
        S_.stt(y, k, -TWO_PI, ang, ALU.mult, ALU.add, r=[pfx + "k", pfx + "ang"], w=[pfx + "y"])
        S_.ts(y, y, shift, ALU.add, -3.14159, ALU.max, r=[pfx + "y"], w=[pfx + "y"])
        S_.ts(y, y, 3.14159, ALU.min, r=[pfx + "y"], w=[pfx + "y"])
        S_.act(dst, y, AF.Sin, r=[pfx + "y"], w=[pfx + which])


def diff_layer(S_, c, l, xin, xout):
    win = c.d_win[l]
    CB_Q, CB_K, CB_V, CB_Z, CB_ZX, CB_QM = 0, 16, 32, 48, 64, 68
    G = c.d_G
    lam_init = 0.8 - 0.6 * math.exp(-0.3 * l)
    for s in range(c.NSQ):
        A = Arena(c)
        hT = phase_a(S_, c, A, xin, s, l)
        wtiles(c, A, 3, 8)
        c.nrot = 4
        sinT = A.f32([SEQ]); cosT = A.f32([SEQ])
        dmask = A.bf16([4 * 512])
        S_.dma(dmask, c.d_dmask[:, :], w=["dmask"], q="pool")
        mark = A.off
        posi = A.t[:, A.off:A.off + SEQ].bitcast(I32)
        A.off += SEQ
        ang = A.f32([SEQ])
        S_.dma(posi, c.d_pos[s:s + 1, :].to_broadcast([128, SEQ]), w=["posi"])
        S_.cp(ang, posi, r=["posi"], w=["r_ang"])
        S_.ts(ang, ang, cs(c, "invf"), ALU.mult, r=["r_ang", "cst"], w=["r_ang"])
        sincos_tables(S_, c, A, ang, SEQ, sinT, cosT, "r_")
        S_.barrier()
        A.off = mark
        lt = A.f32([64])
        sm = c.sm
        for i, (a_, b_) in enumerate((("diff_lq1", "diff_lk1"), ("diff_lq2", "diff_lk2"))):
            S_.tt(lt, V(c, a_), V(c, b_), ALU.mult, r=["vec"], w=["lt"])
            S_.add("dve", lambda e, i=i: e.reduce_sum(out=sm[:, 16 + i:17 + i], in_=lt, axis=AX.X), r=["lt"], w=[f"lsum{i}"])
            S_.act(sm[:, 18 + i:19 + i], sm[:, 16 + i:17 + i], AF.Exp, r=[f"lsum{i}"], w=[f"lexp{i}"])
        S_.tt(sm[:, 20:21], sm[:, 19:20], sm[:, 18:19], ALU.subtract, r=["lexp0", "lexp1"], w=["nl0"])
        S_.ts(sm[:, 21:22], sm[:, 20:21], -lam_init, ALU.add, r=["nl0"], w=["neglam"])
        S_.ts(sm[:, 22:23], V(c, "diff_subln_g"), 1.0 - lam_init, ALU.mult, r=["vec"], w=["gs"])
        neglam, gs = sm[:, 21:22], sm[:, 22:23]
        qr = A.bf16([SEQ]); kr = A.bf16([SEQ]); vtok = A.bf16([16, 128]); szT = A.bf16([SEQ]); gout = A.bf16([SEQ])
        qf = A.f32([512]); sq = A.bf16([512]); rs = A.f32([512]); tmp = A.f32([512]); qn = A.bf16([512])
        t1 = A.f32([512]); t2 = A.f32([512])
        pT = [[A.bf16([512]) for _ in range(2)] for _ in range(2)]
        rl = [A.f32([512]) for _ in range(2)]; on = [A.f32([512]) for _ in range(2)]; dd = A.f32([512]); g1 = A.f32([512])
        for hd in range(16):
            for nm, cb, dst, gname in (("q", CB_Q + hd, qr, "diff_q_g"), ("k", CB_K + hd, kr, "diff_k_g")):
                wt, wk = load_w(S_, c, win[cb], 8)
                for tb in range(4):
                    tsl = slice(tb * 512, (tb + 1) * 512)
                    b = proj_fm(S_, c, wt, wk, hT, tb)
                    S_.cp(qf, c.ps[b][:, :], r=[f"ps{b}"], w=["qf"])
                    S_.act(sq, c.ps[b][:, :], AF.Square, r=[f"ps{b}"], w=["sq"])
                    b2 = nextps(c)
                    S_.mm(c.ps[b2][:, :], c.b["blk64"], sq, r=["cbf", "sq"], w=[f"ps{b2}"])
                    rstd_from_ss(S_, c, c.ps[b2][:, :], rs, 64, [f"ps{b2}"], "rs", tmp)
                    S_.stt(qn, qf, V(c, gname), rs, ALU.mult, ALU.mult, r=["qf", "rs", "vec"], w=["qn"])
                    b3 = nextps(c)
                    S_.mm(c.ps[b3][:, :], c.b["rotT"], qn, r=["cbf", "qn"], w=[f"ps{b3}"])
                    S_.tt(t1, qn, cosT[:, tsl], ALU.mult, r=["qn", "r_c"], w=["t1"], eng="pool")
                    S_.tt(t2, c.ps[b3][:, :], sinT[:, tsl], ALU.mult, r=[f"ps{b3}", "r_s"], w=["t2"])
                    S_.tt(dst[:, tsl], t1, t2, ALU.add, r=["t1", "t2"], w=[nm + "r"])
            wt, wk = load_w(S_, c, win[CB_V + hd], 8)
            for tg in range(4):
                b = nextps(c)
                for j in range(4):
                    tt = tg * 4 + j
                    for kc in range(8):
                        S_.mm(c.ps[b][:, j * 128:(j + 1) * 128], hT[:, kc, tt * 128:(tt + 1) * 128], wt[:, kc, :],
                              start=(kc == 0), stop=(kc == 7), r=["hT", wk], w=[f"ps{b}"])
                S_.cp(vtok[:, tg * 4:tg * 4 + 4, :], c.ps[b][:, :].rearrange("p (j v) -> p j v", j=4), r=[f"ps{b}"], w=["vtok"])
            wt, wk = load_w(S_, c, win[CB_Z + hd], 8)
            for tb in range(4):
                b = proj_fm(S_, c, wt, wk, hT, tb)
                S_.act(szT[:, tb * 512:(tb + 1) * 512], c.ps[b][:, :], AF.Silu, r=[f"ps{b}"], w=["szT"])
            for qb in range(4):
                qsl = slice(qb * 512, (qb + 1) * 512)
                caps = []
                for cp_ in range(2):
                    S_.capture()
                    r0 = 64 * cp_
                    nkt = 4 * qb + 4
                    bo, bl = (6, 7) if cp_ == 0 else (4, 5)

                    def st_mm(kt, cp_=cp_):
                        b_ = 2 * cp_ + (kt % 2)
                        S_.mm(c.ps[b_][:, :], kr[r0:r0 + 64, kt * 128:(kt + 1) * 128], qr[r0:r0 + 64, qsl],
                              r=["kr", "qr"], w=[f"ps{b_}"])
                        return b_
                    bnext = st_mm(0)
                    for kt in range(nkt):
                        b_ = bnext
                        if kt + 1 < nkt:
                            bnext = st_mm(kt + 1)
                        p_ = pT[cp_][kt % 2]; pk = f"pT{cp_}{kt % 2}"
                        S_.act(p_, c.ps[b_][:, :], AF.Exp, r=[f"ps{b_}"], w=[pk], scale=0.125)
                        if kt >= 4 * qb:
                            j = kt - 4 * qb
                            S_.tt(p_, p_, dmask[:, j * 512:(j + 1) * 512], ALU.mult, r=[pk, "dmask"], w=[pk], eng="pool")
                        S_.mm(c.ps[bo][:, :], vtok[:, kt, :], p_, start=(kt == 0), stop=(kt == nkt - 1), r=["vtok", pk], w=[f"ps{bo}"])
                        S_.mm(c.ps[bl][:, :], c.b["ones"], p_, start=(kt == 0), stop=(kt == nkt - 1), r=["cbf", pk], w=[f"ps{bl}"])
                    S_.add("dve", lambda e, cp_=cp_, bl=bl: e.reciprocal(out=rl[cp_], in_=c.ps[bl][:, :]), r=[f"ps{bl}"], w=[f"rl{cp_}"])
                    S_.tt(on[cp_], c.ps[bo][:, :], rl[cp_], ALU.mult, r=[f"ps{bo}", f"rl{cp_}"], w=[f"on{cp_}"])
                    caps.append(S_.end_capture())
                S_.replay_interleaved(caps)
                S_.stt(dd, on[1], neglam, on[0], ALU.mult, ALU.add, r=["on0", "on1", "neglam"], w=["dd"])
                S_.act(sq, dd, AF.Square, r=["dd"], w=["sq"])
                b2 = nextps(c)
                S_.mm(c.ps[b2][:, :], c.b["ones"], sq, r=["cbf", "sq"], w=[f"ps{b2}"])
                rstd_from_ss(S_, c, c.ps[b2][:, :], rs, 128, [f"ps{b2}"], "rs", tmp)
                S_.stt(g1, dd, gs, rs, ALU.mult, ALU.mult, r=["dd", "gs", "rs"], w=["g1"])
                S_.tt(gout[:, qsl], g1, szT[:, qsl], ALU.mult, r=["g1", "szT"], w=["gout"])
            S_.dma(G[s, hd * 128:(hd + 1) * 128, :], gout, r=["gout"], w=[("G", s, hd)])
        c.nrot = 6
        xattn_seq(S_, c, A, l, s, hT, win, CB_ZX, CB_QM, G)
        S_.barrier()
        phase_c(S_, c, l, s, xin, xout, G)

def ssd_layer(S_, c, l, xin, xout):
    win = c.d_win[l]
    CB_XS, CB_B, CB_C, CB_DT, CB_ZM, CB_ZX, CB_QM = 0, 16, 24, 32, 33, 49, 53
    G = c.d_G
    Uf, Of = cs(c, "U"), cs(c, "ones")
    for s in range(c.NSQ):
        A = Arena(c)
        hT = phase_a(S_, c, A, xin, s, l)
        S_.barrier()
        A.off = SEQ * 8 // 2
        wtiles(c, A, 3, 8)
        mark0 = A.off
        yz = A.bf16([16, SEQ])
        dt = A.f32([16, 32]); dta = A.f32([16, 32]); cumT = A.f32([16, 32]); lastT = A.f32([16, 32])
        elast = A.f32([16, 32]); ws = A.f32([16, 32]); aneg = A.f32([32])
        wt, wk = load_w(S_, c, win[CB_DT], 8)
        for ck in range(NT):
            for kc in range(8):
                S_.mm(c.ps[6][:, ck * 32:(ck + 1) * 32], hT[:, kc, ck * 128:(ck + 1) * 128], wt[:, kc, 0:32],
                      start=(kc == 0), stop=(kc == 7), r=["hT", wk], w=["ps6"])
        bias = V(c, "ssd_dt_bias").unsqueeze(1).to_broadcast([128, 16, 32])
        S_.tt(dt, c.ps[6][:, :].rearrange("p (k h) -> p k h", k=16), bias, ALU.add, r=["ps6", "vec"], w=["dt"])
        S_.act(dt, dt, AF.Exp, r=["dt"], w=["dt"])
        S_.act(dt, dt, AF.Ln, r=["dt"], w=["dt"], bias=1.0)
        S_.act(aneg, V(c, "ssd_a_log"), AF.Exp, r=["vec"], w=["aneg"])
        S_.ts(aneg, aneg, -1.0, ALU.mult, r=["aneg"], w=["aneg"])
        S_.tt(dta, dt, aneg.unsqueeze(1).to_broadcast([128, 16, 32]), ALU.mult, r=["dt", "aneg"], w=["dta"])
        for ck in range(NT):
            S_.mm(c.ps[6][:, ck * 32:(ck + 1) * 32], Uf, dta[:, ck, :], r=["cst", "dta"], w=["ps6"])
            S_.mm(c.ps[7][:, ck * 32:(ck + 1) * 32], Of, dta[:, ck, :], r=["cst", "dta"], w=["ps7"])
        S_.cp(cumT, c.ps[6][:, :].rearrange("p (k h) -> p k h", k=16), r=["ps6"], w=["cumT"])
        S_.cp(lastT, c.ps[7][:, :].rearrange("p (k h) -> p k h", k=16), r=["ps7"], w=["lastT"])
        S_.act(elast, lastT, AF.Exp, r=["lastT"], w=["elast"])
        S_.tt(ws, lastT, cumT, ALU.subtract, r=["lastT", "cumT"], w=["ws"])
        S_.act(ws, ws, AF.Exp, r=["ws"], w=["ws"])
        S_.tt(ws, ws, dt, ALU.mult, r=["ws", "dt"], w=["ws"])
        xsT = A.bf16([2, SEQ]); BT = A.bf16([SEQ]); CT = A.bf16([SEQ]); szT = A.bf16([2, SEQ])
        St = A.f32([4, 64]); Sb = A.bf16([4, 64])
        mk_ = A.off
        raw = A.f32([SEQ + 4]); acc = A.f32([SEQ])
        end1 = A.off
        A.off = mk_
        TB_ = [dict(xtok=A.bf16([256]), btok=A.bf16([128]), CBm=A.f32([128]), CBd=A.f32([4, 128]), Z=A.f32([4, 128]),
                    dm=A.f32([4, 128]), mm=A.bf16([4, 128]), ecr=A.f32([4, 128]), Cs=A.bf16([4, 128]), xw=A.bf16([4, 64]),
                    yf=A.f32([2, 128])) for _ in range(2)]
        A.off = max(A.off, end1)
        Ub = Uf.unsqueeze(1).to_broadcast([128, 4, 128])
        for g in range(8):
            S_.barrier()
            S_.memset(raw[:, 0:4], 0.0, w=["raw"], eng="pool")
            srcs = [(CB_XS + 2 * g, xsT[:, 0, :], 2 * g), (CB_XS + 2 * g + 1, xsT[:, 1, :], 2 * g + 1),
                    (CB_B + g, BT, 16 + g), (CB_C + g, CT, 24 + g)]
            for cb, dst, cch in srcs:
                wt, wk = load_w(S_, c, win[cb], 8)
                for tb in range(4):
                    b = proj_fm(S_, c, wt, wk, hT, tb)
                    S_.act(raw[:, 3 + tb * 512:3 + (tb + 1) * 512], c.ps[b][:, :], AF.Copy, r=[f"ps{b}"], w=["raw"])
                cw = lambda j: V(c, "ssd_conv_w", cch * 4 + j)
                S_.ts(acc, raw[:, 3:3 + SEQ], cw(3), ALU.mult, V(c, "ssd_conv_b", cch), ALU.add, r=["raw", "vec"], w=["acc"], eng="pool")
                for j in range(3):
                    S_.stt(acc, raw[:, j:j + SEQ], cw(j), acc, ALU.mult, ALU.add, r=["raw", "vec", "acc"], w=["acc"])
                S_.act(dst, acc, AF.Silu, r=["acc"], w=["xbc"])
            for i in range(2):
                wt, wk = load_w(S_, c, win[CB_ZM + 2 * g + i], 8)
                for tb in range(4):
                    b = proj_fm(S_, c, wt, wk, hT, tb)
                    S_.act(szT[:, i, tb * 512:(tb + 1) * 512], c.ps[b][:, :], AF.Silu, r=[f"ps{b}"], w=["szT"])
            S_.barrier()
            S_.memset(St, 0.0, w=["St"], eng="pool")
            S_.memset(Sb, 0.0, w=["Sb"], eng="pool")
            h0 = 4 * g
            def front(ck):
                    tsl = slice(ck * 128, (ck + 1) * 128)
                    T_ = TB_[ck % 2]
                    xtok, btok, CBm, CBd, Z, dm, mm_, ecr, Cs, xw, yf = (T_[k_] for k_ in
                                                                         ("xtok", "btok", "CBm", "CBd", "Z", "dm", "mm", "ecr", "Cs", "xw", "yf"))
                    P_ = str(ck % 2)
                    b = nextps(c)
                    pb = c.ps[b][:, :].bitcast(BF16)
                    for i in range(2):
                        S_.tr(pb[:, i * 128:(i + 1) * 128], xsT[:, i, tsl], c.b["ident"], r=["xbc", "cbf"], w=[f"ps{b}"])
                    S_.tr(pb[:, 256:384], BT[:, tsl], c.b["ident"], r=["xbc", "cbf"], w=[f"ps{b}"])
                    S_.act(xtok, pb[:, 0:256], AF.Copy, r=[f"ps{b}"], w=["xtok" + P_])
                    S_.act(btok, pb[:, 256:384], AF.Copy, r=[f"ps{b}"], w=["btok" + P_])
                    b1 = nextps(c)
                    S_.mm(c.ps[b1][:, 0:128], BT[:, tsl], CT[:, tsl], r=["xbc"], w=[f"ps{b1}"])
                    S_.tt(CBm, c.ps[b1][:, 0:128], Uf, ALU.mult, r=[f"ps{b1}", "cst"], w=["CBm" + P_])
                    dtb = dt[:, ck, h0:h0 + 4].unsqueeze(2).to_broadcast([128, 4, 128])
                    S_.tt(CBd, CBm.unsqueeze(1).to_broadcast([128, 4, 128]), dtb, ALU.mult, r=["CBm" + P_, "dt"], w=["CBd" + P_], eng="pool")
                    S_.tt(Z, Ub, dta[:, ck, h0:h0 + 4].unsqueeze(2).to_broadcast([128, 4, 128]), ALU.mult,
                          r=["cst", "dta"], w=["Z" + P_], eng="pool")
                    b2 = nextps(c)
                    S_.mm(c.ps[b2][:, :], Of, Z.rearrange("p r t -> p (r t)"), r=["cst", "Z" + P_], w=[f"ps{b2}"])
                    psv = c.ps[b2][:, :].rearrange("p (r t) -> p r t", r=4)
                    S_.act(ecr, psv, AF.Exp, r=[f"ps{b2}"], w=["ecr" + P_])
                    S_.tt(dm, psv, cumT[:, ck, h0:h0 + 4].unsqueeze(2).to_broadcast([128, 4, 128]), ALU.subtract,
                          r=[f"ps{b2}", "cumT"], w=["dm" + P_])
                    S_.ts(dm, dm, 0.0, ALU.min, r=["dm" + P_], w=["dm" + P_])
                    S_.act(dm, dm, AF.Exp, r=["dm" + P_], w=["dm" + P_])
                    S_.tt(mm_, dm, CBd, ALU.mult, r=["dm" + P_, "CBd" + P_], w=["mm" + P_])
                    S_.tt(Cs, CT[:, tsl].unsqueeze(1).to_broadcast([128, 4, 128]), ecr, ALU.mult, r=["xbc", "ecr" + P_], w=["Cs" + P_], eng="pool")
                    S_.tt(xw, xtok.rearrange("p (r q) -> p r q", r=4), ws[:, ck, h0:h0 + 4].unsqueeze(2).to_broadcast([128, 4, 64]),
                          ALU.mult, r=["xtok" + P_, "ws"], w=["xw" + P_], eng="pool")
            def tail(ck):
                    tsl = slice(ck * 128, (ck + 1) * 128)
                    T_ = TB_[ck % 2]
                    xtok, btok, CBm, CBd, Z, dm, mm_, ecr, Cs, xw, yf = (T_[k_] for k_ in
                                                                         ("xtok", "btok", "CBm", "CBd", "Z", "dm", "mm", "ecr", "Cs", "xw", "yf"))
                    P_ = str(ck % 2)
                    by = 7
                    b3 = 6
                    for r_ in range(4):
                        i, po = r_ // 2, 64 * (r_ % 2)
                        S_.mm(c.ps[by][po:po + 64, i * 128:(i + 1) * 128], xtok[:, r_ * 64:(r_ + 1) * 64], mm_[:, r_, :], start=True, stop=False,
                              r=["xtok" + P_, "mm" + P_], w=[f"ps{by}"])
                        S_.mm(c.ps[by][po:po + 64, i * 128:(i + 1) * 128], Sb[:, r_, :], Cs[:, r_, :], start=False, stop=True,
                              r=["Sb", "Cs" + P_], w=[f"ps{by}"])
                    for r_ in range(4):
                        S_.mm(c.ps[b3][:, r_ * 64:(r_ + 1) * 64], btok, xw[:, r_, :], r=["btok" + P_, "xw" + P_], w=[f"ps{b3}"])
                    S_.tt(St, St, elast[:, ck, h0:h0 + 4].unsqueeze(2).to_broadcast([128, 4, 64]), ALU.mult, r=["St", "elast"], w=["St"])
                    S_.tt(St, St, c.ps[b3][:, 0:256].rearrange("p (r q) -> p r q", r=4), ALU.add, r=["St", f"ps{b3}"], w=["St"])
                    S_.act(Sb, St, AF.Copy, r=["St"], w=["Sb"])
                    for i in range(2):
                        S_.stt(yf[:, i, :], xsT[:, i, tsl], V(c, "ssd_d", 2 * g + i), c.ps[by][:, i * 128:(i + 1) * 128], ALU.mult, ALU.add,
                               r=["xbc", "vec", f"ps{by}"], w=["yf" + P_])
                        S_.tt(yz[:, 2 * g + i, tsl], yf[:, i, :], szT[:, i, tsl], ALU.mult, r=["yf" + P_, "szT"], w=["yz"])
            for ck in range(0, NT, 2):
                caps = []
                for d_ in range(2):
                    S_.capture()
                    front(ck + d_)
                    caps.append(S_.end_capture())
                S_.replay_interleaved(caps)
                tail(ck)
                tail(ck + 1)
        S_.barrier()
        A.off = mark0 + 16 * SEQ // 2
        sq = A.bf16([512]); rs = A.f32([512]); tmp = A.f32([512]); gst = A.bf16([2, 512])
        for tb in range(4):
            tsl = slice(tb * 512, (tb + 1) * 512)
            for cc in range(16):
                S_.act(sq, yz[:, cc, tsl], AF.Square, r=["yz"], w=["sq"])
                S_.mm(c.ps[6][:, :], c.b["ones"], sq, start=(cc == 0), stop=(cc == 15), r=["cbf", "sq"], w=["ps6"])
            rstd_from_ss(S_, c, c.ps[6][:, :], rs, 2048, ["ps6"], "rs", tmp)
            for cc in range(16):
                i = cc % 2
                S_.stt(gst[:, i, :], yz[:, cc, tsl], V(c, "ssd_norm_g", cc), rs, ALU.mult, ALU.mult, r=["yz", "vec", "rs"], w=[f"gst{i}"])
                S_.dma(G[s, cc * 128:(cc + 1) * 128, tsl], gst[:, i, :], r=[f"gst{i}"], w=[("G", s, cc)])
        xattn_seq(S_, c, A, l, s, hT, win, CB_ZX, CB_QM, G)
        S_.barrier()
        phase_c(S_, c, l, s, xin, xout, G)

LAYER_FN = {}


def build_nc(layers, NSQ, meta):
    nc = bass.Bass("TRN2", target_bir_lowering=False)
    c = Ctx()
    c.NSQ = NSQ
    c.cmap, c.ncst = meta["cmap"], meta["ncst"]
    c.vmap, c.nvec = meta["vmap"], meta["nvec"]

    def din(name, shape, dt=F32):
        return nc.dram_tensor(name, list(shape), dt, kind="ExternalInput").ap()

    def dscr(name, shape, dt):
        return nc.dram_tensor(name, list(shape), dt, kind="Internal").ap()

    x = din("x", [NSQ, SEQ, D])
    c.d_mem = din("mem", [NSQ, 256, D])
    c.d_pos = din("pos", [NSQ, SEQ], I32)
    c.d_cst = din("cst", [128, c.ncst])
    c.d_vec = din("vec", [128, c.nvec])
    c.d_win, c.d_wout, c.d_wkv = {}, {}, {}
    for l in layers:
        c.d_win[l] = din(f"win{l}", [meta["ncb"][l], 128, 8, 128])
        c.d_wout[l] = din(f"wout{l}", [128, 20, 1024])
        c.d_wkv[l] = din(f"wkv{l}", [128, 8, 1024])
    for nm, shp in meta["extra"].items():
        if int(nm[1]) in layers or nm[0] != "L":
            setattr(c, "d_" + nm[3:], din(nm, shp))
    y = nc.dram_tensor("y", [NSQ, SEQ, D], F32, kind="ExternalOutput").ap()
    c.d_G = dscr("G", [NSQ, 2560, SEQ], BF16)
    if 0 in layers:
        c.d_U = dscr("U", [NSQ, 2048, SEQ], BF16)
        c.d_GS = dscr("GS", [NSQ, 2048, SEQ], BF16)
    xs = [x]
    for i in range(len(layers) - 1):
        xs.append(dscr(f"xs{i}", [NSQ, SEQ, D], F32))
    xs.append(y)
    with ExitStack() as es:
        S_ = Sched(nc, es)
        setup_common(S_, c, nc)
        c.stop = meta.get("stop", 99)
        mem_prologue(S_, c)
        for i, l in enumerate(layers):
            if c.stop < 2:
                break
            xattn_layer_prologue(S_, c, l)
            if c.stop < 3:
                break
            LAYER_FN[l](S_, c, l, xs[i], xs[i + 1])
        print("ops recorded:", len(S_.ops), "arena words:", c.arena_n, flush=True)
        S_.emit()
    return nc


def _blk(W, c0, ncols):
    nb = (ncols + 127) // 128
    K = W.shape[0] // 128
    Wp = np.zeros((W.shape[0], nb * 128), np.float32)
    Wp[:, :ncols] = W[:, c0:c0 + ncols]
    return np.ascontiguousarray(Wp.reshape(K, 128, nb, 128).transpose(2, 1, 0, 3))


def host_prep(inp):
    f = lambda a: np.asarray(a, np.float32)
    meta = {"ncb": {}, "extra": {}}
    shared = {}
    cols = []
    cmap = {}

    def addc(name, arr):
        a = sum(x.shape[1] for x in cols)
        cols.append(arr.astype(np.float32))
        cmap[name] = (a, a + arr.shape[1])

    i128 = np.arange(128)
    addc("ident", np.eye(128))
    addc("U", (i128[:, None] <= i128[None, :]).astype(np.float32))
    addc("ones", np.ones((128, 128)))
    blk = np.zeros((128, 128)); blk[:64, :64] = 1; blk[64:, 64:] = 1
    addc("blk64", blk)
    rot = np.zeros((128, 128))
    for cpt in range(2):
        for d in range(64):
            if d < 32:
                rot[cpt * 64 + d + 32, cpt * 64 + d] = -1.0
            else:
                rot[cpt * 64 + d - 32, cpt * 64 + d] = 1.0
    addc("rotT", rot)
    addc("Lst", (i128[:, None] > i128[None, :]).astype(np.float32))
    addc("eps", np.full((128, 1), EPS))
    addc("maskq", np.stack([((i128 // 16) % 4 == j) for j in range(4)], 1).astype(np.float32))
    inv = 10000.0 ** (-np.arange(0, 64, 2, dtype=np.float32) / 64)
    addc("invf", inv[(i128 % 64) % 32][:, None])
    shared["cst"] = np.ascontiguousarray(np.concatenate(cols, 1))
    meta["cmap"], meta["ncst"] = cmap, shared["cst"].shape[1]
    vcols = []
    vmap = {}

    def addv(name, arr):
        a = sum(x.shape[1] for x in vcols)
        vcols.append(np.asarray(arr, np.float32))
        vmap[name] = (a, a + arr.shape[1])

    addv("norm_g", f(inp["norm_g"]).reshape(4, 8, 128).transpose(2, 0, 1).reshape(128, 32))
    addv("mem_norm_g", f(inp["mem_norm_g"]).reshape(8, 128).T)
    addv("xq_g", f(inp["xq_g"]).T)
    addv("xk_g", f(inp["xk_g"]).T)
    addv("s5_d", f(inp["s5_d"])[0].reshape(16, 128).T)
    addv("gla_norm_g", f(inp["gla_norm_g"])[0].reshape(4, 128).T)
    addv("diff_q_g", np.tile(f(inp["diff_q_g"])[0], 2)[:, None])
    addv("diff_k_g", np.tile(f(inp["diff_k_g"])[0], 2)[:, None])
    addv("diff_subln_g", f(inp["diff_subln_g"])[0][:, None])
    for nm in ("diff_lq1", "diff_lk1", "diff_lq2", "diff_lk2"):
        addv(nm, np.tile(f(inp[nm])[0][None, :], (128, 1)))
    addv("ssd_conv_w", f(inp["ssd_conv_w"])[0].reshape(4, 32, 128).transpose(2, 1, 0).reshape(128, 128))
    addv("ssd_conv_b", f(inp["ssd_conv_b"])[0].reshape(32, 128).T)
    addv("ssd_d", np.repeat(f(inp["ssd_d"])[0], 64).reshape(16, 128).T)
    addv("ssd_norm_g", f(inp["ssd_norm_g"])[0].reshape(16, 128).T)
    addv("ssd_dt_bias", np.tile(f(inp["ssd_dt_bias"])[0][None, :], (128, 1)))
    addv("ssd_a_log", np.tile(f(inp["ssd_a_log"])[0][None, :], (128, 1)))
    shared["vec"] = np.ascontiguousarray(np.concatenate(vcols, 1))
    meta["vmap"], meta["nvec"] = vmap, shared["vec"].shape[1]
    wins = {0: f(inp["s5_w_in"])[0], 1: f(inp["gla_w_in"])[0], 2: f(inp["diff_w_in"])[0], 3: f(inp["ssd_w_in"])[0]}
    shared["win0"] = _blk(wins[0], 0, 5120)
    W = wins[1]
    shared["win1"] = np.concatenate([_blk(W, 0, 3072), _blk(W, 3072, 16), _blk(W, 3088, 3072)], 0)
    shared["win2"] = _blk(wins[2], 0, 9216)
    W = wins[3]
    shared["win3"] = np.concatenate([_blk(W, 0, 4096), _blk(W, 4096, 32), _blk(W, 4128, 3072)], 0)
    for l in range(4):
        meta["ncb"][l] = shared[f"win{l}"].shape[0]
        shared[f"wout{l}"] = np.ascontiguousarray(f(inp["w_out"])[l].reshape(20, 128, 1024).transpose(1, 0, 2))
        shared[f"wkv{l}"] = np.ascontiguousarray(f(inp["w_mem_kv"])[l].reshape(8, 128, 1024).transpose(1, 0, 2))
    ex = {}
    ex["L0_wglu"] = np.ascontiguousarray(f(inp["s5_w_glu"])[0].reshape(16, 128, 16, 128).transpose(2, 1, 0, 3))
    lam_re, lam_im, ls = f(inp["s5_lam_re"])[0], f(inp["s5_lam_im"])[0], f(inp["s5_log_step"])[0]
    toA = lambda a: a.reshape(64, 2, 64).transpose(1, 2, 0).reshape(128, 64)
    lsf = np.repeat(ls[:, None], 64, 1)
    ex["L0_s5A"] = np.ascontiguousarray(np.stack([toA(lam_re), toA(lam_im), toA(lsf)], 1))
    toBl = lambda a: np.repeat(a.reshape(16, 8, 1, 64), 16, 2).transpose(1, 2, 0, 3).reshape(128, 1024)
    toBb = lambda b: b.reshape(16, 8, 64, 16).transpose(1, 3, 0, 2).reshape(128, 1024)
    ex["L0_s5B"] = np.ascontiguousarray(np.stack([toBl(lam_re), toBl(lam_im), toBl(lsf),
                                                    toBb(f(inp["s5_b_re"])[0]), toBb(f(inp["s5_b_im"])[0])], 1))
    Cc = np.zeros((2, 64, 2, 32, 2, 2, 2, 16), np.float32)
    for ri, nm in enumerate(("s5_c_re", "s5_c_im")):
        Cm = f(inp[nm])[0].reshape(32, 2, 2, 16, 64)
        for j in range(2):
            for e_ in range(2):
                Cc[j, :, ri, :, e_, e_, j, :] = Cm[:, e_, j].transpose(2, 0, 1)
    ex["L0_s5C"] = np.ascontiguousarray(Cc.reshape(128, 2, 64, 64))
    wg2 = np.zeros((17, 512), np.float32)
    wg2[:16] = f(inp["gla_w_gk2"])[0]
    wg2[16] = f(inp["gla_b_gk2"])[0]
    ex["L1_wgk2"] = wg2
    qi = np.arange(512)[None, :]
    ex["L2_dmask"] = np.concatenate([((128 * j + np.arange(128)[:, None]) <= qi).astype(np.float32) for j in range(4)], 1)
    for k_, v_ in ex.items():
        meta["extra"][k_] = list(v_.shape)
        shared[k_] = v_
    return shared, meta


def run_layers(inp, layers, NSQ, ncores, xin=None, stop=99, trace=False):
    shared, meta = host_prep(inp)
    meta["stop"] = stop
    nc = build_nc(layers, NSQ, meta)
    x = np.asarray(inp["x"], np.float32) if xin is None else xin
    mem = np.asarray(inp["mem"], np.float32)
    pos = np.asarray(inp["positions"], np.int32)
    names = ["cst", "vec"] + [f"{p}{l}" for l in layers for p in ("win", "wout", "wkv")]
    names += [k_ for k_ in meta["extra"] if int(k_[1]) in layers]
    in_maps = []
    for ci in range(ncores):
        sl = slice(ci * NSQ, (ci + 1) * NSQ)
        m = {"x": np.ascontiguousarray(x[sl]), "mem": np.ascontiguousarray(mem[sl]), "pos": np.ascontiguousarray(pos[sl])}
        for nm in names:
            m[nm] = shared[nm]
        in_maps.append(m)
    res = run_bass_kernel_spmd(nc, in_maps, core_ids=list(range(ncores)), **({"trace": True} if trace else {}))
    if trace:
        print("EXEC_NS", layers, res.exec_time_ns, flush=True)
    return np.concatenate([r["y"] for r in res.results], 0)


def kernel(**inputs):
    return run_layers(inputs, [0, 1, 2, 3], 4, 8)

LAYER_FN[0] = s5_layer
LAYER_FN[1] = gla_layer
LAYER_FN[2] = diff_layer
LAYER_FN[3] = ssd_layer
```

```python
import numpy as np, math
from contextlib import ExitStack
import concourse.bass as bass
import concourse.mybir as mybir
from concourse.bass_utils import run_bass_kernel_spmd

F32 = mybir.dt.float32
BF16 = mybir.dt.bfloat16
I32 = mybir.dt.int32
AF = mybir.ActivationFunctionType
ALU = mybir.AluOpType
AX = mybir.AxisListType


class Sched:
    STREAMS = ("pe", "act", "dve", "pool", "sp")
    NSLOT = 8
    MAXV = 30000

    def __init__(self, nc, es):
        self.nc = nc
        self.es = es
        self.ops = []
        self.lastw = {}
        self.readers = {}
        self.n_ps = 0

    def sb(self, name, shape, dt):
        return self.es.enter_context(self.nc.sbuf_tensor("sb_" + name, list(shape), dt))

    def psum(self, name, shape, dt):
        return self.es.enter_context(self.nc.psum_tensor(name, list(shape), dt))

    def capture(self):
        self._cap = []

    def end_capture(self):
        lst, self._cap = self._cap, None
        return lst

    def replay_interleaved(self, lists):
        its = [list(l) for l in lists]
        pos = [0] * len(its)
        left = sum(len(l) for l in its)
        while left:
            for k, l in enumerate(its):
                if pos[k] < len(l):
                    self.add(*l[pos[k]])
                    pos[k] += 1
                    left -= 1

    def add(self, stream, fn, r=(), w=(), kind="cmp"):
        if getattr(self, "_cap", None) is not None:
            self._cap.append((stream, fn, tuple(r), tuple(w), kind))
            return -1
        i = len(self.ops)
        deps = set()
        px = [k for k in r if isinstance(k, str) and k[:2] == "ps" and k[2:].isdigit()]
        if px:
            r = [k for k in r if k not in px]
            w = list(w) + [k for k in px if k not in w]
        for k in r:
            j = self.lastw.get(k)
            if j is not None:
                deps.add(j)
        for k in w:
            j = self.lastw.get(k)
            if j is not None:
                deps.add(j)
            rd = self.readers.get(k)
            if rd:
                deps.update(rd.values())
        for k in r:
            rd = self.readers.setdefault(k, {})
            rk = stream if kind == "cmp" else ("dma", i)
            rd[rk] = i
        for k in w:
            self.lastw[k] = i
            self.readers[k] = {}
        self.ops.append((stream, kind, fn, deps))
        return i

    def dma(self, out, in_, r=(), w=(), q="sp", **kw):
        return self.add(q, lambda e: e.dma_start(out=out, in_=in_, **kw), r, w, kind="dma")

    def mm(self, out, lhsT, rhs, start=True, stop=True, r=(), w=(), **kw):
        return self.add("pe", lambda e: e.matmul(out, lhsT, rhs, start=start, stop=stop, **kw), r, w)

    def tr(self, out, in_, ident, r=(), w=()):
        return self.add("pe", lambda e: e.transpose(out, in_, ident), r, w)

    def act(self, out, in_, func, r=(), w=(), eng="act", **kw):
        return self.add(eng, lambda e: e.activation(out=out, in_=in_, func=func, **kw), r, w)

    def tt(self, out, in0, in1, op, r=(), w=(), eng="dve"):
        return self.add(eng, lambda e: e.tensor_tensor(out=out, in0=in0, in1=in1, op=op), r, w)

    def ts(self, out, in0, s1, op0, s2=None, op1=None, r=(), w=(), eng="dve", **kw):
        if op1 is None:
            return self.add(eng, lambda e: e.tensor_scalar(out=out, in0=in0, scalar1=s1, scalar2=None, op0=op0, **kw), r, w)
        return self.add(eng, lambda e: e.tensor_scalar(out=out, in0=in0, scalar1=s1, scalar2=s2, op0=op0, op1=op1, **kw), r, w)

    def stt(self, out, in0, scalar, in1, op0, op1, r=(), w=(), eng="dve"):
        return self.add(eng, lambda e: e.scalar_tensor_tensor(out=out, in0=in0, scalar=scalar, in1=in1, op0=op0, op1=op1), r, w)

    def cp(self, out, in_, r=(), w=(), eng="dve"):
        return self.add(eng, lambda e: e.tensor_copy(out=out, in_=in_), r, w)

    def memset(self, ap, val, w=(), eng="pool"):
        return self.add(eng, lambda e: e.memset(ap, val), (), w)

    def barrier(self):
        n = len(self.ops)
        last = {}
        dmas = set()
        start = getattr(self, "_bar_from", 0)
        for i in range(start, n):
            st, kind = self.ops[i][0], self.ops[i][1]
            if self.ops[i][2] is None:
                continue
            if kind == "dma":
                dmas.add(i)
            else:
                last[st] = i
        deps = set(last.values()) | dmas
        for st in self.STREAMS:
            self.ops.append((st, "cmp", None, set(deps)))
        self._bar_from = len(self.ops)
        self.lastw = {}
        self.readers = {}

    def emit(self):
        nc = self.nc
        ops = self.ops
        n = len(ops)
        need = [False] * n
        for i, (st, kind, fn, deps) in enumerate(ops):
            for j in deps:
                sj, kj = ops[j][0], ops[j][1]
                if kj == "dma" or kind == "dma" or sj != st or fn is None or st != "pe":
                    need[j] = True
        sems = {}

        def newsem(name):
            return self.es.enter_context(nc.semaphore(name))

        sig = [None] * n
        guard = [None] * n
        cnt = {s: 0 for s in self.STREAMS}
        cur = {}
        epoch = {s: 0 for s in self.STREAMS}
        dcount = {s: 0 for s in self.STREAMS}
        dslots = {}
        for i, (st, kind, fn, deps) in enumerate(ops):
            if kind == "dma":
                if st not in dslots:
                    dslots[st] = [newsem(f"d_{st}_{k}") for k in range(self.NSLOT)]
                k = dcount[st]
                dcount[st] += 1
                slot = k % self.NSLOT
                gen = k // self.NSLOT
                sig[i] = (dslots[st][slot], 16 * (gen + 1))
                guard[i] = (dslots[st][slot], 16 * gen)
            elif need[i]:
                if st not in cur or cnt[st] >= self.MAXV:
                    cur[st] = newsem(f"c_{st}_{epoch[st]}")
                    epoch[st] += 1
                    cnt[st] = 0
                cnt[st] += 1
                sig[i] = (cur[st], cnt[st])
        self.dslots = dslots
        self.dcount = dcount
        by_stream = {s: [] for s in self.STREAMS}
        for i, op in enumerate(ops):
            by_stream[op[0]].append(i)

        def run(stname, e):
            waited = {}

            def wait(sem, val):
                if val <= 0:
                    return
                key = id(sem)
                if waited.get(key, 0) < val:
                    e.wait_ge(sem, val)
                    waited[key] = val

            for i in by_stream[stname]:
                st, kind, fn, deps = ops[i]
                for j in sorted(deps):
                    sj, kj = ops[j][0], ops[j][1]
                    if kj == "cmp" and kind == "cmp" and sj == st and st == "pe" and fn is not None:
                        continue
                    wait(*sig[j])
                if kind == "dma":
                    wait(*guard[i])
                if fn is None:
                    continue
                ins = fn(e)
                if kind == "dma":
                    ins.then_inc(sig[i][0], 16)
                elif need[i]:
                    ins.then_inc(sig[i][0], 1)
            if stname == "sp":
                for q, sl in dslots.items():
                    tot = dcount[q]
                    for s in range(self.NSLOT):
                        ngen = (tot - s + self.NSLOT - 1) // self.NSLOT if tot > s else 0
                        wait(sl[s], 16 * ngen)

        with nc.Block() as block:
            @block.sync
            def _(e):
                run("sp", e)

            @block.tensor
            def _(e):
                run("pe", e)

            @block.scalar
            def _(e):
                run("act", e)

            @block.vector
            def _(e):
                run("dve", e)

            @block.gpsimd
            def _(e):
                run("pool", e)

D = 1024
SEQ = 2048
NT = 16
EPS = 1e-6
TWO_PI = 2.0 * math.pi
MAGIC = 12582912.0
CW1 = 6.28125
CW2 = 0.0019350051879882812
CW3 = TWO_PI - CW1 - CW2


class Ctx:
    pass


class Arena:
    def __init__(self, c):
        self.t = c.arena
        self.n = c.arena_n
        self.off = 0

    def f32(self, shape):
        n = 1
        for d in shape:
            n *= d
        a = self.off
        self.off += n
        assert self.off <= self.n, ("arena overflow", self.off, self.n)
        ap = self.t[:, a:a + n]
        return self._shape(ap, shape)

    def bf16(self, shape):
        n = 1
        for d in shape:
            n *= d
        nw = (n + 1) // 2
        a = self.off
        self.off += nw
        assert self.off <= self.n, ("arena overflow", self.off, self.n)
        ap = self.t[:, a:a + nw].bitcast(BF16)[:, 0:n]
        return self._shape(ap, shape)

    @staticmethod
    def _shape(ap, shape):
        if len(shape) == 1:
            return ap
        if len(shape) == 2:
            return ap.rearrange("p (a b) -> p a b", a=shape[0])
        if len(shape) == 3:
            return ap.rearrange("p (a b c) -> p a b c", a=shape[0], b=shape[1])
        if len(shape) == 4:
            return ap.rearrange("p (a b c d) -> p a b c d", a=shape[0], b=shape[1], c=shape[2])
        raise ValueError(shape)


def cs(c, name):
    a, b = c.cmap[name]
    return c.cst[:, a:b]


def V(c, name, i=None, n=1):
    a, b = c.vmap[name]
    if i is None:
        return c.vec[:, a:b]
    return c.vec[:, a + i:a + i + n]


def setup_common(S_, c, nc):
    c.ps = [S_.psum(f"ps{i}", [128, 512], F32) for i in range(8)]
    c.psn = 0
    c.wn = 0
    c.cst = S_.sb("cst", [128, c.ncst], F32)
    c.vec = S_.sb("vec", [128, c.nvec], F32)
    c.sm = S_.sb("sm", [128, 64], F32)
    S_.dma(c.cst[:], c.d_cst[:, :], w=["cst"])
    S_.dma(c.vec[:], c.d_vec[:, :], w=["vec"])
    c.identf = cs(c, "ident")
    names = ["ident", "U", "ones", "blk64", "rotT", "Lst"]
    c.cbf = S_.sb("cbf", [128, len(names) * 128], BF16)
    c.b = {}
    for i, nm in enumerate(names):
        S_.cp(c.cbf[:, i * 128:(i + 1) * 128], cs(c, nm), r=["cst"], w=["cbf"])
        c.b[nm] = c.cbf[:, i * 128:(i + 1) * 128]
    c.epsc = cs(c, "eps")
    c.memT = S_.sb("memT", [128, c.NSQ, 8, 256], BF16)
    c.KnT = S_.sb("KnT", [128, c.NSQ, 4, 256], BF16)
    c.Vm = S_.sb("Vm", [128, c.NSQ, 2, 512], BF16)
    rem = nc.sbuf_bytes_remaining
    c.arena_n = (rem - 2048) // 4
    c.arena = S_.sb("arena", [128, c.arena_n], F32)


def nextps(c):
    b = c.psn % getattr(c, "nrot", 6)
    c.psn = (c.psn + 1) % getattr(c, "nrot", 6)
    return b


def rstd_from_ss(S_, c, ss_ap, out_ap, n, rkeys, wkey, tmp):
    S_.act(tmp, ss_ap, AF.Sqrt, r=list(rkeys) + ["cst"], w=[wkey + "_t"], scale=1.0 / n, bias=c.epsc)
    S_.add("dve", lambda e: e.reciprocal(out=out_ap, in_=tmp), r=[wkey + "_t"], w=[wkey])


def norm_T(S_, c, src_ap, srckey, dst, dstkey, gbase, tcol, xt2, xs2, junk):
    i = c.nt_i
    c.nt_i ^= 1
    xt, xs = xt2[:, i, :], xs2[:, i, :]
    k, ks = f"xt{i}", f"xs{i}"
    S_.dma(xt, src_ap, r=srckey, w=[k])
    S_.act(junk, xt, AF.Square, r=[k], w=["junk", "ssa"], accum_out=c.sm[:, 0:1])
    rstd_from_ss(S_, c, c.sm[:, 0:1], c.sm[:, 2:3], 1024, ["ssa"], "rsa", c.sm[:, 1:2])
    S_.act(xs, xt, AF.Copy, r=[k, "rsa"], w=[ks], scale=c.sm[:, 2:3])
    for half in range(2):
        b = nextps(c)
        for j in range(4):
            kc = half * 4 + j
            S_.tr(c.ps[b][:, j * 128:(j + 1) * 128], xs[:, kc * 128:(kc + 1) * 128], c.identf,
                  r=[ks, "cst"], w=[f"ps{b}"])
        a0 = gbase + half * 4
        g = c.vec[:, a0:a0 + 4].unsqueeze(2).to_broadcast([128, 4, 128])
        S_.tt(dst[:, half * 4:half * 4 + 4, tcol:tcol + 128],
              c.ps[b][:, :].rearrange("p (j t) -> p j t", j=4), g, ALU.mult,
              r=[f"ps{b}", "vec"], w=[dstkey])


def phase_a(S_, c, A, xin, s, l):
    hT = A.bf16([8, SEQ])
    xt2 = A.f32([2, 1024])
    xs2 = A.f32([2, 1024])
    junk = A.f32([1024])
    c.nt_i = 0
    for tt in range(NT):
        norm_T(S_, c, xin[s, tt * 128:(tt + 1) * 128, :], [("X", id(xin), s, tt)], hT, "hT",
               c.vmap["norm_g"][0] + l * 8, tt * 128, xt2, xs2, junk)
    return hT


def wtiles(c, A, n=3, kc=16):
    c.wbuf = [A.bf16([kc, 128]) for _ in range(n)]
    c.wn = 0


def load_w(S_, c, dram_blk, kc):
    i = c.wn
    c.wn = (c.wn + 1) % len(c.wbuf)
    wt = c.wbuf[i]
    S_.dma(wt[:, 0:kc, :], dram_blk, r=[], w=[f"w{i}"], q="pool")
    return wt, f"w{i}"


def proj_fm(S_, c, wt, wk, hT, tb, nk=8, hk="hT", mcols=128, ntok=512):
    b = nextps(c)
    for kc in range(nk):
        S_.mm(c.ps[b][0:mcols, 0:ntok], wt[:, kc, 0:mcols], hT[:, kc, tb * ntok:(tb + 1) * ntok],
              start=(kc == 0), stop=(kc == nk - 1), r=[wk, hk], w=[f"ps{b}"])
    return b


def mem_prologue(S_, c):
    A = Arena(c)
    xt2 = A.f32([2, 1024])
    xs2 = A.f32([2, 1024])
    junk = A.f32([1024])
    c.nt_i = 0
    for s in range(c.NSQ):
        for mt in range(2):
            norm_T(S_, c, c.d_mem[s, mt * 128:(mt + 1) * 128, :], [], c.memT[:, s], "memT",
                   c.vmap["mem_norm_g"][0], mt * 128, xt2, xs2, junk)
    S_.barrier()


def xattn_layer_prologue(S_, c, l):
    A = Arena(c)
    wkv = A.bf16([8, 1024])
    kf = A.f32([256])
    sqb = A.bf16([256])
    rsb = A.f32([256])
    tmp = A.f32([256])
    for hf in range(2):
        S_.dma(wkv[:, :, hf * 512:(hf + 1) * 512], c.d_wkv[l][:, :, hf * 512:(hf + 1) * 512], r=[], w=["wkv"], q="pool")
    import os
    XS = int(os.environ.get("XSTOP", "9"))
    for s in range(c.NSQ):
        for h in range(4):
            if XS < 1:
                break
            b = nextps(c)
            for kc in range(8 if os.environ.get("XVAR") != "b" else 0):
                S_.mm(c.ps[b][:, 0:256], wkv[:, kc, h * 128:(h + 1) * 128], c.memT[:, s, kc, :],
                      start=(kc == 0), stop=(kc == 7), r=["wkv", "memT"], w=[f"ps{b}"])
            if os.environ.get("XVAR") != "a":
                S_.cp(kf, c.ps[b][:, 0:256], r=[f"ps{b}"], w=["kf"])
            if XS < 2:
                continue
            S_.act(sqb, c.ps[b][:, 0:256], AF.Square, r=[f"ps{b}"], w=["sqb"])
            b2 = nextps(c)
            S_.mm(c.ps[b2][:, 0:256], c.b["ones"], sqb, r=["cbf", "sqb"], w=[f"ps{b2}"])
            if XS < 3:
                continue
            rstd_from_ss(S_, c, c.ps[b2][:, 0:256], rsb, 128, [f"ps{b2}"], "rsb", tmp)
            if XS < 4:
                continue
            S_.stt(c.KnT[:, s, h, :], kf, V(c, "xk_g", l), rsb, ALU.mult, ALU.mult,
                   r=["kf", "rsb", "vec"], w=["KnT"])
        for mt in range(2):
            if XS < 5:
                break
            b = nextps(c)
            for kc in range(8):
                S_.mm(c.ps[b][:, :], c.memT[:, s, kc, mt * 128:(mt + 1) * 128], wkv[:, kc, 512:1024],
                      start=(kc == 0), stop=(kc == 7), r=["wkv", "memT"], w=[f"ps{b}"])
            S_.cp(c.Vm[:, s, mt, :], c.ps[b][:, :], r=[f"ps{b}"], w=["Vm"])
    S_.barrier()


def xattn_seq(S_, c, A, l, s, hT, win, cb_zx, cb_q, G):
    sc = 128.0 ** -0.5
    kf = A.f32([512])
    sqb = A.bf16([512])
    rsb = A.f32([512])
    tmp = A.f32([512])
    qn = A.bf16([512])
    sz = A.f32([512])
    pT = A.bf16([2, 512])
    gst = A.bf16([SEQ])
    for h in range(4):
        wq, wqk = load_w(S_, c, win[cb_q + h], 8)
        wz, wzk = load_w(S_, c, win[cb_zx + h], 8)
        for tb in range(4):
            bq = proj_fm(S_, c, wq, wqk, hT, tb)
            S_.cp(kf, c.ps[bq][:, :], r=[f"ps{bq}"], w=["x_kf"])
            S_.act(sqb, c.ps[bq][:, :], AF.Square, r=[f"ps{bq}"], w=["x_sqb"])
            b2 = nextps(c)
            S_.mm(c.ps[b2][:, :], c.b["ones"], sqb, r=["cbf", "x_sqb"], w=[f"ps{b2}"])
            rstd_from_ss(S_, c, c.ps[b2][:, :], rsb, 128, [f"ps{b2}"], "x_rsb", tmp)
            S_.stt(qn, kf, V(c, "xq_g", l), rsb, ALU.mult, ALU.mult, r=["x_kf", "x_rsb", "vec"], w=["x_qn"])
            bz = proj_fm(S_, c, wz, wzk, hT, tb)
            S_.act(sz, c.ps[bz][:, :], AF.Silu, r=[f"ps{bz}"], w=["x_sz"])
            for mt in range(2):
                bs = nextps(c)
                S_.mm(c.ps[bs][:, :], c.KnT[:, s, h, mt * 128:(mt + 1) * 128], qn, r=["KnT", "x_qn"], w=[f"ps{bs}"])
                S_.act(pT[:, mt, :], c.ps[bs][:, :], AF.Exp, r=[f"ps{bs}"], w=[f"x_pT{mt}"], scale=sc)
            S_.mm(c.ps[6][:, :], c.Vm[:, s, 0, h * 128:(h + 1) * 128], pT[:, 0, :], start=True, stop=False,
                  r=["Vm", "x_pT0"], w=["ps6"])
            S_.mm(c.ps[6][:, :], c.Vm[:, s, 1, h * 128:(h + 1) * 128], pT[:, 1, :], start=False, stop=True,
                  r=["Vm", "x_pT1"], w=["ps6"])
            S_.mm(c.ps[7][:, :], c.b["ones"], pT[:, 0, :], start=True, stop=False, r=["cbf", "x_pT0"], w=["ps7"])
            S_.mm(c.ps[7][:, :], c.b["ones"], pT[:, 1, :], start=False, stop=True, r=["cbf", "x_pT1"], w=["ps7"])
            S_.add("dve", lambda e, rsb=rsb: e.reciprocal(out=rsb, in_=c.ps[7][:, :]), r=["ps7"], w=["x_rsb"])
            S_.tt(kf, c.ps[6][:, :], rsb, ALU.mult, r=["ps6", "x_rsb"], w=["x_kf"])
            S_.tt(gst[:, tb * 512:(tb + 1) * 512], kf, sz, ALU.mult, r=["x_kf", "x_sz"], w=["x_gst"])
        ch = 16 + h
        S_.dma(G[s, ch * 128:(ch + 1) * 128, :], gst, r=["x_gst"], w=[("G", s, ch)])


def phase_c(S_, c, l, s, xin, xout, G):
    A = Arena(c)
    wo = A.bf16([20, 1024])
    gt2 = A.bf16([2, 20, 512])
    xt2 = A.f32([2, 1024])
    xo2 = A.f32([2, 1024])
    for hf in range(2):
        S_.dma(wo[:, :, hf * 512:(hf + 1) * 512], c.d_wout[l][:, :, hf * 512:(hf + 1) * 512], r=[], w=["wo"], q="pool")
    for tb in range(4):
        gi = tb % 2
        gtb, gk = gt2[:, gi], f"gt{gi}"
        S_.dma(gtb, G[s, :, tb * 512:(tb + 1) * 512].rearrange("(k p) t -> p k t", p=128),
               r=[("G", s, ch) for ch in range(20)], w=[gk])
        for t4 in range(4):
            tt = tb * 4 + t4
            i = tt % 2
            xt, k = xt2[:, i, :], f"cxt{i}"
            S_.dma(xt, xin[s, tt * 128:(tt + 1) * 128, :], r=[("X", id(xin), s, tt)], w=[k])
            xo, ko = xo2[:, i, :], f"cxo{i}"
            for hf in range(2):
                b = nextps(c)
                for kc in range(20):
                    S_.mm(c.ps[b][:, :], gtb[:, kc, t4 * 128:(t4 + 1) * 128], wo[:, kc, hf * 512:(hf + 1) * 512],
                          start=(kc == 0), stop=(kc == 19), r=[gk, "wo"], w=[f"ps{b}"])
                S_.tt(xo[:, hf * 512:(hf + 1) * 512], c.ps[b][:, :], xt[:, hf * 512:(hf + 1) * 512], ALU.add,
                      r=[f"ps{b}", k], w=[ko])
            S_.dma(xout[s, tt * 128:(tt + 1) * 128, :], xo, r=[ko], w=[("X", id(xout), s, tt)])
    S_.barrier()

S5_TB = 16


def lambda_bar(S_, c, A, lamr, lami, lstep, n, pfx):
    step = A.f32([n]); xi = A.f32([n]); xr = A.f32([n]); mag = A.f32([n])
    t = A.f32([n]); k = A.f32([n]); y = A.f32([n])
    sn = A.f32([n]); csn = A.f32([n]); lbr = A.f32([n]); lbi = A.f32([n])
    K = lambda s: pfx + s
    S_.act(step, lstep, AF.Exp, r=[K("in")], w=[K("step")])
    S_.tt(xi, lami, step, ALU.mult, r=[K("in"), K("step")], w=[K("xi")])
    S_.tt(xr, lamr, step, ALU.mult, r=[K("in"), K("step")], w=[K("xr")])
    S_.act(mag, xr, AF.Exp, r=[K("xr")], w=[K("mag")])
    for which, shift, dst in (("s", 0.0, sn), ("c", 0.5 * math.pi, csn)):
        S_.ts(t, xi, shift, ALU.add, 1.0 / TWO_PI, ALU.mult, r=[K("xi")], w=[K("t")])
        S_.ts(k, t, MAGIC, ALU.add, -MAGIC, ALU.add, r=[K("t")], w=[K("k")])
        S_.stt(y, k, -TWO_PI, xi, ALU.mult, ALU.add, r=[K("k"), K("xi")], w=[K("y")])
        S_.ts(y, y, shift, ALU.add, -3.14159, ALU.max, r=[K("y")], w=[K("y")])
        S_.ts(y, y, 3.14159, ALU.min, r=[K("y")], w=[K("y")])
        S_.act(dst, y, AF.Sin, r=[K("y")], w=[K(which)])
    S_.tt(lbr, mag, csn, ALU.mult, r=[K("mag"), K("c")], w=[K("lbr")])
    S_.tt(lbi, mag, sn, ALU.mult, r=[K("mag"), K("s")], w=[K("lbi")])
    return lbr, lbi, K("lbr"), K("lbi")


def s5_layer(S_, c, l, xin, xout):
    NSQ = c.NSQ
    TB = S5_TB
    win = c.d_win[l]
    CB_U, CB_ZM, CB_ZX, CB_Q = 0, 16, 32, 36
    U, GS, G = c.d_U, c.d_GS, c.d_G
    for s in range(NSQ):
        A = Arena(c)
        hT = phase_a(S_, c, A, xin, s, l)
        wtiles(c, A, 3, 8)
        ust = A.bf16([2, SEQ])
        for cb in range(16):
            wt, wk = load_w(S_, c, win[CB_U + cb], 8)
            i = cb % 2
            for tb in range(4):
                b = proj_fm(S_, c, wt, wk, hT, tb)
                if tb % 2 == 0:
                    S_.act(ust[:, i, tb * 512:(tb + 1) * 512], c.ps[b][:, :], AF.Copy, r=[f"ps{b}"], w=[f"ust{i}"])
                else:
                    S_.cp(ust[:, i, tb * 512:(tb + 1) * 512], c.ps[b][:, :], r=[f"ps{b}"], w=[f"ust{i}"])
            S_.dma(U[s, cb * 128:(cb + 1) * 128, :], ust[:, i, :], r=[f"ust{i}"], w=[("U", s, cb)])
        S_.barrier()
    if c.stop < 4:
        return
    A = Arena(c)
    sA = A.f32([3, 64])
    S_.dma(sA, c.d_s5A[:, :, :], w=["A_in"])
    Cc = A.bf16([2, 64, 64])
    S_.dma(Cc, c.d_s5C[:, :, :, :], w=["Cc"], q="pool")
    Bpad = A.bf16([16, 2, 2, 128])
    lbrA, lbiA, kra, kia = lambda_bar(S_, c, A, sA[:, 0, :], sA[:, 1, :], sA[:, 2, :], 64, "A_")
    mark = A.off
    sB = A.f32([5, 1024])
    S_.dma(sB, c.d_s5B[:, :, :], w=["B_in"])
    lbrB, lbiB, krb, kib = lambda_bar(S_, c, A, sB[:, 0, :], sB[:, 1, :], sB[:, 2, :], 1024, "B_")
    n = 1024
    nr = A.f32([n]); den = A.f32([n]); t1 = A.f32([n]); t2 = A.f32([n]); cor = A.f32([n]); coi = A.f32([n])
    lamr, lami, bre, bim = sB[:, 0, :], sB[:, 1, :], sB[:, 3, :], sB[:, 4, :]
    S_.ts(nr, lbrB, -1.0, ALU.add, r=[krb], w=["nr"])
    S_.tt(den, lamr, lamr, ALU.mult, r=["B_in"], w=["den"])
    S_.tt(t1, lami, lami, ALU.mult, r=["B_in"], w=["t1"])
    S_.tt(den, den, t1, ALU.add, r=["den", "t1"], w=["den"])
    S_.add("dve", lambda e: e.reciprocal(out=den, in_=den), r=["den"], w=["den"])
    S_.tt(t1, nr, lamr, ALU.mult, r=["nr", "B_in"], w=["t1"])
    S_.tt(t2, lbiB, lami, ALU.mult, r=[kib, "B_in"], w=["t2"])
    S_.tt(t1, t1, t2, ALU.add, r=["t1", "t2"], w=["t1"])
    S_.tt(cor, t1, den, ALU.mult, r=["t1", "den"], w=["cor"])
    S_.tt(t1, lbiB, lamr, ALU.mult, r=[kib, "B_in"], w=["t1"])
    S_.tt(t2, nr, lami, ALU.mult, r=["nr", "B_in"], w=["t2"])
    S_.tt(t1, t1, t2, ALU.subtract, r=["t1", "t2"], w=["t1"])
    S_.tt(coi, t1, den, ALU.mult, r=["t1", "den"], w=["coi"])
    mj = cs(c, "maskq")
    for ri, (a0, a1, op) in enumerate(((bre, bim, ALU.subtract), (bim, bre, ALU.add))):
        S_.tt(t1, cor, a0, ALU.mult, r=["cor", "B_in"], w=["t1"])
        S_.tt(t2, coi, a1, ALU.mult, r=["coi", "B_in"], w=["t2"])
        S_.tt(t1, t1, t2, op, r=["t1", "t2"], w=["t1"])
        for e_ in range(2):
            for j in range(2):
                S_.ts(Bpad[:, :, ri, e_, j * 64:(j + 1) * 64], t1.rearrange("p (k q) -> p k q", k=16),
                      mj[:, 2 * e_ + j:2 * e_ + j + 1], ALU.mult, r=["t1", "cst"], w=["Bpad"])
    S_.barrier()
    if c.stop < 5:
        return
    A.off = mark
    NTK = NSQ * TB
    Dt2 = [A.f32([TB, 2, NSQ, 64]) for _ in range(2)]
    Hc = A.f32([2, NSQ, 64])
    Hb2 = [A.bf16([2, 64, NSQ, TB]) for _ in range(2)]
    UB = 64
    ublk = A.bf16([16, NSQ, UB])
    udk2 = [A.f32([16, NSQ, TB]) for _ in range(3)]
    gst = A.bf16([16, NSQ, UB])
    tm = [A.f32([NSQ, 64]) for _ in range(4)]
    lr = lbrA.unsqueeze(1).to_broadcast([128, NSQ, 64])
    li = lbiA.unsqueeze(1).to_broadcast([128, NSQ, 64])
    dsk = V(c, "s5_d").unsqueeze(2).unsqueeze(3).to_broadcast([128, 16, NSQ, TB])
    S_.memset(Hc, 0.0, w=["Hc"], eng="dve")
    BPU = UB // TB
    NBLK = (SEQ // TB) if c.stop > 5 else BPU
    NPB = 512 // NTK

    def stage1(blk):
        p = blk % 2
        p3 = blk % 3
        Dt, udk = Dt2[p], udk2[p3]
        ub, tq = blk // BPU, blk % BPU
        if tq == 0:
            for s in range(NSQ):
                S_.dma(ublk[:, :, s, :], U[s, :, ub * UB:(ub + 1) * UB].rearrange("(k p) t -> p k t", p=128),
                       r=[("U", s, cb) for cb in range(16)], w=["ublk"])
        tsl = slice(tq * TB, (tq + 1) * TB)
        S_.tt(udk, ublk[:, :, :, tsl], dsk, ALU.mult, r=["ublk", "vec"], w=[f"udk{p3}"], eng="pool")
        nch = min(16, NPB // 2)
        for ri in range(2):
            for hq in range(2):
                for g2 in range(16 // nch):
                    b = nextps(c)
                    for c2 in range(nch):
                        ch = nch * g2 + c2
                        for e_ in range(2):
                            col = (c2 * 2 + e_) * NTK
                            S_.mm(c.ps[b][:, col:col + NTK], Bpad[64 * hq:64 * hq + 64, ch, ri, e_, :],
                                  ublk[64 * hq:64 * hq + 64, ch, :, tsl], r=["Bpad", "ublk"], w=[f"ps{b}"])
                    for c2 in range(nch):
                        ch = nch * g2 + c2
                        p0 = 4 * ch + 2 * hq
                        S_.act(Dt[:, :, ri, :, p0:p0 + 2].rearrange("p t s q -> p q s t"),
                               c.ps[b][:, c2 * 2 * NTK:(c2 + 1) * 2 * NTK].rearrange("p (q s t) -> p q s t", q=2, s=NSQ),
                               AF.Copy, r=[f"ps{b}"], w=[f"D{p}"])

    def stage2(blk):
        p = blk % 2
        Dt, Hb = Dt2[p], Hb2[p]
        dk = f"D{p}"
        for t in range(TB):
            Pr = Hc[:, 0] if t == 0 else Dt[:, t - 1, 0]
            Pi = Hc[:, 1] if t == 0 else Dt[:, t - 1, 1]
            rk = [dk, "Hc", kra, kia]
            S_.tt(tm[0], Pr, lr, ALU.mult, r=rk, w=["tm"])
            S_.tt(tm[1], Pi, li, ALU.mult, r=rk, w=["tm"])
            S_.tt(tm[0], tm[0], tm[1], ALU.subtract, r=["tm"], w=["tm"])
            S_.tt(tm[2], Pi, lr, ALU.mult, r=rk, w=["tm"])
            S_.tt(tm[3], Pr, li, ALU.mult, r=rk, w=["tm"])
            S_.tt(tm[2], tm[2], tm[3], ALU.add, r=["tm"], w=["tm"])
            S_.tt(Dt[:, t, 0], Dt[:, t, 0], tm[0], ALU.add, r=[dk, "tm"], w=[dk])
            S_.tt(Dt[:, t, 1], Dt[:, t, 1], tm[2], ALU.add, r=[dk, "tm"], w=[dk])
        S_.cp(Hc, Dt[:, TB - 1], r=[dk], w=["Hc"])
        for ri in range(2):
            S_.act(Hb[:, ri], Dt[:, :, ri, :, :].rearrange("p t s q -> p q s t"), AF.Copy,
                   r=[dk], w=[f"Hb{p}"], scale=(1.0 if ri == 0 else -1.0))

    def stage3(blk):
        p = blk % 2
        p3 = blk % 3
        Hb, udk = Hb2[p], udk2[p3]
        ub, tq = blk // BPU, blk % BPU
        tsl = slice(tq * TB, (tq + 1) * TB)
        ncb = min(16, 512 // NTK)
        for cg in range(16 // ncb):
            b = nextps(c)
            for cc in range(ncb):
                ch = cg * ncb + cc
                for q in range(4):
                    pair = ch * 4 + q
                    hq, e_ = q // 2, q % 2
                    for ri in range(2):
                        S_.mm(c.ps[b][64 * hq:64 * hq + 64, cc * NTK:(cc + 1) * NTK], Cc[:, ri, pair, :],
                              Hb[:, ri, pair], start=(e_ == 0 and ri == 0), stop=(e_ == 1 and ri == 1),
                              r=["Cc", f"Hb{p}"], w=[f"ps{b}"])
            S_.tt(udk[:, cg * ncb:(cg + 1) * ncb], c.ps[b][:, 0:ncb * NTK].rearrange("p (k s t) -> p k s t", k=ncb, s=NSQ),
                  udk[:, cg * ncb:(cg + 1) * ncb], ALU.add, r=[f"ps{b}", f"udk{p3}"], w=[f"udk{p3}"])
        S_.act(gst[:, :, :, tsl], udk, AF.Gelu_apprx_tanh, r=[f"udk{p3}"], w=["gst"])
        if tq == BPU - 1:
            for s in range(NSQ):
                S_.dma(GS[s, :, ub * UB:(ub + 1) * UB].rearrange("(k p) t -> p k t", p=128), gst[:, :, s, :],
                       r=["gst"], w=[("GS", s, k_) for k_ in range(16)])

    stage1(0)
    for blk in range(NBLK):
        if blk + 1 < NBLK:
            stage1(blk + 1)
        stage2(blk)
        if blk >= 1:
            stage3(blk - 1)
    stage3(NBLK - 1)
    S_.barrier()
    if c.stop < 7:
        return
    for s in range(NSQ):
        A = Arena(c)
        hT = phase_a(S_, c, A, xin, s, l)
        A2off = A.off
        wtiles(c, A, 3, 16)
        gT = A.bf16([16, SEQ])
        for hf in range(2):
            S_.dma(gT[:, hf * 8:(hf + 1) * 8, :], GS[s, hf * 1024:(hf + 1) * 1024, :].rearrange("(k p) t -> p k t", p=128),
                   r=[("GS", s, k_) for k_ in range(16)], w=["gT"])
        sg = A.f32([512]); sz = A.f32([512]); tg = A.f32([512]); gout = A.bf16([2, SEQ])
        for cb in range(16):
            wg, wgk = load_w(S_, c, c.d_wglu[cb], 16)
            wz, wzk = load_w(S_, c, win[CB_ZM + cb], 8)
            i = cb % 2
            for tb in range(4):
                bg = proj_fm(S_, c, wg, wgk, gT, tb, nk=16, hk="gT")
                S_.act(sg, c.ps[bg][:, :], AF.Sigmoid, r=[f"ps{bg}"], w=["sg"])
                bz = proj_fm(S_, c, wz, wzk, hT, tb)
                S_.act(sz, c.ps[bz][:, :], AF.Silu, r=[f"ps{bz}"], w=["sz"])
                S_.tt(tg, gT[:, cb, tb * 512:(tb + 1) * 512], sg, ALU.mult, r=["gT", "sg"], w=["tg"])
                S_.tt(gout[:, i, tb * 512:(tb + 1) * 512], tg, sz, ALU.mult, r=["tg", "sz"], w=[f"gout{i}"])
            S_.dma(G[s, cb * 128:(cb + 1) * 128, :], gout[:, i, :], r=[f"gout{i}"], w=[("G", s, cb)])
        xattn_seq(S_, c, A, l, s, hT, win, CB_ZX, CB_Q, G)
        S_.barrier()
        phase_c(S_, c, l, s, xin, xout, G)

def gla_layer(S_, c, l, xin, xout):
    win = c.d_win[l]
    CB_Q, CB_K, CB_V, CB_GK, CB_ZM, CB_ZX, CB_QM = 0, 4, 8, 24, 25, 41, 45
    G = c.d_G
    Uf, Lf = cs(c, "U"), cs(c, "Lst")
    for s in range(c.NSQ):
        A = Arena(c)
        hT = phase_a(S_, c, A, xin, s, l)
        wtiles(c, A, 3, 8)
        wv = A.bf16([8, 512])
        gkT = A.bf16([SEQ])
        wg2 = A.bf16([512])
        qT = A.f32([SEQ]); kT = A.f32([SEQ])
        vtok = A.bf16([16, 512])
        szT = A.bf16([4, SEQ])
        gout = A.bf16([4, SEQ])
        St = A.f32([512]); Sb = A.bf16([512])
        e1 = A.f32([128]); la = A.f32([128]); eb = A.f32([128]); enb = A.f32([128]); ebl = A.f32([128])
        qt = A.bf16([128]); kt = A.bf16([128]); khat = A.bf16([128]); attm = A.bf16([128]); on = A.bf16([512])
        junk = A.f32([512])
        S_.memset(gkT[0:32, :], 1.0, w=["gkT"], eng="pool")
        S_.dma(wg2[0:17, :], c.d_wgk2[:, :], w=["wg2"], q="pool")
        wt, wk = load_w(S_, c, win[CB_GK], 8)
        for tb in range(4):
            b = proj_fm(S_, c, wt, wk, hT, tb, mcols=16)
            S_.cp(gkT[0:16, tb * 512:(tb + 1) * 512], c.ps[b][0:16, :], r=[f"ps{b}"], w=["gkT"])
        import os
        GS_ = int(os.environ.get("GSTOP", "99"))
        for h in range(4 if GS_ > 0 else 0):
            for nm, cb, dst in (("q", CB_Q + h, qT), ("k", CB_K + h, kT)):
                wt, wk = load_w(S_, c, win[cb], 8)
                for tb in range(4):
                    b = proj_fm(S_, c, wt, wk, hT, tb)
                    S_.act(dst[:, tb * 512:(tb + 1) * 512], c.ps[b][:, :], AF.Copy, r=[f"ps{b}"], w=[nm + "T"])
            for j in range(4):
                S_.dma(wv[:, :, j * 128:(j + 1) * 128], win[CB_V + 4 * h + j], w=["wv"], q="pool")
            for tt in range(NT):
                b = nextps(c)
                for kc in range(8):
                    S_.mm(c.ps[b][:, :], hT[:, kc, tt * 128:(tt + 1) * 128], wv[:, kc, :], start=(kc == 0), stop=(kc == 7),
                          r=["hT", "wv"], w=[f"ps{b}"])
                if tt % 2 == 0:
                    S_.cp(vtok[:, tt, :], c.ps[b][:, :], r=[f"ps{b}"], w=["vtok"])
                else:
                    S_.act(vtok[:, tt, :], c.ps[b][:, :], AF.Copy, r=[f"ps{b}"], w=["vtok"])
            for j in range(4):
                wt, wk = load_w(S_, c, win[CB_ZM + 4 * h + j], 8)
                for tb in range(4):
                    b = proj_fm(S_, c, wt, wk, hT, tb)
                    S_.act(szT[:, j, tb * 512:(tb + 1) * 512], c.ps[b][:, :], AF.Silu, r=[f"ps{b}"], w=["szT"])
            S_.memset(St, 0.0, w=["St"], eng="pool")
            S_.memset(Sb, 0.0, w=["Sb"], eng="pool")
            for ck in range(NT if GS_ > 1 else 0):
                tsl = slice(ck * 128, (ck + 1) * 128)
                b = nextps(c)
                S_.mm(c.ps[b][:, 0:128], gkT[0:17, tsl], wg2[0:17, h * 128:(h + 1) * 128], r=["gkT", "wg2"], w=[f"ps{b}"])
                S_.act(e1, c.ps[b][:, 0:128], AF.Exp, r=[f"ps{b}"], w=["e1"], scale=-1.0)
                S_.act(e1, e1, AF.Ln, r=["e1"], w=["e1"], bias=1.0)
                S_.ts(la, e1, -1.0 / 16.0, ALU.mult, r=["e1"], w=["la"])
                if GS_ < 3:
                    continue
                b1 = nextps(c)
                S_.mm(c.ps[b1][:, 0:128], la, Uf, r=["la", "cst"], w=[f"ps{b1}"])
                S_.act(eb, c.ps[b1][:, 0:128], AF.Exp, r=[f"ps{b1}"], w=["eb"])
                S_.act(enb, c.ps[b1][:, 0:128], AF.Exp, r=[f"ps{b1}"], w=["enb"], scale=-1.0)
                b2 = nextps(c)
                S_.mm(c.ps[b2][:, 0:128], Lf, la, r=["la", "cst"], w=[f"ps{b2}"])
                S_.act(ebl, c.ps[b2][:, 0:128], AF.Exp, r=[f"ps{b2}"], w=["ebl"])
                if GS_ < 4:
                    continue
                S_.stt(qt, qT[:, tsl], 128.0 ** -0.5, eb, ALU.mult, ALU.mult, r=["qT", "eb"], w=["qt"])
                S_.tt(kt, kT[:, tsl], enb, ALU.mult, r=["kT", "enb"], w=["kt"])
                b3 = nextps(c)
                S_.tr(c.ps[b3][:, 0:128], kT[:, tsl], c.identf, r=["kT", "cst"], w=[f"ps{b3}"])
                S_.tt(khat, c.ps[b3][:, 0:128], ebl, ALU.mult, r=[f"ps{b3}", "ebl"], w=["khat"])
                b4 = nextps(c)
                S_.mm(c.ps[b4][:, 0:128], kt, qt, r=["kt", "qt"], w=[f"ps{b4}"])
                S_.tt(attm, c.ps[b4][:, 0:128], Uf, ALU.mult, r=[f"ps{b4}", "cst"], w=["attm"])
                if GS_ < 5:
                    continue
                S_.mm(c.ps[6][:, :], attm, vtok[:, ck, :], start=True, stop=False, r=["attm", "vtok"], w=["ps6"])
                S_.mm(c.ps[6][:, :], qt, Sb, start=False, stop=True, r=["qt", "Sb"], w=["ps6"])
                S_.mm(c.ps[7][:, :], khat, vtok[:, ck, :], r=["khat", "vtok"], w=["ps7"])
                S_.stt(St, St, eb[:, 127:128], c.ps[7][:, :], ALU.mult, ALU.add, r=["St", "eb", "ps7"], w=["St"])
                S_.cp(Sb, St, r=["St"], w=["Sb"], eng="pool")
                if GS_ < 6:
                    continue
                S_.act(junk, c.ps[6][:, :], AF.Square, r=["ps6"], w=["junk", "g_ss"], accum_out=c.sm[:, 8:9])
                rstd_from_ss(S_, c, c.sm[:, 8:9], c.sm[:, 10:11], 512, ["g_ss"], "g_rs", c.sm[:, 9:10])
                S_.act(on, c.ps[6][:, :], AF.Copy, r=["ps6", "g_rs"], w=["on"], scale=c.sm[:, 10:11])
                b5 = nextps(c)
                pb = c.ps[b5][:, :].bitcast(BF16)
                for j in range(4):
                    S_.tr(pb[:, j * 128:(j + 1) * 128], on[:, j * 128:(j + 1) * 128], c.b["ident"], r=["on", "cbf"], w=[f"ps{b5}"])
                for j in range(4):
                    S_.stt(gout[:, j, tsl], pb[:, j * 128:(j + 1) * 128], V(c, "gla_norm_g", j), szT[:, j, tsl],
                           ALU.mult, ALU.mult, r=[f"ps{b5}", "vec", "szT"], w=["gout"])
            for j in range(4):
                chn = h * 4 + j
                S_.dma(G[s, chn * 128:(chn + 1) * 128, :], gout[:, j, :], r=["gout"], w=[("G", s, chn)])
        xattn_seq(S_, c, A, l, s, hT, win, CB_ZX, CB_QM, G)
        S_.barrier()
        phase_c(S_, c, l, s, xin, xout, G)

def sincos_tables(S_, c, A, ang, n, sinT, cosT, pfx):
    t = A.f32([n]); k = A.f32([n]); y = A.f32([n])
    for which, shift, dst in (("s", 0.0, sinT), ("c", 0.5 * math.pi, cosT)):
        S_.ts(t, ang, shift, ALU.add, 1.0 / TWO_PI, ALU.mult, r=[pfx + "ang"], w=[pfx + "t"])
        S_.ts(k, t, MAGIC, ALU.add, -MAGIC, ALU.add, r=[pfx + "t"], w=[pfx + "k"])
        S_.stt(y, k, -TWO_PI, ang, ALU.mult, ALU.add, r=[pfx + "k", pfx + "ang"], w=[pfx + "y"])
        S_.ts(y, y, shift, ALU.add, -3.14159, ALU.max, r=[pfx + "y"], w=[pfx + "y"])
        S_.ts(y, y, 3.14159, ALU.min, r=[pfx + "y"], w=[pfx + "y"])
        S_.act(dst, y, AF.Sin, r=[pfx + "y"], w=[pfx + which])


def diff_layer(S_, c, l, xin, xout):
    win = c.d_win[l]
    CB_Q, CB_K, CB_V, CB_Z, CB_ZX, CB_QM = 0, 16, 32, 48, 64, 68
    G = c.d_G
    lam_init = 0.8 - 0.6 * math.exp(-0.3 * l)
    for s in range(c.NSQ):
        A = Arena(c)
        hT = phase_a(S_, c, A, xin, s, l)
        wtiles(c, A, 3, 8)
        c.nrot = 4
        sinT = A.f32([SEQ]); cosT = A.f32([SEQ])
        dmask = A.bf16([4 * 512])
        S_.dma(dmask, c.d_dmask[:, :], w=["dmask"], q="pool")
        mark = A.off
        posi = A.t[:, A.off:A.off + SEQ].bitcast(I32)
        A.off += SEQ
        ang = A.f32([SEQ])
        S_.dma(posi, c.d_pos[s:s + 1, :].to_broadcast([128, SEQ]), w=["posi"])
        S_.cp(ang, posi, r=["posi"], w=["r_ang"])
        S_.ts(ang, ang, cs(c, "invf"), ALU.mult, r=["r_ang", "cst"], w=["r_ang"])
        sincos_tables(S_, c, A, ang, SEQ, sinT, cosT, "r_")
        S_.barrier()
        A.off = mark
        lt = A.f32([64])
        sm = c.sm
        for i, (a_, b_) in enumerate((("diff_lq1", "diff_lk1"), ("diff_lq2", "diff_lk2"))):
            S_.tt(lt, V(c, a_), V(c, b_), ALU.mult, r=["vec"], w=["lt"])
            S_.add("dve", lambda e, i=i: e.reduce_sum(out=sm[:, 16 + i:17 + i], in_=lt, axis=AX.X), r=["lt"], w=[f"lsum{i}"])
            S_.act(sm[:, 18 + i:19 + i], sm[:, 16 + i:17 + i], AF.Exp, r=[f"lsum{i}"], w=[f"lexp{i}"])
        S_.tt(sm[:, 20:21], sm[:, 19:20], sm[:, 18:19], ALU.subtract, r=["lexp0", "lexp1"], w=["nl0"])
        S_.ts(sm[:, 21:22], sm[:, 20:21], -lam_init, ALU.add, r=["nl0"], w=["neglam"])
        S_.ts(sm[:, 22:23], V(c, "diff_subln_g"), 1.0 - lam_init, ALU.mult, r=["vec"], w=["gs"])
        neglam, gs = sm[:, 21:22], sm[:, 22:23]
        qr = A.bf16([SEQ]); kr = A.bf16([SEQ]); vtok = A.bf16([16, 128]); szT = A.bf16([SEQ]); gout = A.bf16([SEQ])
        qf = A.f32([512]); sq = A.bf16([512]); rs = A.f32([512]); tmp = A.f32([512]); qn = A.bf16([512])
        t1 = A.f32([512]); t2 = A.f32([512])
        pT = [[A.bf16([512]) for _ in range(2)] for _ in range(2)]
        rl = [A.f32([512]) for _ in range(2)]; on = [A.f32([512]) for _ in range(2)]; dd = A.f32([512]); g1 = A.f32([512])
        for hd in range(16):
            for nm, cb, dst, gname in (("q", CB_Q + hd, qr, "diff_q_g"), ("k", CB_K + hd, kr, "diff_k_g")):
                wt, wk = load_w(S_, c, win[cb], 8)
                for tb in range(4):
                    tsl = slice(tb * 512, (tb + 1) * 512)
                    b = proj_fm(S_, c, wt, wk, hT, tb)
                    S_.cp(qf, c.ps[b][:, :], r=[f"ps{b}"], w=["qf"])
                    S_.act(sq, c.ps[b][:, :], AF.Square, r=[f"ps{b}"], w=["sq"])
                    b2 = nextps(c)
                    S_.mm(c.ps[b2][:, :], c.b["blk64"], sq, r=["cbf", "sq"], w=[f"ps{b2}"])
                    rstd_from_ss(S_, c, c.ps[b2][:, :], rs, 64, [f"ps{b2}"], "rs", tmp)
                    S_.stt(qn, qf, V(c, gname), rs, ALU.mult, ALU.mult, r=["qf", "rs", "vec"], w=["qn"])
                    b3 = nextps(c)
                    S_.mm(c.ps[b3][:, :], c.b["rotT"], qn, r=["cbf", "qn"], w=[f"ps{b3}"])
                    S_.tt(t1, qn, cosT[:, tsl], ALU.mult, r=["qn", "r_c"], w=["t1"], eng="pool")
                    S_.tt(t2, c.ps[b3][:, :], sinT[:, tsl], ALU.mult, r=[f"ps{b3}", "r_s"], w=["t2"])
                    S_.tt(dst[:, tsl], t1, t2, ALU.add, r=["t1", "t2"], w=[nm + "r"])
            wt, wk = load_w(S_, c, win[CB_V + hd], 8)
            for tg in range(4):
                b = nextps(c)
                for j in range(4):
                    tt = tg * 4 + j
                    for kc in range(8):
                        S_.mm(c.ps[b][:, j * 128:(j + 1) * 128], hT[:, kc, tt * 128:(tt + 1) * 128], wt[:, kc, :],
                              start=(kc == 0), stop=(kc == 7), r=["hT", wk], w=[f"ps{b}"])
                S_.cp(vtok[:, tg * 4:tg * 4 + 4, :], c.ps[b][:, :].rearrange("p (j v) -> p j v", j=4), r=[f"ps{b}"], w=["vtok"])
            wt, wk = load_w(S_, c, win[CB_Z + hd], 8)
            for tb in range(4):
                b = proj_fm(S_, c, wt, wk, hT, tb)
                S_.act(szT[:, tb * 512:(tb + 1) * 512], c.ps[b][:, :], AF.Silu, r=[f"ps{b}"], w=["szT"])
            for qb in range(4):
                qsl = slice(qb * 512, (qb + 1) * 512)
                caps = []
                for cp_ in range(2):
                    S_.capture()
                    r0 = 64 * cp_
                    nkt = 4 * qb + 4
                    bo, bl = (6, 7) if cp_ == 0 else (4, 5)

                    def st_mm(kt, cp_=cp_):
                        b_ = 2 * cp_ + (kt % 2)
                        S_.mm(c.ps[b_][:, :], kr[r0:r0 + 64, kt * 128:(kt + 1) * 128], qr[r0:r0 + 64, qsl],
                              r=["kr", "qr"], w=[f"ps{b_}"])
                        return b_
                    bnext = st_mm(0)
                    for kt in range(nkt):
                        b_ = bnext
                        if kt + 1 < nkt:
                            bnext = st_mm(kt + 1)
                        p_ = pT[cp_][kt % 2]; pk = f"pT{cp_}{kt % 2}"
                        S_.act(p_, c.ps[b_][:, :], AF.Exp, r=[f"ps{b_}"], w=[pk], scale=0.125)
                        if kt >= 4 * qb:
                            j = kt - 4 * qb
                            S_.tt(p_, p_, dmask[:, j * 512:(j + 1) * 512], ALU.mult, r=[pk, "dmask"], w=[pk], eng="pool")
                        S_.mm(c.ps[bo][:, :], vtok[:, kt, :], p_, start=(kt == 0), stop=(kt == nkt - 1), r=["vtok", pk], w=[f"ps{bo}"])
                        S_.mm(c.ps[bl][:, :], c.b["ones"], p_, start=(kt == 0), stop=(kt == nkt - 1), r=["cbf", pk], w=[f"ps{bl}"])
                    S_.add("dve", lambda e, cp_=cp_, bl=bl: e.reciprocal(out=rl[cp_], in_=c.ps[bl][:, :]), r=[f"ps{bl}"], w=[f"rl{cp_}"])
                    S_.tt(on[cp_], c.ps[bo][:, :], rl[cp_], ALU.mult, r=[f"ps{bo}", f"rl{cp_}"], w=[f"on{cp_}"])
                    caps.append(S_.end_capture())
                S_.replay_interleaved(caps)
                S_.stt(dd, on[1], neglam, on[0], ALU.mult, ALU.add, r=["on0", "on1", "neglam"], w=["dd"])
                S_.act(sq, dd, AF.Square, r=["dd"], w=["sq"])
                b2 = nextps(c)
                S_.mm(c.ps[b2][:, :], c.b["ones"], sq, r=["cbf", "sq"], w=[f"ps{b2}"])
                rstd_from_ss(S_, c, c.ps[b2][:, :], rs, 128, [f"ps{b2}"], "rs", tmp)
                S_.stt(g1, dd, gs, rs, ALU.mult, ALU.mult, r=["dd", "gs", "rs"], w=["g1"])
                S_.tt(gout[:, qsl], g1, szT[:, qsl], ALU.mult, r=["g1", "szT"], w=["gout"])
            S_.dma(G[s, hd * 128:(hd + 1) * 128, :], gout, r=["gout"], w=[("G", s, hd)])
        c.nrot = 6
        xattn_seq(S_, c, A, l, s, hT, win, CB_ZX, CB_QM, G)
        S_.barrier()
        phase_c(S_, c, l, s, xin, xout, G)

def ssd_layer(S_, c, l, xin, xout):
    win = c.d_win[l]
    CB_XS, CB_B, CB_C, CB_DT, CB_ZM, CB_ZX, CB_QM = 0, 16, 24, 32, 33, 49, 53
    G = c.d_G
    Uf, Of = cs(c, "U"), cs(c, "ones")
    for s in range(c.NSQ):
        A = Arena(c)
        hT = phase_a(S_, c, A, xin, s, l)
        S_.barrier()
        A.off = SEQ * 8 // 2
        wtiles(c, A, 3, 8)
        mark0 = A.off
        yz = A.bf16([16, SEQ])
        dt = A.f32([16, 32]); dta = A.f32([16, 32]); cumT = A.f32([16, 32]); lastT = A.f32([16, 32])
        elast = A.f32([16, 32]); ws = A.f32([16, 32]); aneg = A.f32([32])
        wt, wk = load_w(S_, c, win[CB_DT], 8)
        for ck in range(NT):
            for kc in range(8):
                S_.mm(c.ps[6][:, ck * 32:(ck + 1) * 32], hT[:, kc, ck * 128:(ck + 1) * 128], wt[:, kc, 0:32],
                      start=(kc == 0), stop=(kc == 7), r=["hT", wk], w=["ps6"])
        bias = V(c, "ssd_dt_bias").unsqueeze(1).to_broadcast([128, 16, 32])
        S_.tt(dt, c.ps[6][:, :].rearrange("p (k h) -> p k h", k=16), bias, ALU.add, r=["ps6", "vec"], w=["dt"])
        S_.act(dt, dt, AF.Exp, r=["dt"], w=["dt"])
        S_.act(dt, dt, AF.Ln, r=["dt"], w=["dt"], bias=1.0)
        S_.act(aneg, V(c, "ssd_a_log"), AF.Exp, r=["vec"], w=["aneg"])
        S_.ts(aneg, aneg, -1.0, ALU.mult, r=["aneg"], w=["aneg"])
        S_.tt(dta, dt, aneg.unsqueeze(1).to_broadcast([128, 16, 32]), ALU.mult, r=["dt", "aneg"], w=["dta"])
        for ck in range(NT):
            S_.mm(c.ps[6][:, ck * 32:(ck + 1) * 32], Uf, dta[:, ck, :], r=["cst", "dta"], w=["ps6"])
            S_.mm(c.ps[7][:, ck * 32:(ck + 1) * 32], Of, dta[:, ck, :], r=["cst", "dta"], w=["ps7"])
        S_.cp(cumT, c.ps[6][:, :].rearrange("p (k h) -> p k h", k=16), r=["ps6"], w=["cumT"])
        S_.cp(lastT, c.ps[7][:, :].rearrange("p (k h) -> p k h", k=16), r=["ps7"], w=["lastT"])
        S_.act(elast, lastT, AF.Exp, r=["lastT"], w=["elast"])
        S_.tt(ws, lastT, cumT, ALU.subtract, r=["lastT", "cumT"], w=["ws"])
        S_.act(ws, ws, AF.Exp, r=["ws"], w=["ws"])
        S_.tt(ws, ws, dt, ALU.mult, r=["ws", "dt"], w=["ws"])
        xsT = A.bf16([2, SEQ]); BT = A.bf16([SEQ]); CT = A.bf16([SEQ]); szT = A.bf16([2, SEQ])
        St = A.f32([4, 64]); Sb = A.bf16([4, 64])
        mk_ = A.off
        raw = A.f32([SEQ + 4]); acc = A.f32([SEQ])
        end1 = A.off
        A.off = mk_
        TB_ = [dict(xtok=A.bf16([256]), btok=A.bf16([128]), CBm=A.f32([128]), CBd=A.f32([4, 128]), Z=A.f32([4, 128]),
                    dm=A.f32([4, 128]), mm=A.bf16([4, 128]), ecr=A.f32([4, 128]), Cs=A.bf16([4, 128]), xw=A.bf16([4, 64]),
                    yf=A.f32([2, 128])) for _ in range(2)]
        A.off = max(A.off, end1)
        Ub = Uf.unsqueeze(1).to_broadcast([128, 4, 128])
        for g in range(8):
            S_.barrier()
            S_.memset(raw[:, 0:4], 0.0, w=["raw"], eng="pool")
            srcs = [(CB_XS + 2 * g, xsT[:, 0, :], 2 * g), (CB_XS + 2 * g + 1, xsT[:, 1, :], 2 * g + 1),
                    (CB_B + g, BT, 16 + g), (CB_C + g, CT, 24 + g)]
            for cb, dst, cch in srcs:
                wt, wk = load_w(S_, c, win[cb], 8)
                for tb in range(4):
                    b = proj_fm(S_, c, wt, wk, hT, tb)
                    S_.act(raw[:, 3 + tb * 512:3 + (tb + 1) * 512], c.ps[b][:, :], AF.Copy, r=[f"ps{b}"], w=["raw"])
                cw = lambda j: V(c, "ssd_conv_w", cch * 4 + j)
                S_.ts(acc, raw[:, 3:3 + SEQ], cw(3), ALU.mult, V(c, "ssd_conv_b", cch), ALU.add, r=["raw", "vec"], w=["acc"], eng="pool")
                for j in range(3):
                    S_.stt(acc, raw[:, j:j + SEQ], cw(j), acc, ALU.mult, ALU.add, r=["raw", "vec", "acc"], w=["acc"])
                S_.act(dst, acc, AF.Silu, r=["acc"], w=["xbc"])
            for i in range(2):
                wt, wk = load_w(S_, c, win[CB_ZM + 2 * g + i], 8)
                for tb in range(4):
                    b = proj_fm(S_, c, wt, wk, hT, tb)
                    S_.act(szT[:, i, tb * 512:(tb + 1) * 512], c.ps[b][:, :], AF.Silu, r=[f"ps{b}"], w=["szT"])
            S_.barrier()
            S_.memset(St, 0.0, w=["St"], eng="pool")
            S_.memset(Sb, 0.0, w=["Sb"], eng="pool")
            h0 = 4 * g
            def front(ck):
                    tsl = slice(ck * 128, (ck + 1) * 128)
                    T_ = TB_[ck % 2]
                    xtok, btok, CBm, CBd, Z, dm, mm_, ecr, Cs, xw, yf = (T_[k_] for k_ in
                                                                         ("xtok", "btok", "CBm", "CBd", "Z", "dm", "mm", "ecr", "Cs", "xw", "yf"))
                    P_ = str(ck % 2)
                    b = nextps(c)
                    pb = c.ps[b][:, :].bitcast(BF16)
                    for i in range(2):
                        S_.tr(pb[:, i * 128:(i + 1) * 128], xsT[:, i, tsl], c.b["ident"], r=["xbc", "cbf"], w=[f"ps{b}"])
                    S_.tr(pb[:, 256:384], BT[:, tsl], c.b["ident"], r=["xbc", "cbf"], w=[f"ps{b}"])
                    S_.act(xtok, pb[:, 0:256], AF.Copy, r=[f"ps{b}"], w=["xtok" + P_])
                    S_.act(btok, pb[:, 256:384], AF.Copy, r=[f"ps{b}"], w=["btok" + P_])
                    b1 = nextps(c)
                    S_.mm(c.ps[b1][:, 0:128], BT[:, tsl], CT[:, tsl], r=["xbc"], w=[f"ps{b1}"])
                    S_.tt(CBm, c.ps[b1][:, 0:128], Uf, ALU.mult, r=[f"ps{b1}", "cst"], w=["CBm" + P_])
                    dtb = dt[:, ck, h0:h0 + 4].unsqueeze(2).to_broadcast([128, 4, 128])
                    S_.tt(CBd, CBm.unsqueeze(1).to_broadcast([128, 4, 128]), dtb, ALU.mult, r=["CBm" + P_, "dt"], w=["CBd" + P_], eng="pool")
                    S_.tt(Z, Ub, dta[:, ck, h0:h0 + 4].unsqueeze(2).to_broadcast([128, 4, 128]), ALU.mult,
                          r=["cst", "dta"], w=["Z" + P_], eng="pool")
                    b2 = nextps(c)
                    S_.mm(c.ps[b2][:, :], Of, Z.rearrange("p r t -> p (r t)"), r=["cst", "Z" + P_], w=[f"ps{b2}"])
                    psv = c.ps[b2][:, :].rearrange("p (r t) -> p r t", r=4)
                    S_.act(ecr, psv, AF.Exp, r=[f"ps{b2}"], w=["ecr" + P_])
                    S_.tt(dm, psv, cumT[:, ck, h0:h0 + 4].unsqueeze(2).to_broadcast([128, 4, 128]), ALU.subtract,
                          r=[f"ps{b2}", "cumT"], w=["dm" + P_])
                    S_.ts(dm, dm, 0.0, ALU.min, r=["dm" + P_], w=["dm" + P_])
                    S_.act(dm, dm, AF.Exp, r=["dm" + P_], w=["dm" + P_])
                    S_.tt(mm_, dm, CBd, ALU.mult, r=["dm" + P_, "CBd" + P_], w=["mm" + P_])
                    S_.tt(Cs, CT[:, tsl].unsqueeze(1).to_broadcast([128, 4, 128]), ecr, ALU.mult, r=["xbc", "ecr" + P_], w=["Cs" + P_], eng="pool")
                    S_.tt(xw, xtok.rearrange("p (r q) -> p r q", r=4), ws[:, ck, h0:h0 + 4].unsqueeze(2).to_broadcast([128, 4, 64]),
                          ALU.mult, r=["xtok" + P_, "ws"], w=["xw" + P_], eng="pool")
            def tail(ck):
                    tsl = slice(ck * 128, (ck + 1) * 128)
                    T_ = TB_[ck % 2]
                    xtok, btok, CBm, CBd, Z, dm, mm_, ecr, Cs, xw, yf = (T_[k_] for k_ in
                                                                         ("xtok", "btok", "CBm", "CBd", "Z", "dm", "mm", "ecr", "Cs", "xw", "yf"))
                    P_ = str(ck % 2)
                    by = 7
                    b3 = 6
                    for r_ in range(4):
                        i, po = r_ // 2, 64 * (r_ % 2)
                        S_.mm(c.ps[by][po:po + 64, i * 128:(i + 1) * 128], xtok[:, r_ * 64:(r_ + 1) * 64], mm_[:, r_, :], start=True, stop=False,
                              r=["xtok" + P_, "mm" + P_], w=[f"ps{by}"])
                        S_.mm(c.ps[by][po:po + 64, i * 128:(i + 1) * 128], Sb[:, r_, :], Cs[:, r_, :], start=False, stop=True,
                              r=["Sb", "Cs" + P_], w=[f"ps{by}"])
                    for r_ in range(4):
                        S_.mm(c.ps[b3][:, r_ * 64:(r_ + 1) * 64], btok, xw[:, r_, :], r=["btok" + P_, "xw" + P_], w=[f"ps{b3}"])
                    S_.tt(St, St, elast[:, ck, h0:h0 + 4].unsqueeze(2).to_broadcast([128, 4, 64]), ALU.mult, r=["St", "elast"], w=["St"])
                    S_.tt(St, St, c.ps[b3][:, 0:256].rearrange("p (r q) -> p r q", r=4), ALU.add, r=["St", f"ps{b3}"], w=["St"])
                    S_.act(Sb, St, AF.Copy, r=["St"], w=["Sb"])
                    for i in range(2):
                        S_.stt(yf[:, i, :], xsT[:, i, tsl], V(c, "ssd_d", 2 * g + i), c.ps[by][:, i * 128:(i + 1) * 128], ALU.mult, ALU.add,
                               r=["xbc", "vec", f"ps{by}"], w=["yf" + P_])
                        S_.tt(yz[:, 2 * g + i, tsl], yf[:, i, :], szT[:, i, tsl], ALU.mult, r=["yf" + P_, "szT"], w=["yz"])
            for ck in range(0, NT, 2):
                caps = []
                for d_ in range(2):
                    S_.capture()
                    front(ck + d_)
                    caps.append(S_.end_capture())
                S_.replay_interleaved(caps)
                tail(ck)
                tail(ck + 1)
        S_.barrier()
        A.off = mark0 + 16 * SEQ // 2
        sq = A.bf16([512]); rs = A.f32([512]); tmp = A.f32([512]); gst = A.bf16([2, 512])
        for tb in range(4):
            tsl = slice(tb * 512, (tb + 1) * 512)
            for cc in range(16):
                S_.act(sq, yz[:, cc, tsl], AF.Square, r=["yz"], w=["sq"])
                S_.mm(c.ps[6][:, :], c.b["ones"], sq, start=(cc == 0), stop=(cc == 15), r=["cbf", "sq"], w=["ps6"])
            rstd_from_ss(S_, c, c.ps[6][:, :], rs, 2048, ["ps6"], "rs", tmp)
            for cc in range(16):
                i = cc % 2
                S_.stt(gst[:, i, :], yz[:, cc, tsl], V(c, "ssd_norm_g", cc), rs, ALU.mult, ALU.mult, r=["yz", "vec", "rs"], w=[f"gst{i}"])
                S_.dma(G[s, cc * 128:(cc + 1) * 128, tsl], gst[:, i, :], r=[f"gst{i}"], w=[("G", s, cc)])
        xattn_seq(S_, c, A, l, s, hT, win, CB_ZX, CB_QM, G)
        S_.barrier()
        phase_c(S_, c, l, s, xin, xout, G)

LAYER_FN = {}


def build_nc(layers, NSQ, meta):
    nc = bass.Bass("TRN2", target_bir_lowering=False)
    c = Ctx()
    c.NSQ = NSQ
    c.cmap, c.ncst = meta["cmap"], meta["ncst"]
    c.vmap, c.nvec = meta["vmap"], meta["nvec"]

    def din(name, shape, dt=F32):
        return nc.dram_tensor(name, list(shape), dt, kind="ExternalInput").ap()

    def dscr(name, shape, dt):
        return nc.dram_tensor(name, list(shape), dt, kind="Internal").ap()

    x = din("x", [NSQ, SEQ, D])
    c.d_mem = din("mem", [NSQ, 256, D])
    c.d_pos = din("pos", [NSQ, SEQ], I32)
    c.d_cst = din("cst", [128, c.ncst])
    c.d_vec = din("vec", [128, c.nvec])
    c.d_win, c.d_wout, c.d_wkv = {}, {}, {}
    for l in layers:
        c.d_win[l] = din(f"win{l}", [meta["ncb"][l], 128, 8, 128])
        c.d_wout[l] = din(f"wout{l}", [128, 20, 1024])
        c.d_wkv[l] = din(f"wkv{l}", [128, 8, 1024])
    for nm, shp in meta["extra"].items():
        if int(nm[1]) in layers or nm[0] != "L":
            setattr(c, "d_" + nm[3:], din(nm, shp))
    y = nc.dram_tensor("y", [NSQ, SEQ, D], F32, kind="ExternalOutput").ap()
    c.d_G = dscr("G", [NSQ, 2560, SEQ], BF16)
    if 0 in layers:
        c.d_U = dscr("U", [NSQ, 2048, SEQ], BF16)
        c.d_GS = dscr("GS", [NSQ, 2048, SEQ], BF16)
    xs = [x]
    for i in range(len(layers) - 1):
        xs.append(dscr(f"xs{i}", [NSQ, SEQ, D], F32))
    xs.append(y)
    with ExitStack() as es:
        S_ = Sched(nc, es)
        setup_common(S_, c, nc)
        c.stop = meta.get("stop", 99)
        mem_prologue(S_, c)
        for i, l in enumerate(layers):
            if c.stop < 2:
                break
            xattn_layer_prologue(S_, c, l)
            if c.stop < 3:
                break
            LAYER_FN[l](S_, c, l, xs[i], xs[i + 1])
        print("ops recorded:", len(S_.ops), "arena words:", c.arena_n, flush=True)
        S_.emit()
    return nc


def _blk(W, c0, ncols):
    nb = (ncols + 127) // 128
    K = W.shape[0] // 128
    Wp = np.zeros((W.shape[0], nb * 128), np.float32)
    Wp[:, :ncols] = W[:, c0:c0 + ncols]
    return np.ascontiguousarray(Wp.reshape(K, 128, nb, 128).transpose(2, 1, 0, 3))


def host_prep(inp):
    f = lambda a: np.asarray(a, np.float32)
    meta = {"ncb": {}, "extra": {}}
    shared = {}
    cols = []
    cmap = {}

    def addc(name, arr):
        a = sum(x.shape[1] for x in cols)
        cols.append(arr.astype(np.float32))
        cmap[name] = (a, a + arr.shape[1])

    i128 = np.arange(128)
    addc("ident", np.eye(128))
    addc("U", (i128[:, None] <= i128[None, :]).astype(np.float32))
    addc("ones", np.ones((128, 128)))
    blk = np.zeros((128, 128)); blk[:64, :64] = 1; blk[64:, 64:] = 1
    addc("blk64", blk)
    rot = np.zeros((128, 128))
    for cpt in range(2):
        for d in range(64):
            if d < 32:
                rot[cpt * 64 + d + 32, cpt * 64 + d] = -1.0
            else:
                rot[cpt * 64 + d - 32, cpt * 64 + d] = 1.0
    addc("rotT", rot)
    addc("Lst", (i128[:, None] > i128[None, :]).astype(np.float32))
    addc("eps", np.full((128, 1), EPS))
    addc("maskq", np.stack([((i128 // 16) % 4 == j) for j in range(4)], 1).astype(np.float32))
    inv = 10000.0 ** (-np.arange(0, 64, 2, dtype=np.float32) / 64)
    addc("invf", inv[(i128 % 64) % 32][:, None])
    shared["cst"] = np.ascontiguousarray(np.concatenate(cols, 1))
    meta["cmap"], meta["ncst"] = cmap, shared["cst"].shape[1]
    vcols = []
    vmap = {}

    def addv(name, arr):
        a = sum(x.shape[1] for x in vcols)
        vcols.append(np.asarray(arr, np.float32))
        vmap[name] = (a, a + arr.shape[1])

    addv("norm_g", f(inp["norm_g"]).reshape(4, 8, 128).transpose(2, 0, 1).reshape(128, 32))
    addv("mem_norm_g", f(inp["mem_norm_g"]).reshape(8, 128).T)
    addv("xq_g", f(inp["xq_g"]).T)
    addv("xk_g", f(inp["xk_g"]).T)
    addv("s5_d", f(inp["s5_d"])[0].reshape(16, 128).T)
    addv("gla_norm_g", f(inp["gla_norm_g"])[0].reshape(4, 128).T)
    addv("diff_q_g", np.tile(f(inp["diff_q_g"])[0], 2)[:, None])
    addv("diff_k_g", np.tile(f(inp["diff_k_g"])[0], 2)[:, None])
    addv("diff_subln_g", f(inp["diff_subln_g"])[0][:, None])
    for nm in ("diff_lq1", "diff_lk1", "diff_lq2", "diff_lk2"):
        addv(nm, np.tile(f(inp[nm])[0][None, :], (128, 1)))
    addv("ssd_conv_w", f(inp["ssd_conv_w"])[0].reshape(4, 32, 128).transpose(2, 1, 0).reshape(128, 128))
    addv("ssd_conv_b", f(inp["ssd_conv_b"])[0].reshape(32, 128).T)
    addv("ssd_d", np.repeat(f(inp["ssd_d"])[0], 64).reshape(16, 128).T)
    addv("ssd_norm_g", f(inp["ssd_norm_g"])[0].reshape(16, 128).T)
    addv("ssd_dt_bias", np.tile(f(inp["ssd_dt_bias"])[0][None, :], (128, 1)))
    addv("ssd_a_log", np.tile(f(inp["ssd_a_log"])[0][None, :], (128, 1)))
    shared["vec"] = np.ascontiguousarray(np.concatenate(vcols, 1))
    meta["vmap"], meta["nvec"] = vmap, shared["vec"].shape[1]
    wins = {0: f(inp["s5_w_in"])[0], 1: f(inp["gla_w_in"])[0], 2: f(inp["diff_w_in"])[0], 3: f(inp["ssd_w_in"])[0]}
    shared["win0"] = _blk(wins[0], 0, 5120)
    W = wins[1]
    shared["win1"] = np.concatenate([_blk(W, 0, 3072), _blk(W, 3072, 16), _blk(W, 3088, 3072)], 0)
    shared["win2"] = _blk(wins[2], 0, 9216)
    W = wins[3]
    shared["win3"] = np.concatenate([_blk(W, 0, 4096), _blk(W, 4096, 32), _blk(W, 4128, 3072)], 0)
    for l in range(4):
        meta["ncb"][l] = shared[f"win{l}"].shape[0]
        shared[f"wout{l}"] = np.ascontiguousarray(f(inp["w_out"])[l].reshape(20, 128, 1024).transpose(1, 0, 2))
        shared[f"wkv{l}"] = np.ascontiguousarray(f(inp["w_mem_kv"])[l].reshape(8, 128, 1024).transpose(1, 0, 2))
    ex = {}
    ex["L0_wglu"] = np.ascontiguousarray(f(inp["s5_w_glu"])[0].reshape(16, 128, 16, 128).transpose(2, 1, 0, 3))
    lam_re, lam_im, ls = f(inp["s5_lam_re"])[0], f(inp["s5_lam_im"])[0], f(inp["s5_log_step"])[0]
    toA = lambda a: a.reshape(64, 2, 64).transpose(1, 2, 0).reshape(128, 64)
    lsf = np.repeat(ls[:, None], 64, 1)
    ex["L0_s5A"] = np.ascontiguousarray(np.stack([toA(lam_re), toA(lam_im), toA(lsf)], 1))
    toBl = lambda a: np.repeat(a.reshape(16, 8, 1, 64), 16, 2).transpose(1, 2, 0, 3).reshape(128, 1024)
    toBb = lambda b: b.reshape(16, 8, 64, 16).transpose(1, 3, 0, 2).reshape(128, 1024)
    ex["L0_s5B"] = np.ascontiguousarray(np.stack([toBl(lam_re), toBl(lam_im), toBl(lsf),
                                                    toBb(f(inp["s5_b_re"])[0]), toBb(f(inp["s5_b_im"])[0])], 1))
    Cc = np.zeros((2, 64, 2, 32, 2, 2, 2, 16), np.float32)
    for ri, nm in enumerate(("s5_c_re", "s5_c_im")):
        Cm = f(inp[nm])[0].reshape(32, 2, 2, 16, 64)
        for j in range(2):
            for e_ in range(2):
                Cc[j, :, ri, :, e_, e_, j, :] = Cm[:, e_, j].transpose(2, 0, 1)
    ex["L0_s5C"] = np.ascontiguousarray(Cc.reshape(128, 2, 64, 64))
    wg2 = np.zeros((17, 512), np.float32)
    wg2[:16] = f(inp["gla_w_gk2"])[0]
    wg2[16] = f(inp["gla_b_gk2"])[0]
    ex["L1_wgk2"] = wg2
    qi = np.arange(512)[None, :]
    ex["L2_dmask"] = np.concatenate([((128 * j + np.arange(128)[:, None]) <= qi).astype(np.float32) for j in range(4)], 1)
    for k_, v_ in ex.items():
        meta["extra"][k_] = list(v_.shape)
        shared[k_] = v_
    return shared, meta


def run_layers(inp, layers, NSQ, ncores, xin=None, stop=99, trace=False):
    shared, meta = host_prep(inp)
    meta["stop"] = stop
    nc = build_nc(layers, NSQ, meta)
    x = np.asarray(inp["x"], np.float32) if xin is None else xin
    mem = np.asarray(inp["mem"], np.float32)
    pos = np.asarray(inp["positions"], np.int32)
    names = ["cst", "vec"] + [f"{p}{l}" for l in layers for p in ("win", "wout", "wkv")]
    names += [k_ for k_ in meta["extra"] if int(k_[1]) in layers]
    in_maps = []
    for ci in range(ncores):
        sl = slice(ci * NSQ, (ci + 1) * NSQ)
        m = {"x": np.ascontiguousarray(x[sl]), "mem": np.ascontiguousarray(mem[sl]), "pos": np.ascontiguousarray(pos[sl])}
        for nm in names:
            m[nm] = shared[nm]
        in_maps.append(m)
    res = run_bass_kernel_spmd(nc, in_maps, core_ids=list(range(ncores)), **({"trace": True} if trace else {}))
    if trace:
        print("EXEC_NS", layers, res.exec_time_ns, flush=True)
    return np.concatenate([r["y"] for r in res.results], 0)


def kernel(**inputs):
    return run_layers(inputs, [0, 1, 2, 3], 4, 8)

LAYER_FN[0] = s5_layer
LAYER_FN[1] = gla_layer
LAYER_FN[2] = diff_layer
LAYER_FN[3] = ssd_layer
```

```python
import numpy as np, math
from contextlib import ExitStack
import concourse.bass as bass
import concourse.mybir as mybir
from concourse.bass_utils import run_bass_kernel_spmd

F32 = mybir.dt.float32
BF16 = mybir.dt.bfloat16
I32 = mybir.dt.int32
AF = mybir.ActivationFunctionType
ALU = mybir.AluOpType
AX = mybir.AxisListType


class Sched:
    STREAMS = ("pe", "act", "dve", "pool", "sp")
    NSLOT = 8
    MAXV = 30000

    def __init__(self, nc, es):
        self.nc = nc
        self.es = es
        self.ops = []
        self.lastw = {}
        self.readers = {}
        self.n_ps = 0

    def sb(self, name, shape, dt):
        return self.es.enter_context(self.nc.sbuf_tensor("sb_" + name, list(shape), dt))

    def psum(self, name, shape, dt):
        return self.es.enter_context(self.nc.psum_tensor(name, list(shape), dt))

    def capture(self):
        self._cap = []

    def end_capture(self):
        lst, self._cap = self._cap, None
        return lst

    def replay_interleaved(self, lists):
        its = [list(l) for l in lists]
        pos = [0] * len(its)
        left = sum(len(l) for l in its)
        while left:
            for k, l in enumerate(its):
                if pos[k] < len(l):
                    self.add(*l[pos[k]])
                    pos[k] += 1
                    left -= 1

    def add(self, stream, fn, r=(), w=(), kind="cmp"):
        if getattr(self, "_cap", None) is not None:
            self._cap.append((stream, fn, tuple(r), tuple(w), kind))
            return -1
        i = len(self.ops)
        deps = set()
        px = [k for k in r if isinstance(k, str) and k[:2] == "ps" and k[2:].isdigit()]
        if px:
            r = [k for k in r if k not in px]
            w = list(w) + [k for k in px if k not in w]
        for k in r:
            j = self.lastw.get(k)
            if j is not None:
                deps.add(j)
        for k in w:
            j = self.lastw.get(k)
            if j is not None:
                deps.add(j)
            rd = self.readers.get(k)
            if rd:
                deps.update(rd.values())
        for k in r:
            rd = self.readers.setdefault(k, {})
            rk = stream if kind == "cmp" else ("dma", i)
            rd[rk] = i
        for k in w:
            self.lastw[k] = i
            self.readers[k] = {}
        self.ops.append((stream, kind, fn, deps))
        return i

    def dma(self, out, in_, r=(), w=(), q="sp", **kw):
        return self.add(q, lambda e: e.dma_start(out=out, in_=in_, **kw), r, w, kind="dma")

    def mm(self, out, lhsT, rhs, start=True, stop=True, r=(), w=(), **kw):
        return self.add("pe", lambda e: e.matmul(out, lhsT, rhs, start=start, stop=stop, **kw), r, w)

    def tr(self, out, in_, ident, r=(), w=()):
        return self.add("pe", lambda e: e.transpose(out, in_, ident), r, w)

    def act(self, out, in_, func, r=(), w=(), eng="act", **kw):
        return self.add(eng, lambda e: e.activation(out=out, in_=in_, func=func, **kw), r, w)

    def tt(self, out, in0, in1, op, r=(), w=(), eng="dve"):
        return self.add(eng, lambda e: e.tensor_tensor(out=out, in0=in0, in1=in1, op=op), r, w)

    def ts(self, out, in0, s1, op0, s2=None, op1=None, r=(), w=(), eng="dve", **kw):
        if op1 is None:
            return self.add(eng, lambda e: e.tensor_scalar(out=out, in0=in0, scalar1=s1, scalar2=None, op0=op0, **kw), r, w)
        return self.add(eng, lambda e: e.tensor_scalar(out=out, in0=in0, scalar1=s1, scalar2=s2, op0=op0, op1=op1, **kw), r, w)

    def stt(self, out, in0, scalar, in1, op0, op1, r=(), w=(), eng="dve"):
        return self.add(eng, lambda e: e.scalar_tensor_tensor(out=out, in0=in0, scalar=scalar, in1=in1, op0=op0, op1=op1), r, w)

    def cp(self, out, in_, r=(), w=(), eng="dve"):
        return self.add(eng, lambda e: e.tensor_copy(out=out, in_=in_), r, w)

    def memset(self, ap, val, w=(), eng="pool"):
        return self.add(eng, lambda e: e.memset(ap, val), (), w)

    def barrier(self):
        n = len(self.ops)
        last = {}
        dmas = set()
        start = getattr(self, "_bar_from", 0)
        for i in range(start, n):
            st, kind = self.ops[i][0], self.ops[i][1]
            if self.ops[i][2] is None:
                continue
            if kind == "dma":
                dmas.add(i)
            else:
                last[st] = i
        deps = set(last.values()) | dmas
        for st in self.STREAMS:
            self.ops.append((st, "cmp", None, set(deps)))
        self._bar_from = len(self.ops)
        self.lastw = {}
        self.readers = {}

    def emit(self):
        nc = self.nc
        ops = self.ops
        n = len(ops)
        need = [False] * n
        for i, (st, kind, fn, deps) in enumerate(ops):
            for j in deps:
                sj, kj = ops[j][0], ops[j][1]
                if kj == "dma" or kind == "dma" or sj != st or fn is None or st != "pe":
                    need[j] = True
        sems = {}

        def newsem(name):
            return self.es.enter_context(nc.semaphore(name))

        sig = [None] * n
        guard = [None] * n
        cnt = {s: 0 for s in self.STREAMS}
        cur = {}
        epoch = {s: 0 for s in self.STREAMS}
        dcount = {s: 0 for s in self.STREAMS}
        dslots = {}
        for i, (st, kind, fn, deps) in enumerate(ops):
            if kind == "dma":
                if st not in dslots:
                    dslots[st] = [newsem(f"d_{st}_{k}") for k in range(self.NSLOT)]
                k = dcount[st]
                dcount[st] += 1
                slot = k % self.NSLOT
                gen = k // self.NSLOT
                sig[i] = (dslots[st][slot], 16 * (gen + 1))
                guard[i] = (dslots[st][slot], 16 * gen)
            elif need[i]:
                if st not in cur or cnt[st] >= self.MAXV:
                    cur[st] = newsem(f"c_{st}_{epoch[st]}")
                    epoch[st] += 1
                    cnt[st] = 0
                cnt[st] += 1
                sig[i] = (cur[st], cnt[st])
        self.dslots = dslots
        self.dcount = dcount
        by_stream = {s: [] for s in self.STREAMS}
        for i, op in enumerate(ops):
            by_stream[op[0]].append(i)

        def run(stname, e):
            waited = {}

            def wait(sem, val):
                if val <= 0:
                    return
                key = id(sem)
                if waited.get(key, 0) < val:
                    e.wait_ge(sem, val)
                    waited[key] = val

            for i in by_stream[stname]:
                st, kind, fn, deps = ops[i]
                for j in sorted(deps):
                    sj, kj = ops[j][0], ops[j][1]
                    if kj == "cmp" and kind == "cmp" and sj == st and st == "pe" and fn is not None:
                        continue
                    wait(*sig[j])
                if kind == "dma":
                    wait(*guard[i])
                if fn is None:
                    continue
                ins = fn(e)
                if kind == "dma":
                    ins.then_inc(sig[i][0], 16)
                elif need[i]:
                    ins.then_inc(sig[i][0], 1)
            if stname == "sp":
                for q, sl in dslots.items():
                    tot = dcount[q]
                    for s in range(self.NSLOT):
                        ngen = (tot - s + self.NSLOT - 1) // self.NSLOT if tot > s else 0
                        wait(sl[s], 16 * ngen)

        with nc.Block() as block:
            @block.sync
            def _(e):
                run("sp", e)

            @block.tensor
            def _(e):
                run("pe", e)

            @block.scalar
            def _(e):
                run("act", e)

            @block.vector
            def _(e):
                run("dve", e)

            @block.gpsimd
            def _(e):
                run("pool", e)

D = 1024
SEQ = 2048
NT = 16
EPS = 1e-6
TWO_PI = 2.0 * math.pi
MAGIC = 12582912.0
CW1 = 6.28125
CW2 = 0.0019350051879882812
CW3 = TWO_PI - CW1 - CW2


class Ctx:
    pass


class Arena:
    def __init__(self, c):
        self.t = c.arena
        self.n = c.arena_n
        self.off = 0

    def f32(self, shape):
        n = 1
        for d in shape:
            n *= d
        a = self.off
        self.off += n
        assert self.off <= self.n, ("arena overflow", self.off, self.n)
        ap = self.t[:, a:a + n]
        return self._shape(ap, shape)

    def bf16(self, shape):
        n = 1
        for d in shape:
            n *= d
        nw = (n + 1) // 2
        a = self.off
        self.off += nw
        assert self.off <= self.n, ("arena overflow", self.off, self.n)
        ap = self.t[:, a:a + nw].bitcast(BF16)[:, 0:n]
        return self._shape(ap, shape)

    @staticmethod
    def _shape(ap, shape):
        if len(shape) == 1:
            return ap
        if len(shape) == 2:
            return ap.rearrange("p (a b) -> p a b", a=shape[0])
        if len(shape) == 3:
            return ap.rearrange("p (a b c) -> p a b c", a=shape[0], b=shape[1])
        if len(shape) == 4:
            return ap.rearrange("p (a b c d) -> p a b c d", a=shape[0], b=shape[1], c=shape[2])
        raise ValueError(shape)


def cs(c, name):
    a, b = c.cmap[name]
    return c.cst[:, a:b]


def V(c, name, i=None, n=1):
    a, b = c.vmap[name]
    if i is None:
        return c.vec[:, a:b]
    return c.vec[:, a + i:a + i + n]


def setup_common(S_, c, nc):
    c.ps = [S_.psum(f"ps{i}", [128, 512], F32) for i in range(8)]
    c.psn = 0
    c.wn = 0
    c.cst = S_.sb("cst", [128, c.ncst], F32)
    c.vec = S_.sb("vec", [128, c.nvec], F32)
    c.sm = S_.sb("sm", [128, 64], F32)
    S_.dma(c.cst[:], c.d_cst[:, :], w=["cst"])
    S_.dma(c.vec[:], c.d_vec[:, :], w=["vec"])
    c.identf = cs(c, "ident")
    names = ["ident", "U", "ones", "blk64", "rotT", "Lst"]
    c.cbf = S_.sb("cbf", [128, len(names) * 128], BF16)
    c.b = {}
    for i, nm in enumerate(names):
        S_.cp(c.cbf[:, i * 128:(i + 1) * 128], cs(c, nm), r=["cst"], w=["cbf"])
        c.b[nm] = c.cbf[:, i * 128:(i + 1) * 128]
    c.epsc = cs(c, "eps")
    c.memT = S_.sb("memT", [128, c.NSQ, 8, 256], BF16)
    c.KnT = S_.sb("KnT", [128, c.NSQ, 4, 256], BF16)
    c.Vm = S_.sb("Vm", [128, c.NSQ, 2, 512], BF16)
    rem = nc.sbuf_bytes_remaining
    c.arena_n = (rem - 2048) // 4
    c.arena = S_.sb("arena", [128, c.arena_n], F32)


def nextps(c):
    b = c.psn % getattr(c, "nrot", 6)
    c.psn = (c.psn + 1) % getattr(c, "nrot", 6)
    return b


def rstd_from_ss(S_, c, ss_ap, out_ap, n, rkeys, wkey, tmp):
    S_.act(tmp, ss_ap, AF.Sqrt, r=list(rkeys) + ["cst"], w=[wkey + "_t"], scale=1.0 / n, bias=c.epsc)
    S_.add("dve", lambda e: e.reciprocal(out=out_ap, in_=tmp), r=[wkey + "_t"], w=[wkey])


def norm_T(S_, c, src_ap, srckey, dst, dstkey, gbase, tcol, xt2, xs2, junk):
    i = c.nt_i
    c.nt_i ^= 1
    xt, xs = xt2[:, i, :], xs2[:, i, :]
    k, ks = f"xt{i}", f"xs{i}"
    sm0 = 32 + 4 * i
    S_.dma(xt, src_ap, r=srckey, w=[k])
    S_.act(junk, xt, AF.Square, r=[k], w=["junk", f"ssa{i}"], accum_out=c.sm[:, sm0:sm0 + 1])
    rstd_from_ss(S_, c, c.sm[:, sm0:sm0 + 1], c.sm[:, sm0 + 2:sm0 + 3], 1024, [f"ssa{i}"], f"rsa{i}", c.sm[:, sm0 + 1:sm0 + 2])
    S_.act(xs, xt, AF.Copy, r=[k, f"rsa{i}"], w=[ks], scale=c.sm[:, sm0 + 2:sm0 + 3])
    for half in range(2):
        b = nextps(c)
        for j in range(4):
            kc = half * 4 + j
            S_.tr(c.ps[b][:, j * 128:(j + 1) * 128], xs[:, kc * 128:(kc + 1) * 128], c.identf,
                  r=[ks, "cst"], w=[f"ps{b}"])
        a0 = gbase + half * 4
        g = c.vec[:, a0:a0 + 4].unsqueeze(2).to_broadcast([128, 4, 128])
        S_.tt(dst[:, half * 4:half * 4 + 4, tcol:tcol + 128],
              c.ps[b][:, :].rearrange("p (j t) -> p j t", j=4), g, ALU.mult,
              r=[f"ps{b}", "vec"], w=[dstkey])


def phase_a(S_, c, A, xin, s, l):
    hT = A.bf16([8, SEQ])
    xt2 = A.f32([2, 1024])
    xs2 = A.f32([2, 1024])
    junk = A.f32([1024])
    c.nt_i = 0
    for t2 in range(0, NT, 2):
        caps = []
        for tt in (t2, t2 + 1):
            S_.capture()
            norm_T(S_, c, xin[s, tt * 128:(tt + 1) * 128, :], [("X", id(xin), s, tt)], hT, "hT",
                   c.vmap["norm_g"][0] + l * 8, tt * 128, xt2, xs2, junk)
            caps.append(S_.end_capture())
        S_.replay_interleaved(caps)
    return hT


def wtiles(c, A, n=3, kc=16):
    c.wbuf = [A.bf16([kc, 128]) for _ in range(n)]
    c.wn = 0


def load_w(S_, c, dram_blk, kc):
    i = c.wn
    c.wn = (c.wn + 1) % len(c.wbuf)
    wt = c.wbuf[i]
    S_.dma(wt[:, 0:kc, :], dram_blk, r=[], w=[f"w{i}"], q="pool")
    return wt, f"w{i}"


def proj_fm(S_, c, wt, wk, hT, tb, nk=8, hk="hT", mcols=128, ntok=512):
    b = nextps(c)
    for kc in range(nk):
        S_.mm(c.ps[b][0:mcols, 0:ntok], wt[:, kc, 0:mcols], hT[:, kc, tb * ntok:(tb + 1) * ntok],
              start=(kc == 0), stop=(kc == nk - 1), r=[wk, hk], w=[f"ps{b}"])
    return b


def mem_prologue(S_, c):
    A = Arena(c)
    xt2 = A.f32([2, 1024])
    xs2 = A.f32([2, 1024])
    junk = A.f32([1024])
    c.nt_i = 0
    for s in range(c.NSQ):
        for mt in range(2):
            norm_T(S_, c, c.d_mem[s, mt * 128:(mt + 1) * 128, :], [], c.memT[:, s], "memT",
                   c.vmap["mem_norm_g"][0], mt * 128, xt2, xs2, junk)
    S_.barrier()


def xattn_layer_prologue(S_, c, l):
    A = Arena(c)
    wkv = A.bf16([8, 1024])
    kf = A.f32([256])
    sqb = A.bf16([256])
    rsb = A.f32([256])
    tmp = A.f32([256])
    for hf in range(2):
        S_.dma(wkv[:, :, hf * 512:(hf + 1) * 512], c.d_wkv[l][:, :, hf * 512:(hf + 1) * 512], r=[], w=["wkv"], q="pool")
    import os
    XS = int(os.environ.get("XSTOP", "9"))
    for s in range(c.NSQ):
        for h in range(4):
            if XS < 1:
                break
            b = nextps(c)
            for kc in range(8 if os.environ.get("XVAR") != "b" else 0):
                S_.mm(c.ps[b][:, 0:256], wkv[:, kc, h * 128:(h + 1) * 128], c.memT[:, s, kc, :],
                      start=(kc == 0), stop=(kc == 7), r=["wkv", "memT"], w=[f"ps{b}"])
            if os.environ.get("XVAR") != "a":
                S_.cp(kf, c.ps[b][:, 0:256], r=[f"ps{b}"], w=["kf"])
            if XS < 2:
                continue
            S_.act(sqb, c.ps[b][:, 0:256], AF.Square, r=[f"ps{b}"], w=["sqb"])
            b2 = nextps(c)
            S_.mm(c.ps[b2][:, 0:256], c.b["ones"], sqb, r=["cbf", "sqb"], w=[f"ps{b2}"])
            if XS < 3:
                continue
            rstd_from_ss(S_, c, c.ps[b2][:, 0:256], rsb, 128, [f"ps{b2}"], "rsb", tmp)
            if XS < 4:
                continue
            S_.stt(c.KnT[:, s, h, :], kf, V(c, "xk_g", l), rsb, ALU.mult, ALU.mult,
                   r=["kf", "rsb", "vec"], w=["KnT"])
        for mt in range(2):
            if XS < 5:
                break
            b = nextps(c)
            for kc in range(8):
                S_.mm(c.ps[b][:, :], c.memT[:, s, kc, mt * 128:(mt + 1) * 128], wkv[:, kc, 512:1024],
                      start=(kc == 0), stop=(kc == 7), r=["wkv", "memT"], w=[f"ps{b}"])
            S_.cp(c.Vm[:, s, mt, :], c.ps[b][:, :], r=[f"ps{b}"], w=["Vm"])
    S_.barrier()


def xattn_seq(S_, c, A, l, s, hT, win, cb_zx, cb_q, G):
    sc = 128.0 ** -0.5
    XB = [dict(kf=A.f32([512]), sqb=A.bf16([512]), rsb=A.f32([512]), tmp=A.f32([512]), qn=A.bf16([512]),
               sz=A.bf16([512]), pT=A.bf16([2, 512]), gst=A.bf16([2, 512]), wq=A.bf16([8, 128]), wz=A.bf16([8, 128]))
          for _ in range(2)]
    nrot_save = getattr(c, "nrot", 6)
    for h2 in range(0, 4, 2):
        caps = []
        for si in range(2):
            h = h2 + si
            X = XB[si]
            kf, sqb, rsb, tmp, qn, sz, pT, gst, wq, wz = (X[k_] for k_ in ("kf", "sqb", "rsb", "tmp", "qn", "sz", "pT", "gst", "wq", "wz"))
            P_ = f"x{si}_"
            S_.dma(wq, win[cb_q + h], r=[], w=[P_ + "wq"], q="pool")
            S_.dma(wz, win[cb_zx + h], r=[], w=[P_ + "wz"], q="pool")
            bo, bl = (6, 7) if si == 0 else (4, 5)
            bk = (2 * si, 2 * si + 1)
            S_.capture()
            for tb in range(4):
                tsl = slice(tb * 512, (tb + 1) * 512)
                bq = bk[0]
                for kc in range(8):
                    S_.mm(c.ps[bq][:, :], wq[:, kc, :], hT[:, kc, tsl], start=(kc == 0), stop=(kc == 7), r=[P_ + "wq", "hT"], w=[f"ps{bq}"])
                S_.cp(kf, c.ps[bq][:, :], r=[f"ps{bq}"], w=[P_ + "kf"])
                S_.act(sqb, c.ps[bq][:, :], AF.Square, r=[f"ps{bq}"], w=[P_ + "sqb"])
                b2 = bk[1]
                S_.mm(c.ps[b2][:, :], c.b["ones"], sqb, r=["cbf", P_ + "sqb"], w=[f"ps{b2}"])
                rstd_from_ss(S_, c, c.ps[b2][:, :], rsb, 128, [f"ps{b2}"], P_ + "rsb", tmp)
                S_.stt(qn, kf, V(c, "xq_g", l), rsb, ALU.mult, ALU.mult, r=[P_ + "kf", P_ + "rsb", "vec"], w=[P_ + "qn"])
                bz = bk[0]
                for kc in range(8):
                    S_.mm(c.ps[bz][:, :], wz[:, kc, :], hT[:, kc, tsl], start=(kc == 0), stop=(kc == 7), r=[P_ + "wz", "hT"], w=[f"ps{bz}"])
                S_.act(sz, c.ps[bz][:, :], AF.Silu, r=[f"ps{bz}"], w=[P_ + "sz"])
                for mt in range(2):
                    bs = bk[(mt + 1) % 2]
                    S_.mm(c.ps[bs][:, :], c.KnT[:, s, h, mt * 128:(mt + 1) * 128], qn, r=["KnT", P_ + "qn"], w=[f"ps{bs}"])
                    S_.act(pT[:, mt, :], c.ps[bs][:, :], AF.Exp, r=[f"ps{bs}"], w=[P_ + f"pT{mt}"], scale=sc)
                S_.mm(c.ps[bo][:, :], c.Vm[:, s, 0, h * 128:(h + 1) * 128], pT[:, 0, :], start=True, stop=False,
                      r=["Vm", P_ + "pT0"], w=[f"ps{bo}"])
                S_.mm(c.ps[bo][:, :], c.Vm[:, s, 1, h * 128:(h + 1) * 128], pT[:, 1, :], start=False, stop=True,
                      r=["Vm", P_ + "pT1"], w=[f"ps{bo}"])
                S_.mm(c.ps[bl][:, :], c.b["ones"], pT[:, 0, :], start=True, stop=False, r=["cbf", P_ + "pT0"], w=[f"ps{bl}"])
                S_.mm(c.ps[bl][:, :], c.b["ones"], pT[:, 1, :], start=False, stop=True, r=["cbf", P_ + "pT1"], w=[f"ps{bl}"])
                S_.add("dve", lambda e, rsb=rsb, bl=bl: e.reciprocal(out=rsb, in_=c.ps[bl][:, :]), r=[f"ps{bl}"], w=[P_ + "rsb"])
                S_.tt(kf, c.ps[bo][:, :], rsb, ALU.mult, r=[f"ps{bo}", P_ + "rsb"], w=[P_ + "kf"])
                gi = tb % 2
                S_.tt(gst[:, gi, :], kf, sz, ALU.mult, r=[P_ + "kf", P_ + "sz"], w=[P_ + f"gst{gi}"])
                S_.dma(G[s, (16 + h) * 128:(17 + h) * 128, tsl], gst[:, gi, :], r=[P_ + f"gst{gi}"], w=[("G", s, 16 + h)])
            caps.append(S_.end_capture())
        S_.replay_interleaved(caps)
    c.nrot = nrot_save


def phase_c(S_, c, l, s, xin, xout, G):
    A = Arena(c)
    wo = A.bf16([20, 1024])
    gt2 = A.bf16([2, 20, 512])
    xt2 = A.f32([2, 1024])
    xo2 = A.f32([2, 1024])
    for hf in range(2):
        S_.dma(wo[:, :, hf * 512:(hf + 1) * 512], c.d_wout[l][:, :, hf * 512:(hf + 1) * 512], r=[], w=["wo"], q="pool")
    for tb in range(4):
        gi = tb % 2
        gtb, gk = gt2[:, gi], f"gt{gi}"
        S_.dma(gtb, G[s, :, tb * 512:(tb + 1) * 512].rearrange("(k p) t -> p k t", p=128),
               r=[("G", s, ch) for ch in range(20)], w=[gk])
        for t4 in range(4):
            tt = tb * 4 + t4
            i = tt % 2
            xt, k = xt2[:, i, :], f"cxt{i}"
            S_.dma(xt, xin[s, tt * 128:(tt + 1) * 128, :], r=[("X", id(xin), s, tt)], w=[k])
            xo, ko = xo2[:, i, :], f"cxo{i}"
            for hf in range(2):
                b = nextps(c)
                for kc in range(20):
                    S_.mm(c.ps[b][:, :], gtb[:, kc, t4 * 128:(t4 + 1) * 128], wo[:, kc, hf * 512:(hf + 1) * 512],
                          start=(kc == 0), stop=(kc == 19), r=[gk, "wo"], w=[f"ps{b}"])
                S_.tt(xo[:, hf * 512:(hf + 1) * 512], c.ps[b][:, :], xt[:, hf * 512:(hf + 1) * 512], ALU.add,
                      r=[f"ps{b}", k], w=[ko])
            S_.dma(xout[s, tt * 128:(tt + 1) * 128, :], xo, r=[ko], w=[("X", id(xout), s, tt)])
    S_.barrier()

S5_TB = 16


def lambda_bar(S_, c, A, lamr, lami, lstep, n, pfx):
    step = A.f32([n]); xi = A.f32([n]); xr = A.f32([n]); mag = A.f32([n])
    t = A.f32([n]); k = A.f32([n]); y = A.f32([n])
    sn = A.f32([n]); csn = A.f32([n]); lbr = A.f32([n]); lbi = A.f32([n])
    K = lambda s: pfx + s
    S_.act(step, lstep, AF.Exp, r=[K("in")], w=[K("step")])
    S_.tt(xi, lami, step, ALU.mult, r=[K("in"), K("step")], w=[K("xi")])
    S_.tt(xr, lamr, step, ALU.mult, r=[K("in"), K("step")], w=[K("xr")])
    S_.act(mag, xr, AF.Exp, r=[K("xr")], w=[K("mag")])
    for which, shift, dst in (("s", 0.0, sn), ("c", 0.5 * math.pi, csn)):
        S_.ts(t, xi, shift, ALU.add, 1.0 / TWO_PI, ALU.mult, r=[K("xi")], w=[K("t")])
        S_.ts(k, t, MAGIC, ALU.add, -MAGIC, ALU.add, r=[K("t")], w=[K("k")])
        S_.stt(y, k, -TWO_PI, xi, ALU.mult, ALU.add, r=[K("k"), K("xi")], w=[K("y")])
        S_.ts(y, y, shift, ALU.add, -3.14159, ALU.max, r=[K("y")], w=[K("y")])
        S_.ts(y, y, 3.14159, ALU.min, r=[K("y")], w=[K("y")])
        S_.act(dst, y, AF.Sin, r=[K("y")], w=[K(which)])
    S_.tt(lbr, mag, csn, ALU.mult, r=[K("mag"), K("c")], w=[K("lbr")])
    S_.tt(lbi, mag, sn, ALU.mult, r=[K("mag"), K("s")], w=[K("lbi")])
    return lbr, lbi, K("lbr"), K("lbi")


def s5_layer(S_, c, l, xin, xout):
    NSQ = c.NSQ
    TB = S5_TB
    win = c.d_win[l]
    CB_U, CB_ZM, CB_ZX, CB_Q = 0, 16, 32, 36
    U, GS, G = c.d_U, c.d_GS, c.d_G
    for s in range(NSQ):
        A = Arena(c)
        hT = phase_a(S_, c, A, xin, s, l)
        wtiles(c, A, 3, 8)
        ust = A.bf16([2, SEQ])
        for cb in range(16):
            wt, wk = load_w(S_, c, win[CB_U + cb], 8)
            i = cb % 2
            for tb in range(4):
                b = proj_fm(S_, c, wt, wk, hT, tb)
                if tb % 2 == 0:
                    S_.act(ust[:, i, tb * 512:(tb + 1) * 512], c.ps[b][:, :], AF.Copy, r=[f"ps{b}"], w=[f"ust{i}"])
                else:
                    S_.cp(ust[:, i, tb * 512:(tb + 1) * 512], c.ps[b][:, :], r=[f"ps{b}"], w=[f"ust{i}"])
            S_.dma(U[s, cb * 128:(cb + 1) * 128, :], ust[:, i, :], r=[f"ust{i}"], w=[("U", s, cb)])
        S_.barrier()
    if c.stop < 4:
        return
    A = Arena(c)
    sA = A.f32([3, 64])
    S_.dma(sA, c.d_s5A[:, :, :], w=["A_in"])
    Cc = A.bf16([2, 64, 64])
    S_.dma(Cc, c.d_s5C[:, :, :, :], w=["Cc"], q="pool")
    Bpad = A.bf16([16, 2, 2, 128])
    lbrA, lbiA, kra, kia = lambda_bar(S_, c, A, sA[:, 0, :], sA[:, 1, :], sA[:, 2, :], 64, "A_")
    mark = A.off
    sB = A.f32([5, 1024])
    S_.dma(sB, c.d_s5B[:, :, :], w=["B_in"])
    lbrB, lbiB, krb, kib = lambda_bar(S_, c, A, sB[:, 0, :], sB[:, 1, :], sB[:, 2, :], 1024, "B_")
    n = 1024
    nr = A.f32([n]); den = A.f32([n]); t1 = A.f32([n]); t2 = A.f32([n]); cor = A.f32([n]); coi = A.f32([n])
    lamr, lami, bre, bim = sB[:, 0, :], sB[:, 1, :], sB[:, 3, :], sB[:, 4, :]
    S_.ts(nr, lbrB, -1.0, ALU.add, r=[krb], w=["nr"])
    S_.tt(den, lamr, lamr, ALU.mult, r=["B_in"], w=["den"])
    S_.tt(t1, lami, lami, ALU.mult, r=["B_in"], w=["t1"])
    S_.tt(den, den, t1, ALU.add, r=["den", "t1"], w=["den"])
    S_.add("dve", lambda e: e.reciprocal(out=den, in_=den), r=["den"], w=["den"])
    S_.tt(t1, nr, lamr, ALU.mult, r=["nr", "B_in"], w=["t1"])
    S_.tt(t2, lbiB, lami, ALU.mult, r=[kib, "B_in"], w=["t2"])
    S_.tt(t1, t1, t2, ALU.add, r=["t1", "t2"], w=["t1"])
    S_.tt(cor, t1, den, ALU.mult, r=["t1", "den"], w=["cor"])
    S_.tt(t1, lbiB, lamr, ALU.mult, r=[kib, "B_in"], w=["t1"])
    S_.tt(t2, nr, lami, ALU.mult, r=["nr", "B_in"], w=["t2"])
    S_.tt(t1, t1, t2, ALU.subtract, r=["t1", "t2"], w=["t1"])
    S_.tt(coi, t1, den, ALU.mult, r=["t1", "den"], w=["coi"])
    mj = cs(c, "maskq")
    for ri, (a0, a1, op) in enumerate(((bre, bim, ALU.subtract), (bim, bre, ALU.add))):
        S_.tt(t1, cor, a0, ALU.mult, r=["cor", "B_in"], w=["t1"])
        S_.tt(t2, coi, a1, ALU.mult, r=["coi", "B_in"], w=["t2"])
        S_.tt(t1, t1, t2, op, r=["t1", "t2"], w=["t1"])
        for e_ in range(2):
            for j in range(2):
                S_.ts(Bpad[:, :, ri, e_, j * 64:(j + 1) * 64], t1.rearrange("p (k q) -> p k q", k=16),
                      mj[:, 2 * e_ + j:2 * e_ + j + 1], ALU.mult, r=["t1", "cst"], w=["Bpad"])
    S_.barrier()
    if c.stop < 5:
        return
    A.off = mark
    NTK = NSQ * TB
    Dt2 = [A.f32([TB, 2, NSQ, 64]) for _ in range(2)]
    Hc = A.f32([2, NSQ, 64])
    Hb2 = [A.bf16([2, 64, NSQ, TB]) for _ in range(2)]
    UB = 64
    ublk = A.bf16([16, NSQ, UB])
    udk2 = [A.f32([16, NSQ, TB]) for _ in range(3)]
    gst = A.bf16([16, NSQ, UB])
    tm = [A.f32([NSQ, 64]) for _ in range(4)]
    lr = lbrA.unsqueeze(1).to_broadcast([128, NSQ, 64])
    li = lbiA.unsqueeze(1).to_broadcast([128, NSQ, 64])
    dsk = V(c, "s5_d").unsqueeze(2).unsqueeze(3).to_broadcast([128, 16, NSQ, TB])
    S_.memset(Hc, 0.0, w=["Hc"], eng="dve")
    BPU = UB // TB
    NBLK = (SEQ // TB) if c.stop > 5 else BPU
    NPB = 512 // NTK

    def stage1(blk):
        p = blk % 2
        p3 = blk % 3
        Dt, udk = Dt2[p], udk2[p3]
        ub, tq = blk // BPU, blk % BPU
        if tq == 0:
            for s in range(NSQ):
                S_.dma(ublk[:, :, s, :], U[s, :, ub * UB:(ub + 1) * UB].rearrange("(k p) t -> p k t", p=128),
                       r=[("U", s, cb) for cb in range(16)], w=["ublk"])
        tsl = slice(tq * TB, (tq + 1) * TB)
        S_.tt(udk, ublk[:, :, :, tsl], dsk, ALU.mult, r=["ublk", "vec"], w=[f"udk{p3}"], eng="pool")
        nch = min(16, NPB // 2)
        for ri in range(2):
            for hq in range(2):
                for g2 in range(16 // nch):
                    b = nextps(c)
                    for c2 in range(nch):
                        ch = nch * g2 + c2
                        for e_ in range(2):
                            col = (c2 * 2 + e_) * NTK
                            S_.mm(c.ps[b][:, col:col + NTK], Bpad[64 * hq:64 * hq + 64, ch, ri, e_, :],
                                  ublk[64 * hq:64 * hq + 64, ch, :, tsl], r=["Bpad", "ublk"], w=[f"ps{b}"])
                    for c2 in range(nch):
                        ch = nch * g2 + c2
                        p0 = 4 * ch + 2 * hq
                        S_.act(Dt[:, :, ri, :, p0:p0 + 2].rearrange("p t s q -> p q s t"),
                               c.ps[b][:, c2 * 2 * NTK:(c2 + 1) * 2 * NTK].rearrange("p (q s t) -> p q s t", q=2, s=NSQ),
                               AF.Copy, r=[f"ps{b}"], w=[f"D{p}"])

    def stage2(blk):
        p = blk % 2
        Dt, Hb = Dt2[p], Hb2[p]
        dk = f"D{p}"
        for t in range(TB):
            Pr = Hc[:, 0] if t == 0 else Dt[:, t - 1, 0]
            Pi = Hc[:, 1] if t == 0 else Dt[:, t - 1, 1]
            rk = [dk, "Hc", kra, kia]
            S_.tt(tm[0], Pr, lr, ALU.mult, r=rk, w=["tm"])
            S_.tt(tm[1], Pi, li, ALU.mult, r=rk, w=["tm"])
            S_.tt(tm[0], tm[0], tm[1], ALU.subtract, r=["tm"], w=["tm"])
            S_.tt(tm[2], Pi, lr, ALU.mult, r=rk, w=["tm"])
            S_.tt(tm[3], Pr, li, ALU.mult, r=rk, w=["tm"])
            S_.tt(tm[2], tm[2], tm[3], ALU.add, r=["tm"], w=["tm"])
            S_.tt(Dt[:, t, 0], Dt[:, t, 0], tm[0], ALU.add, r=[dk, "tm"], w=[dk])
            S_.tt(Dt[:, t, 1], Dt[:, t, 1], tm[2], ALU.add, r=[dk, "tm"], w=[dk])
        S_.cp(Hc, Dt[:, TB - 1], r=[dk], w=["Hc"])
        for ri in range(2):
            S_.act(Hb[:, ri], Dt[:, :, ri, :, :].rearrange("p t s q -> p q s t"), AF.Copy,
                   r=[dk], w=[f"Hb{p}"], scale=(1.0 if ri == 0 else -1.0))

    def stage3(blk):
        p = blk % 2
        p3 = blk % 3
        Hb, udk = Hb2[p], udk2[p3]
        ub, tq = blk // BPU, blk % BPU
        tsl = slice(tq * TB, (tq + 1) * TB)
        ncb = min(16, 512 // NTK)
        for cg in range(16 // ncb):
            b = nextps(c)
            for cc in range(ncb):
                ch = cg * ncb + cc
                for q in range(4):
                    pair = ch * 4 + q
                    hq, e_ = q // 2, q % 2
                    for ri in range(2):
                        S_.mm(c.ps[b][64 * hq:64 * hq + 64, cc * NTK:(cc + 1) * NTK], Cc[:, ri, pair, :],
                              Hb[:, ri, pair], start=(e_ == 0 and ri == 0), stop=(e_ == 1 and ri == 1),
                              r=["Cc", f"Hb{p}"], w=[f"ps{b}"])
            S_.tt(udk[:, cg * ncb:(cg + 1) * ncb], c.ps[b][:, 0:ncb * NTK].rearrange("p (k s t) -> p k s t", k=ncb, s=NSQ),
                  udk[:, cg * ncb:(cg + 1) * ncb], ALU.add, r=[f"ps{b}", f"udk{p3}"], w=[f"udk{p3}"])
        S_.act(gst[:, :, :, tsl], udk, AF.Gelu_apprx_tanh, r=[f"udk{p3}"], w=["gst"])
        if tq == BPU - 1:
            for s in range(NSQ):
                S_.dma(GS[s, :, ub * UB:(ub + 1) * UB].rearrange("(k p) t -> p k t", p=128), gst[:, :, s, :],
                       r=["gst"], w=[("GS", s, k_) for k_ in range(16)])

    stage1(0)
    for blk in range(NBLK):
        if blk + 1 < NBLK:
            stage1(blk + 1)
        stage2(blk)
        if blk >= 1:
            stage3(blk - 1)
    stage3(NBLK - 1)
    S_.barrier()
    if c.stop < 7:
        return
    for s in range(NSQ):
        A = Arena(c)
        hT = phase_a(S_, c, A, xin, s, l)
        S_.barrier()
        A.off = SEQ * 8 // 2
        wtiles(c, A, 3, 16)
        gT = A.bf16([16, SEQ])
        for hf in range(2):
            S_.dma(gT[:, hf * 8:(hf + 1) * 8, :], GS[s, hf * 1024:(hf + 1) * 1024, :].rearrange("(k p) t -> p k t", p=128),
                   r=[("GS", s, k_) for k_ in range(16)], w=["gT"])
        sg = A.f32([512]); sz = A.f32([512]); tg = A.f32([512]); gout = A.bf16([2, SEQ])
        for cb in range(16):
            wg, wgk = load_w(S_, c, c.d_wglu[cb], 16)
            wz, wzk = load_w(S_, c, win[CB_ZM + cb], 8)
            i = cb % 2
            for tb in range(4):
                bg = proj_fm(S_, c, wg, wgk, gT, tb, nk=16, hk="gT")
                S_.act(sg, c.ps[bg][:, :], AF.Sigmoid, r=[f"ps{bg}"], w=["sg"])
                bz = proj_fm(S_, c, wz, wzk, hT, tb)
                S_.act(sz, c.ps[bz][:, :], AF.Silu, r=[f"ps{bz}"], w=["sz"])
                S_.tt(tg, gT[:, cb, tb * 512:(tb + 1) * 512], sg, ALU.mult, r=["gT", "sg"], w=["tg"])
                S_.tt(gout[:, i, tb * 512:(tb + 1) * 512], tg, sz, ALU.mult, r=["tg", "sz"], w=[f"gout{i}"])
            S_.dma(G[s, cb * 128:(cb + 1) * 128, :], gout[:, i, :], r=[f"gout{i}"], w=[("G", s, cb)])
        xattn_seq(S_, c, A, l, s, hT, win, CB_ZX, CB_Q, G)
        S_.barrier()
        phase_c(S_, c, l, s, xin, xout, G)

def gla_layer(S_, c, l, xin, xout):
    win = c.d_win[l]
    CB_Q, CB_K, CB_V, CB_GK, CB_ZM, CB_ZX, CB_QM = 0, 4, 8, 24, 25, 41, 45
    G = c.d_G
    Uf, Lf = cs(c, "U"), cs(c, "Lst")
    for s in range(c.NSQ):
        A = Arena(c)
        hT = phase_a(S_, c, A, xin, s, l)
        S_.barrier()
        A.off = SEQ * 8 // 2
        wtiles(c, A, 3, 8)
        wv = A.bf16([8, 512])
        gkT = A.bf16([SEQ])
        wg2 = A.bf16([512])
        qT = A.f32([SEQ]); kT = A.f32([SEQ])
        vtok = A.bf16([16, 512])
        szT = A.bf16([4, SEQ])
        gout = A.bf16([4, SEQ])
        St = A.f32([512]); Sb = A.bf16([512])
        TP = [(A.f32([128]), A.f32([128]), A.f32([128]), A.f32([128]), A.f32([128]),
               A.bf16([128]), A.bf16([128]), A.bf16([128]), A.bf16([128])) for _ in range(2)]
        on = A.bf16([512])
        junk = A.f32([512])
        S_.memset(gkT[0:32, :], 1.0, w=["gkT"], eng="pool")
        S_.dma(wg2[0:17, :], c.d_wgk2[:, :], w=["wg2"], q="pool")
        wt, wk = load_w(S_, c, win[CB_GK], 8)
        for tb in range(4):
            b = proj_fm(S_, c, wt, wk, hT, tb, mcols=16)
            S_.cp(gkT[0:16, tb * 512:(tb + 1) * 512], c.ps[b][0:16, :], r=[f"ps{b}"], w=["gkT"])
        import os
        GS_ = int(os.environ.get("GSTOP", "99"))
        for h in range(4 if GS_ > 0 else 0):
            for nm, cb, dst in (("q", CB_Q + h, qT), ("k", CB_K + h, kT)):
                wt, wk = load_w(S_, c, win[cb], 8)
                for tb in range(4):
                    b = proj_fm(S_, c, wt, wk, hT, tb)
                    S_.act(dst[:, tb * 512:(tb + 1) * 512], c.ps[b][:, :], AF.Copy, r=[f"ps{b}"], w=[nm + "T"])
            for j in range(4):
                S_.dma(wv[:, :, j * 128:(j + 1) * 128], win[CB_V + 4 * h + j], w=["wv"], q="pool")
            for tt in range(NT):
                b = nextps(c)
                for kc in range(8):
                    S_.mm(c.ps[b][:, :], hT[:, kc, tt * 128:(tt + 1) * 128], wv[:, kc, :], start=(kc == 0), stop=(kc == 7),
                          r=["hT", "wv"], w=[f"ps{b}"])
                if tt % 2 == 0:
                    S_.cp(vtok[:, tt, :], c.ps[b][:, :], r=[f"ps{b}"], w=["vtok"])
                else:
                    S_.act(vtok[:, tt, :], c.ps[b][:, :], AF.Copy, r=[f"ps{b}"], w=["vtok"])
            for j in range(4):
                wt, wk = load_w(S_, c, win[CB_ZM + 4 * h + j], 8)
                for tb in range(4):
                    b = proj_fm(S_, c, wt, wk, hT, tb)
                    S_.act(szT[:, j, tb * 512:(tb + 1) * 512], c.ps[b][:, :], AF.Silu, r=[f"ps{b}"], w=["szT"])
            S_.memset(St, 0.0, w=["St"], eng="pool")
            S_.memset(Sb, 0.0, w=["Sb"], eng="pool")
            def front(ck):
                p = ck % 2
                e1, la, eb, enb, ebl, qt, kt, khat, attm = TP[p]
                P_ = str(p)
                bank = [3 * p + (i_ % 3) for i_ in range(5)]
                tsl = slice(ck * 128, (ck + 1) * 128)
                b = bank[0]
                S_.mm(c.ps[b][:, 0:128], gkT[0:17, tsl], wg2[0:17, h * 128:(h + 1) * 128], r=["gkT", "wg2"], w=[f"ps{b}"])
                S_.act(e1, c.ps[b][:, 0:128], AF.Exp, r=[f"ps{b}"], w=["e1" + P_], scale=-1.0)
                S_.act(e1, e1, AF.Ln, r=["e1" + P_], w=["e1" + P_], bias=1.0)
                S_.ts(la, e1, -1.0 / 16.0, ALU.mult, r=["e1" + P_], w=["la" + P_])
                b1 = bank[1]
                S_.mm(c.ps[b1][:, 0:128], la, Uf, r=["la" + P_, "cst"], w=[f"ps{b1}"])
                S_.act(eb, c.ps[b1][:, 0:128], AF.Exp, r=[f"ps{b1}"], w=["eb" + P_])
                S_.act(enb, c.ps[b1][:, 0:128], AF.Exp, r=[f"ps{b1}"], w=["enb" + P_], scale=-1.0)
                b2 = bank[2]
                S_.mm(c.ps[b2][:, 0:128], Lf, la, r=["la" + P_, "cst"], w=[f"ps{b2}"])
                S_.act(ebl, c.ps[b2][:, 0:128], AF.Exp, r=[f"ps{b2}"], w=["ebl" + P_])
                S_.stt(qt, qT[:, tsl], 128.0 ** -0.5, eb, ALU.mult, ALU.mult, r=["qT", "eb" + P_], w=["qt" + P_])
                S_.tt(kt, kT[:, tsl], enb, ALU.mult, r=["kT", "enb" + P_], w=["kt" + P_])
                b3 = bank[3]
                S_.tr(c.ps[b3][:, 0:128], kT[:, tsl], c.identf, r=["kT", "cst"], w=[f"ps{b3}"])
                S_.tt(khat, c.ps[b3][:, 0:128], ebl, ALU.mult, r=[f"ps{b3}", "ebl" + P_], w=["khat" + P_])
                b4 = bank[4]
                S_.mm(c.ps[b4][:, 0:128], kt, qt, r=["kt" + P_, "qt" + P_], w=[f"ps{b4}"])
                S_.tt(attm, c.ps[b4][:, 0:128], Uf, ALU.mult, r=[f"ps{b4}", "cst"], w=["attm" + P_])

            def tail(ck):
                p = ck % 2
                e1, la, eb, enb, ebl, qt, kt, khat, attm = TP[p]
                P_ = str(p)
                tsl = slice(ck * 128, (ck + 1) * 128)
                S_.mm(c.ps[6][:, :], attm, vtok[:, ck, :], start=True, stop=False, r=["attm" + P_, "vtok"], w=["ps6"])
                S_.mm(c.ps[6][:, :], qt, Sb, start=False, stop=True, r=["qt" + P_, "Sb"], w=["ps6"])
                S_.mm(c.ps[7][:, :], khat, vtok[:, ck, :], r=["khat" + P_, "vtok"], w=["ps7"])
                S_.stt(St, St, eb[:, 127:128], c.ps[7][:, :], ALU.mult, ALU.add, r=["St", "eb" + P_, "ps7"], w=["St"])
                S_.cp(Sb, St, r=["St"], w=["Sb"], eng="pool")
                S_.act(junk, c.ps[6][:, :], AF.Square, r=["ps6"], w=["junk", "g_ss"], accum_out=c.sm[:, 8:9])
                rstd_from_ss(S_, c, c.sm[:, 8:9], c.sm[:, 10:11], 512, ["g_ss"], "g_rs", c.sm[:, 9:10])
                S_.act(on, c.ps[6][:, :], AF.Copy, r=["ps6", "g_rs"], w=["on"], scale=c.sm[:, 10:11])
                pb = c.ps[7][:, :].bitcast(BF16)
                for j in range(4):
                    S_.tr(pb[:, j * 128:(j + 1) * 128], on[:, j * 128:(j + 1) * 128], c.b["ident"], r=["on", "cbf"], w=["ps7"])
                for j in range(4):
                    S_.stt(gout[:, j, tsl], pb[:, j * 128:(j + 1) * 128], V(c, "gla_norm_g", j), szT[:, j, tsl],
                           ALU.mult, ALU.mult, r=["ps7", "vec", "szT"], w=["gout"])

            for ck in range(0, NT, 2):
                caps = []
                for d_ in range(2):
                    S_.capture()
                    front(ck + d_)
                    caps.append(S_.end_capture())
                S_.replay_interleaved(caps)
                tail(ck)
                tail(ck + 1)
            for j in range(4):
                chn = h * 4 + j
                S_.dma(G[s, chn * 128:(chn + 1) * 128, :], gout[:, j, :], r=["gout"], w=[("G", s, chn)])
        xattn_seq(S_, c, A, l, s, hT, win, CB_ZX, CB_QM, G)
        S_.barrier()
        phase_c(S_, c, l, s, xin, xout, G)

def sincos_tables(S_, c, A, ang, n, sinT, cosT, pfx):
    t = A.f32([n]); k = A.f32([n]); y = A.f32([n])
    for which, shift, dst in (("s", 0.0, sinT), ("c", 0.5 * math.pi, cosT)):
        S_.ts(t, ang, shift, ALU.add, 1.0 / TWO_PI, ALU.mult, r=[pfx + "ang"], w=[pfx + "t"])
        S_.ts(k, t, MAGIC, ALU.add, -MAGIC, ALU.add, r=[pfx + "t"], w=[pfx + "k"])
        S_.stt(y, k, -TWO_PI, ang, ALU.mult, ALU.add, r=[pfx + "k", pfx + "ang"], w=[pfx + "y"])
        S_.ts(y, y, shift, ALU.add, -3.14159, ALU.max, r=[pfx + "y"], w=[pfx + "y"])
        S_.ts(y, y, 3.14159, ALU.min, r=[pfx + "y"], w=[pfx + "y"])
        S_.act(dst, y, AF.Sin, r=[pfx + "y"], w=[pfx + which])


def diff_layer(S_, c, l, xin, xout):
    win = c.d_win[l]
    CB_Q, CB_K, CB_V, CB_Z, CB_ZX, CB_QM = 0, 16, 32, 48, 64, 68
    G = c.d_G
    lam_init = 0.8 - 0.6 * math.exp(-0.3 * l)
    for s in range(c.NSQ):
        A = Arena(c)
        hT = phase_a(S_, c, A, xin, s, l)
        S_.barrier()
        A.off = SEQ * 8 // 2
        wtiles(c, A, 3, 8)
        c.nrot = 4
        sinT = A.f32([SEQ]); cosT = A.f32([SEQ])
        dmask = A.bf16([4 * 512])
        S_.dma(dmask, c.d_dmask[:, :], w=["dmask"], q="pool")
        mark = A.off
        posi = A.t[:, A.off:A.off + SEQ].bitcast(I32)
        A.off += SEQ
        ang = A.f32([SEQ])
        S_.dma(posi, c.d_pos[s:s + 1, :].to_broadcast([128, SEQ]), w=["posi"])
        S_.cp(ang, posi, r=["posi"], w=["r_ang"])
        S_.ts(ang, ang, cs(c, "invf"), ALU.mult, r=["r_ang", "cst"], w=["r_ang"])
        sincos_tables(S_, c, A, ang, SEQ, sinT, cosT, "r_")
        S_.barrier()
        A.off = mark
        lt = A.f32([64])
        sm = c.sm
        for i, (a_, b_) in enumerate((("diff_lq1", "diff_lk1"), ("diff_lq2", "diff_lk2"))):
            S_.tt(lt, V(c, a_), V(c, b_), ALU.mult, r=["vec"], w=["lt"])
            S_.add("dve", lambda e, i=i: e.reduce_sum(out=sm[:, 16 + i:17 + i], in_=lt, axis=AX.X), r=["lt"], w=[f"lsum{i}"])
            S_.act(sm[:, 18 + i:19 + i], sm[:, 16 + i:17 + i], AF.Exp, r=[f"lsum{i}"], w=[f"lexp{i}"])
        S_.tt(sm[:, 20:21], sm[:, 19:20], sm[:, 18:19], ALU.subtract, r=["lexp0", "lexp1"], w=["nl0"])
        S_.ts(sm[:, 21:22], sm[:, 20:21], -lam_init, ALU.add, r=["nl0"], w=["neglam"])
        S_.ts(sm[:, 22:23], V(c, "diff_subln_g"), 1.0 - lam_init, ALU.mult, r=["vec"], w=["gs"])
        neglam, gs = sm[:, 21:22], sm[:, 22:23]
        qr = A.bf16([SEQ]); kr = A.bf16([SEQ]); vtok = A.bf16([16, 128]); szT = A.bf16([SEQ]); gout = A.bf16([SEQ])
        RT = [(A.f32([512]), A.bf16([512]), A.f32([512]), A.f32([512]), A.bf16([512]), A.f32([512]), A.f32([512])) for _ in range(2)]
        sq = A.bf16([512]); rs = A.f32([512]); tmp = A.f32([512])
        pT = [[A.bf16([512]) for _ in range(2)] for _ in range(2)]
        rl = [A.f32([512]) for _ in range(2)]; on = [A.f32([512]) for _ in range(2)]; dd = A.f32([512]); g1 = A.f32([512])
        for hd in range(16):
            caps = []
            for si, (nm, cb, dst, gname) in enumerate((("q", CB_Q + hd, qr, "diff_q_g"), ("k", CB_K + hd, kr, "diff_k_g"))):
                wt, wk = load_w(S_, c, win[cb], 8)
                S_.capture()
                qf, sq_, rs_, tmp_, qn, t1, t2 = RT[si]
                P_ = str(si)
                bk = (2 * si, 2 * si + 1)
                for tb in range(4):
                    tsl = slice(tb * 512, (tb + 1) * 512)
                    b = bk[0]
                    for kc in range(8):
                        S_.mm(c.ps[b][:, :], wt[:, kc, :], hT[:, kc, tsl], start=(kc == 0), stop=(kc == 7), r=[wk, "hT"], w=[f"ps{b}"])
                    S_.cp(qf, c.ps[b][:, :], r=[f"ps{b}"], w=["qf" + P_])
                    S_.act(sq_, c.ps[b][:, :], AF.Square, r=[f"ps{b}"], w=["sq" + P_])
                    b2 = bk[1]
                    S_.mm(c.ps[b2][:, :], c.b["blk64"], sq_, r=["cbf", "sq" + P_], w=[f"ps{b2}"])
                    rstd_from_ss(S_, c, c.ps[b2][:, :], rs_, 64, [f"ps{b2}"], "rs" + P_, tmp_)
                    S_.stt(qn, qf, V(c, gname), rs_, ALU.mult, ALU.mult, r=["qf" + P_, "rs" + P_, "vec"], w=["qn" + P_])
                    b3 = bk[0]
                    S_.mm(c.ps[b3][:, :], c.b["rotT"], qn, r=["cbf", "qn" + P_], w=[f"ps{b3}"])
                    S_.tt(t1, qn, cosT[:, tsl], ALU.mult, r=["qn" + P_, "r_c"], w=["t1" + P_], eng="pool")
                    S_.tt(t2, c.ps[b3][:, :], sinT[:, tsl], ALU.mult, r=[f"ps{b3}", "r_s"], w=["t2" + P_])
                    S_.tt(dst[:, tsl], t1, t2, ALU.add, r=["t1" + P_, "t2" + P_], w=[nm + "r"])
                caps.append(S_.end_capture())
            S_.replay_interleaved(caps)
            wt, wk = load_w(S_, c, win[CB_V + hd], 8)
            for tg in range(4):
                b = nextps(c)
                for j in range(4):
                    tt = tg * 4 + j
                    for kc in range(8):
                        S_.mm(c.ps[b][:, j * 128:(j + 1) * 128], hT[:, kc, tt * 128:(tt + 1) * 128], wt[:, kc, :],
                              start=(kc == 0), stop=(kc == 7), r=["hT", wk], w=[f"ps{b}"])
                S_.cp(vtok[:, tg * 4:tg * 4 + 4, :], c.ps[b][:, :].rearrange("p (j v) -> p j v", j=4), r=[f"ps{b}"], w=["vtok"])
            wt, wk = load_w(S_, c, win[CB_Z + hd], 8)
            for tb in range(4):
                b = proj_fm(S_, c, wt, wk, hT, tb)
                S_.act(szT[:, tb * 512:(tb + 1) * 512], c.ps[b][:, :], AF.Silu, r=[f"ps{b}"], w=["szT"])
            for qb in range(4):
                qsl = slice(qb * 512, (qb + 1) * 512)
                caps = []
                for cp_ in range(2):
                    S_.capture()
                    r0 = 64 * cp_
                    nkt = 4 * qb + 4
                    bo, bl = (6, 7) if cp_ == 0 else (4, 5)

                    def st_mm(kt, cp_=cp_):
                        b_ = 2 * cp_ + (kt % 2)
                        S_.mm(c.ps[b_][:, :], kr[r0:r0 + 64, kt * 128:(kt + 1) * 128], qr[r0:r0 + 64, qsl],
                              r=["kr", "qr"], w=[f"ps{b_}"])
                        return b_
                    bnext = st_mm(0)
                    for kt in range(nkt):
                        b_ = bnext
                        if kt + 1 < nkt:
                            bnext = st_mm(kt + 1)
                        p_ = pT[cp_][kt % 2]; pk = f"pT{cp_}{kt % 2}"
                        S_.act(p_, c.ps[b_][:, :], AF.Exp, r=[f"ps{b_}"], w=[pk], scale=0.125)
                        if kt >= 4 * qb:
                            j = kt - 4 * qb
                            S_.tt(p_, p_, dmask[:, j * 512:(j + 1) * 512], ALU.mult, r=[pk, "dmask"], w=[pk], eng="pool")
                        S_.mm(c.ps[bo][:, :], vtok[:, kt, :], p_, start=(kt == 0), stop=(kt == nkt - 1), r=["vtok", pk], w=[f"ps{bo}"])
                        S_.mm(c.ps[bl][:, :], c.b["ones"], p_, start=(kt == 0), stop=(kt == nkt - 1), r=["cbf", pk], w=[f"ps{bl}"])
                    S_.add("dve", lambda e, cp_=cp_, bl=bl: e.reciprocal(out=rl[cp_], in_=c.ps[bl][:, :]), r=[f"ps{bl}"], w=[f"rl{cp_}"])
                    S_.tt(on[cp_], c.ps[bo][:, :], rl[cp_], ALU.mult, r=[f"ps{bo}", f"rl{cp_}"], w=[f"on{cp_}"])
                    caps.append(S_.end_capture())
                S_.replay_interleaved(caps)
                S_.stt(dd, on[1], neglam, on[0], ALU.mult, ALU.add, r=["on0", "on1", "neglam"], w=["dd"])
                S_.act(sq, dd, AF.Square, r=["dd"], w=["sq"])
                b2 = nextps(c)
                S_.mm(c.ps[b2][:, :], c.b["ones"], sq, r=["cbf", "sq"], w=[f"ps{b2}"])
                rstd_from_ss(S_, c, c.ps[b2][:, :], rs, 128, [f"ps{b2}"], "rs", tmp)
                S_.stt(g1, dd, gs, rs, ALU.mult, ALU.mult, r=["dd", "gs", "rs"], w=["g1"])
                S_.tt(gout[:, qsl], g1, szT[:, qsl], ALU.mult, r=["g1", "szT"], w=["gout"])
            S_.dma(G[s, hd * 128:(hd + 1) * 128, :], gout, r=["gout"], w=[("G", s, hd)])
        c.nrot = 6
        xattn_seq(S_, c, A, l, s, hT, win, CB_ZX, CB_QM, G)
        S_.barrier()
        phase_c(S_, c, l, s, xin, xout, G)

def ssd_layer(S_, c, l, xin, xout):
    win = c.d_win[l]
    CB_XS, CB_B, CB_C, CB_DT, CB_ZM, CB_ZX, CB_QM = 0, 16, 24, 32, 33, 49, 53
    G = c.d_G
    Uf, Of = cs(c, "U"), cs(c, "ones")
    for s in range(c.NSQ):
        A = Arena(c)
        hT = phase_a(S_, c, A, xin, s, l)
        S_.barrier()
        A.off = SEQ * 8 // 2
        wtiles(c, A, 3, 8)
        mark0 = A.off
        yz = A.bf16([16, SEQ])
        dt = A.f32([16, 32]); dta = A.f32([16, 32]); cumT = A.f32([16, 32]); lastT = A.f32([16, 32])
        elast = A.f32([16, 32]); ws = A.f32([16, 32]); aneg = A.f32([32])
        wt, wk = load_w(S_, c, win[CB_DT], 8)
        for ck in range(NT):
            for kc in range(8):
                S_.mm(c.ps[6][:, ck * 32:(ck + 1) * 32], hT[:, kc, ck * 128:(ck + 1) * 128], wt[:, kc, 0:32],
                      start=(kc == 0), stop=(kc == 7), r=["hT", wk], w=["ps6"])
        bias = V(c, "ssd_dt_bias").unsqueeze(1).to_broadcast([128, 16, 32])
        S_.tt(dt, c.ps[6][:, :].rearrange("p (k h) -> p k h", k=16), bias, ALU.add, r=["ps6", "vec"], w=["dt"])
        S_.act(dt, dt, AF.Exp, r=["dt"], w=["dt"])
        S_.act(dt, dt, AF.Ln, r=["dt"], w=["dt"], bias=1.0)
        S_.act(aneg, V(c, "ssd_a_log"), AF.Exp, r=["vec"], w=["aneg"])
        S_.ts(aneg, aneg, -1.0, ALU.mult, r=["aneg"], w=["aneg"])
        S_.tt(dta, dt, aneg.unsqueeze(1).to_broadcast([128, 16, 32]), ALU.mult, r=["dt", "aneg"], w=["dta"])
        for ck in range(NT):
            S_.mm(c.ps[6][:, ck * 32:(ck + 1) * 32], Uf, dta[:, ck, :], r=["cst", "dta"], w=["ps6"])
            S_.mm(c.ps[7][:, ck * 32:(ck + 1) * 32], Of, dta[:, ck, :], r=["cst", "dta"], w=["ps7"])
        S_.cp(cumT, c.ps[6][:, :].rearrange("p (k h) -> p k h", k=16), r=["ps6"], w=["cumT"])
        S_.cp(lastT, c.ps[7][:, :].rearrange("p (k h) -> p k h", k=16), r=["ps7"], w=["lastT"])
        S_.act(elast, lastT, AF.Exp, r=["lastT"], w=["elast"])
        S_.tt(ws, lastT, cumT, ALU.subtract, r=["lastT", "cumT"], w=["ws"])
        S_.act(ws, ws, AF.Exp, r=["ws"], w=["ws"])
        S_.tt(ws, ws, dt, ALU.mult, r=["ws", "dt"], w=["ws"])
        xsT = A.bf16([2, SEQ]); BT = A.bf16([SEQ]); CT = A.bf16([SEQ]); szT = A.bf16([2, SEQ])
        St = A.f32([4, 64]); Sb = A.bf16([4, 64])
        mk_ = A.off
        raw = A.f32([SEQ + 4]); acc = A.f32([SEQ])
        end1 = A.off
        A.off = mk_
        TB_ = [dict(xtok=A.bf16([256]), btok=A.bf16([128]), CBm=A.f32([128]), CBd=A.f32([4, 128]), Z=A.f32([4, 128]),
                    dm=A.f32([4, 128]), mm=A.bf16([4, 128]), ecr=A.f32([4, 128]), Cs=A.bf16([4, 128]), xw=A.bf16([4, 64]),
                    yf=A.f32([2, 128])) for _ in range(2)]
        A.off = max(A.off, end1)
        Ub = Uf.unsqueeze(1).to_broadcast([128, 4, 128])
        for g in range(8):
            S_.barrier()
            S_.memset(raw[:, 0:4], 0.0, w=["raw"], eng="pool")
            srcs = [(CB_XS + 2 * g, xsT[:, 0, :], 2 * g), (CB_XS + 2 * g + 1, xsT[:, 1, :], 2 * g + 1),
                    (CB_B + g, BT, 16 + g), (CB_C + g, CT, 24 + g)]
            for cb, dst, cch in srcs:
                wt, wk = load_w(S_, c, win[cb], 8)
                for tb in range(4):
                    b = proj_fm(S_, c, wt, wk, hT, tb)
                    S_.act(raw[:, 3 + tb * 512:3 + (tb + 1) * 512], c.ps[b][:, :], AF.Copy, r=[f"ps{b}"], w=["raw"])
                cw = lambda j: V(c, "ssd_conv_w", cch * 4 + j)
                S_.ts(acc, raw[:, 3:3 + SEQ], cw(3), ALU.mult, V(c, "ssd_conv_b", cch), ALU.add, r=["raw", "vec"], w=["acc"], eng="pool")
                for j in range(3):
                    S_.stt(acc, raw[:, j:j + SEQ], cw(j), acc, ALU.mult, ALU.add, r=["raw", "vec", "acc"], w=["acc"])
                S_.act(dst, acc, AF.Silu, r=["acc"], w=["xbc"])
            for i in range(2):
                wt, wk = load_w(S_, c, win[CB_ZM + 2 * g + i], 8)
                for tb in range(4):
                    b = proj_fm(S_, c, wt, wk, hT, tb)
                    S_.act(szT[:, i, tb * 512:(tb + 1) * 512], c.ps[b][:, :], AF.Silu, r=[f"ps{b}"], w=["szT"])
            S_.barrier()
            S_.memset(St, 0.0, w=["St"], eng="pool")
            S_.memset(Sb, 0.0, w=["Sb"], eng="pool")
            h0 = 4 * g
            def front(ck):
                    tsl = slice(ck * 128, (ck + 1) * 128)
                    T_ = TB_[ck % 2]
                    xtok, btok, CBm, CBd, Z, dm, mm_, ecr, Cs, xw, yf = (T_[k_] for k_ in
                                                                         ("xtok", "btok", "CBm", "CBd", "Z", "dm", "mm", "ecr", "Cs", "xw", "yf"))
                    P_ = str(ck % 2)
                    b = nextps(c)
                    pb = c.ps[b][:, :].bitcast(BF16)
                    for i in range(2):
                        S_.tr(pb[:, i * 128:(i + 1) * 128], xsT[:, i, tsl], c.b["ident"], r=["xbc", "cbf"], w=[f"ps{b}"])
                    S_.tr(pb[:, 256:384], BT[:, tsl], c.b["ident"], r=["xbc", "cbf"], w=[f"ps{b}"])
                    S_.act(xtok, pb[:, 0:256], AF.Copy, r=[f"ps{b}"], w=["xtok" + P_])
                    S_.act(btok, pb[:, 256:384], AF.Copy, r=[f"ps{b}"], w=["btok" + P_])
                    b1 = nextps(c)
                    S_.mm(c.ps[b1][:, 0:128], BT[:, tsl], CT[:, tsl], r=["xbc"], w=[f"ps{b1}"])
                    S_.tt(CBm, c.ps[b1][:, 0:128], Uf, ALU.mult, r=[f"ps{b1}", "cst"], w=["CBm" + P_])
                    dtb = dt[:, ck, h0:h0 + 4].unsqueeze(2).to_broadcast([128, 4, 128])
                    S_.tt(CBd, CBm.unsqueeze(1).to_broadcast([128, 4, 128]), dtb, ALU.mult, r=["CBm" + P_, "dt"], w=["CBd" + P_], eng="pool")
                    S_.tt(Z, Ub, dta[:, ck, h0:h0 + 4].unsqueeze(2).to_broadcast([128, 4, 128]), ALU.mult,
                          r=["cst", "dta"], w=["Z" + P_], eng="pool")
                    b2 = nextps(c)
                    S_.mm(c.ps[b2][:, :], Of, Z.rearrange("p r t -> p (r t)"), r=["cst", "Z" + P_], w=[f"ps{b2}"])
                    psv = c.ps[b2][:, :].rearrange("p (r t) -> p r t", r=4)
                    S_.act(ecr, psv, AF.Exp, r=[f"ps{b2}"], w=["ecr" + P_])
                    S_.tt(dm, psv, cumT[:, ck, h0:h0 + 4].unsqueeze(2).to_broadcast([128, 4, 128]), ALU.subtract,
                          r=[f"ps{b2}", "cumT"], w=["dm" + P_])
                    S_.ts(dm, dm, 0.0, ALU.min, r=["dm" + P_], w=["dm" + P_])
                    S_.act(dm, dm, AF.Exp, r=["dm" + P_], w=["dm" + P_])
                    S_.tt(mm_, dm, CBd, ALU.mult, r=["dm" + P_, "CBd" + P_], w=["mm" + P_])
                    S_.tt(Cs, CT[:, tsl].unsqueeze(1).to_broadcast([128, 4, 128]), ecr, ALU.mult, r=["xbc", "ecr" + P_], w=["Cs" + P_], eng="pool")
                    S_.tt(xw, xtok.rearrange("p (r q) -> p r q", r=4), ws[:, ck, h0:h0 + 4].unsqueeze(2).to_broadcast([128, 4, 64]),
                          ALU.mult, r=["xtok" + P_, "ws"], w=["xw" + P_], eng="pool")
            def tail(ck):
                    tsl = slice(ck * 128, (ck + 1) * 128)
                    T_ = TB_[ck % 2]
                    xtok, btok, CBm, CBd, Z, dm, mm_, ecr, Cs, xw, yf = (T_[k_] for k_ in
                                                                         ("xtok", "btok", "CBm", "CBd", "Z", "dm", "mm", "ecr", "Cs", "xw", "yf"))
                    P_ = str(ck % 2)
                    by = 7
                    b3 = 6
                    for r_ in range(4):
                        i, po = r_ // 2, 64 * (r_ % 2)
                        S_.mm(c.ps[by][po:po + 64, i * 128:(i + 1) * 128], xtok[:, r_ * 64:(r_ + 1) * 64], mm_[:, r_, :], start=True, stop=False,
                              r=["xtok" + P_, "mm" + P_], w=[f"ps{by}"])
                        S_.mm(c.ps[by][po:po + 64, i * 128:(i + 1) * 128], Sb[:, r_, :], Cs[:, r_, :], start=False, stop=True,
                              r=["Sb", "Cs" + P_], w=[f"ps{by}"])
                    for r_ in range(4):
                        S_.mm(c.ps[b3][:, r_ * 64:(r_ + 1) * 64], btok, xw[:, r_, :], r=["btok" + P_, "xw" + P_], w=[f"ps{b3}"])
                    S_.tt(St, St, elast[:, ck, h0:h0 + 4].unsqueeze(2).to_broadcast([128, 4, 64]), ALU.mult, r=["St", "elast"], w=["St"])
                    S_.tt(St, St, c.ps[b3][:, 0:256].rearrange("p (r q) -> p r q", r=4), ALU.add, r=["St", f"ps{b3}"], w=["St"])
                    S_.act(Sb, St, AF.Copy, r=["St"], w=["Sb"])
                    for i in range(2):
                        S_.stt(yf[:, i, :], xsT[:, i, tsl], V(c, "ssd_d", 2 * g + i), c.ps[by][:, i * 128:(i + 1) * 128], ALU.mult, ALU.add,
                               r=["xbc", "vec", f"ps{by}"], w=["yf" + P_])
                        S_.tt(yz[:, 2 * g + i, tsl], yf[:, i, :], szT[:, i, tsl], ALU.mult, r=["yf" + P_, "szT"], w=["yz"])
            for ck in range(0, NT, 2):
                caps = []
                for d_ in range(2):
                    S_.capture()
                    front(ck + d_)
                    caps.append(S_.end_capture())
                S_.replay_interleaved(caps)
                tail(ck)
                tail(ck + 1)
        S_.barrier()
        A.off = mark0 + 16 * SEQ // 2
        sq = A.bf16([512]); rs = A.f32([512]); tmp = A.f32([512]); gst = A.bf16([2, 512])
        for tb in range(4):
            tsl = slice(tb * 512, (tb + 1) * 512)
            for cc in range(16):
                S_.act(sq, yz[:, cc, tsl], AF.Square, r=["yz"], w=["sq"])
                S_.mm(c.ps[6][:, :], c.b["ones"], sq, start=(cc == 0), stop=(cc == 15), r=["cbf", "sq"], w=["ps6"])
            rstd_from_ss(S_, c, c.ps[6][:, :], rs, 2048, ["ps6"], "rs", tmp)
            for cc in range(16):
                i = cc % 2
                S_.stt(gst[:, i, :], yz[:, cc, tsl], V(c, "ssd_norm_g", cc), rs, ALU.mult, ALU.mult, r=["yz", "vec", "rs"], w=[f"gst{i}"])
                S_.dma(G[s, cc * 128:(cc + 1) * 128, tsl], gst[:, i, :], r=[f"gst{i}"], w=[("G", s, cc)])
        xattn_seq(S_, c, A, l, s, hT, win, CB_ZX, CB_QM, G)
        S_.barrier()
        phase_c(S_, c, l, s, xin, xout, G)

LAYER_FN = {}


def build_nc(layers, NSQ, meta):
    nc = bass.Bass("TRN2", target_bir_lowering=False)
    c = Ctx()
    c.NSQ = NSQ
    c.cmap, c.ncst = meta["cmap"], meta["ncst"]
    c.vmap, c.nvec = meta["vmap"], meta["nvec"]

    def din(name, shape, dt=F32):
        return nc.dram_tensor(name, list(shape), dt, kind="ExternalInput").ap()

    def dscr(name, shape, dt):
        return nc.dram_tensor(name, list(shape), dt, kind="Internal").ap()

    x = din("x", [NSQ, SEQ, D])
    c.d_mem = din("mem", [NSQ, 256, D])
    c.d_pos = din("pos", [NSQ, SEQ], I32)
    c.d_cst = din("cst", [128, c.ncst])
    c.d_vec = din("vec", [128, c.nvec])
    c.d_win, c.d_wout, c.d_wkv = {}, {}, {}
    for l in layers:
        c.d_win[l] = din(f"win{l}", [meta["ncb"][l], 128, 8, 128])
        c.d_wout[l] = din(f"wout{l}", [128, 20, 1024])
        c.d_wkv[l] = din(f"wkv{l}", [128, 8, 1024])
    for nm, shp in meta["extra"].items():
        if int(nm[1]) in layers or nm[0] != "L":
            setattr(c, "d_" + nm[3:], din(nm, shp))
    y = nc.dram_tensor("y", [NSQ, SEQ, D], F32, kind="ExternalOutput").ap()
    c.d_G = dscr("G", [NSQ, 2560, SEQ], BF16)
    if 0 in layers:
        c.d_U = dscr("U", [NSQ, 2048, SEQ], BF16)
        c.d_GS = dscr("GS", [NSQ, 2048, SEQ], BF16)
    xs = [x]
    for i in range(len(layers) - 1):
        xs.append(dscr(f"xs{i}", [NSQ, SEQ, D], F32))
    xs.append(y)
    with ExitStack() as es:
        S_ = Sched(nc, es)
        setup_common(S_, c, nc)
        c.stop = meta.get("stop", 99)
        mem_prologue(S_, c)
        for i, l in enumerate(layers):
            if c.stop < 2:
                break
            xattn_layer_prologue(S_, c, l)
            if c.stop < 3:
                break
            LAYER_FN[l](S_, c, l, xs[i], xs[i + 1])
        print("ops recorded:", len(S_.ops), "arena words:", c.arena_n, flush=True)
        S_.emit()
    return nc


def _blk(W, c0, ncols):
    nb = (ncols + 127) // 128
    K = W.shape[0] // 128
    Wp = np.zeros((W.shape[0], nb * 128), np.float32)
    Wp[:, :ncols] = W[:, c0:c0 + ncols]
    return np.ascontiguousarray(Wp.reshape(K, 128, nb, 128).transpose(2, 1, 0, 3))


def host_prep(inp):
    f = lambda a: np.asarray(a, np.float32)
    meta = {"ncb": {}, "extra": {}}
    shared = {}
    cols = []
    cmap = {}

    def addc(name, arr):
        a = sum(x.shape[1] for x in cols)
        cols.append(arr.astype(np.float32))
        cmap[name] = (a, a + arr.shape[1])

    i128 = np.arange(128)
    addc("ident", np.eye(128))
    addc("U", (i128[:, None] <= i128[None, :]).astype(np.float32))
    addc("ones", np.ones((128, 128)))
    blk = np.zeros((128, 128)); blk[:64, :64] = 1; blk[64:, 64:] = 1
    addc("blk64", blk)
    rot = np.zeros((128, 128))
    for cpt in range(2):
        for d in range(64):
            if d < 32:
                rot[cpt * 64 + d + 32, cpt * 64 + d] = -1.0
            else:
                rot[cpt * 64 + d - 32, cpt * 64 + d] = 1.0
    addc("rotT", rot)
    addc("Lst", (i128[:, None] > i128[None, :]).astype(np.float32))
    addc("eps", np.full((128, 1), EPS))
    addc("maskq", np.stack([((i128 // 16) % 4 == j) for j in range(4)], 1).astype(np.float32))
    inv = 10000.0 ** (-np.arange(0, 64, 2, dtype=np.float32) / 64)
    addc("invf", inv[(i128 % 64) % 32][:, None])
    shared["cst"] = np.ascontiguousarray(np.concatenate(cols, 1))
    meta["cmap"], meta["ncst"] = cmap, shared["cst"].shape[1]
    vcols = []
    vmap = {}

    def addv(name, arr):
        a = sum(x.shape[1] for x in vcols)
        vcols.append(np.asarray(arr, np.float32))
        vmap[name] = (a, a + arr.shape[1])

    addv("norm_g", f(inp["norm_g"]).reshape(4, 8, 128).transpose(2, 0, 1).reshape(128, 32))
    addv("mem_norm_g", f(inp["mem_norm_g"]).reshape(8, 128).T)
    addv("xq_g", f(inp["xq_g"]).T)
    addv("xk_g", f(inp["xk_g"]).T)
    addv("s5_d", f(inp["s5_d"])[0].reshape(16, 128).T)
    addv("gla_norm_g", f(inp["gla_norm_g"])[0].reshape(4, 128).T)
    addv("diff_q_g", np.tile(f(inp["diff_q_g"])[0], 2)[:, None])
    addv("diff_k_g", np.tile(f(inp["diff_k_g"])[0], 2)[:, None])
    addv("diff_subln_g", f(inp["diff_subln_g"])[0][:, None])
    for nm in ("diff_lq1", "diff_lk1", "diff_lq2", "diff_lk2"):
        addv(nm, np.tile(f(inp[nm])[0][None, :], (128, 1)))
    addv("ssd_conv_w", f(inp["ssd_conv_w"])[0].reshape(4, 32, 128).transpose(2, 1, 0).reshape(128, 128))
    addv("ssd_conv_b", f(inp["ssd_conv_b"])[0].reshape(32, 128).T)
    addv("ssd_d", np.repeat(f(inp["ssd_d"])[0], 64).reshape(16, 128).T)
    addv("ssd_norm_g", f(inp["ssd_norm_g"])[0].reshape(16, 128).T)
    addv("ssd_dt_bias", np.tile(f(inp["ssd_dt_bias"])[0][None, :], (128, 1)))
    addv("ssd_a_log", np.tile(f(inp["ssd_a_log"])[0][None, :], (128, 1)))
    shared["vec"] = np.ascontiguousarray(np.concatenate(vcols, 1))
    meta["vmap"], meta["nvec"] = vmap, shared["vec"].shape[1]
    wins = {0: f(inp["s5_w_in"])[0], 1: f(inp["gla_w_in"])[0], 2: f(inp["diff_w_in"])[0], 3: f(inp["ssd_w_in"])[0]}
    shared["win0"] = _blk(wins[0], 0, 5120)
    W = wins[1]
    shared["win1"] = np.concatenate([_blk(W, 0, 3072), _blk(W, 3072, 16), _blk(W, 3088, 3072)], 0)
    shared["win2"] = _blk(wins[2], 0, 9216)
    W = wins[3]
    shared["win3"] = np.concatenate([_blk(W, 0, 4096), _blk(W, 4096, 32), _blk(W, 4128, 3072)], 0)
    for l in range(4):
        meta["ncb"][l] = shared[f"win{l}"].shape[0]
        shared[f"wout{l}"] = np.ascontiguousarray(f(inp["w_out"])[l].reshape(20, 128, 1024).transpose(1, 0, 2))
        shared[f"wkv{l}"] = np.ascontiguousarray(f(inp["w_mem_kv"])[l].reshape(8, 128, 1024).transpose(1, 0, 2))
    ex = {}
    ex["L0_wglu"] = np.ascontiguousarray(f(inp["s5_w_glu"])[0].reshape(16, 128, 16, 128).transpose(2, 1, 0, 3))
    lam_re, lam_im, ls = f(inp["s5_lam_re"])[0], f(inp["s5_lam_im"])[0], f(inp["s5_log_step"])[0]
    toA = lambda a: a.reshape(64, 2, 64).transpose(1, 2, 0).reshape(128, 64)
    lsf = np.repeat(ls[:, None], 64, 1)
    ex["L0_s5A"] = np.ascontiguousarray(np.stack([toA(lam_re), toA(lam_im), toA(lsf)], 1))
    toBl = lambda a: np.repeat(a.reshape(16, 8, 1, 64), 16, 2).transpose(1, 2, 0, 3).reshape(128, 1024)
    toBb = lambda b: b.reshape(16, 8, 64, 16).transpose(1, 3, 0, 2).reshape(128, 1024)
    ex["L0_s5B"] = np.ascontiguousarray(np.stack([toBl(lam_re), toBl(lam_im), toBl(lsf),
                                                    toBb(f(inp["s5_b_re"])[0]), toBb(f(inp["s5_b_im"])[0])], 1))
    Cc = np.zeros((2, 64, 2, 32, 2, 2, 2, 16), np.float32)
    for ri, nm in enumerate(("s5_c_re", "s5_c_im")):
        Cm = f(inp[nm])[0].reshape(32, 2, 2, 16, 64)
        for j in range(2):
            for e_ in range(2):
                Cc[j, :, ri, :, e_, e_, j, :] = Cm[:, e_, j].transpose(2, 0, 1)
    ex["L0_s5C"] = np.ascontiguousarray(Cc.reshape(128, 2, 64, 64))
    wg2 = np.zeros((17, 512), np.float32)
    wg2[:16] = f(inp["gla_w_gk2"])[0]
    wg2[16] = f(inp["gla_b_gk2"])[0]
    ex["L1_wgk2"] = wg2
    qi = np.arange(512)[None, :]
    ex["L2_dmask"] = np.concatenate([((128 * j + np.arange(128)[:, None]) <= qi).astype(np.float32) for j in range(4)], 1)
    for k_, v_ in ex.items():
        meta["extra"][k_] = list(v_.shape)
        shared[k_] = v_
    return shared, meta


def run_layers(inp, layers, NSQ, ncores, xin=None, stop=99, trace=False):
    shared, meta = host_prep(inp)
    meta["stop"] = stop
    nc = build_nc(layers, NSQ, meta)
    x = np.asarray(inp["x"], np.float32) if xin is None else xin
    mem = np.asarray(inp["mem"], np.float32)
    pos = np.asarray(inp["positions"], np.int32)
    names = ["cst", "vec"] + [f"{p}{l}" for l in layers for p in ("win", "wout", "wkv")]
    names += [k_ for k_ in meta["extra"] if int(k_[1]) in layers]
    in_maps = []
    for ci in range(ncores):
        sl = slice(ci * NSQ, (ci + 1) * NSQ)
        m = {"x": np.ascontiguousarray(x[sl]), "mem": np.ascontiguousarray(mem[sl]), "pos": np.ascontiguousarray(pos[sl])}
        for nm in names:
            m[nm] = shared[nm]
        in_maps.append(m)
    res = run_bass_kernel_spmd(nc, in_maps, core_ids=list(range(ncores)), **({"trace": True} if trace else {}))
    if trace:
        print("EXEC_NS", layers, res.exec_time_ns, flush=True)
    return np.concatenate([r["y"] for r in res.results], 0)


def kernel(**inputs):
    return run_layers(inputs, [0, 1, 2, 3], 4, 8)

LAYER_FN[0] = s5_layer
LAYER_FN[1] = gla_layer
LAYER_FN[2] = diff_layer
LAYER_FN[3] = ssd_layer
```

```python
import numpy as np, math
from contextlib import ExitStack
import concourse.bass as bass
import concourse.mybir as mybir
from concourse.bass_utils import run_bass_kernel_spmd

F32 = mybir.dt.float32
BF16 = mybir.dt.bfloat16
I32 = mybir.dt.int32
AF = mybir.ActivationFunctionType
ALU = mybir.AluOpType
AX = mybir.AxisListType


class Sched:
    STREAMS = ("pe", "act", "dve", "pool", "sp")
    NSLOT = 8
    MAXV = 30000

    def __init__(self, nc, es):
        self.nc = nc
        self.es = es
        self.ops = []
        self.lastw = {}
        self.readers = {}
        self.n_ps = 0

    def sb(self, name, shape, dt):
        return self.es.enter_context(self.nc.sbuf_tensor("sb_" + name, list(shape), dt))

    def psum(self, name, shape, dt):
        return self.es.enter_context(self.nc.psum_tensor(name, list(shape), dt))

    def capture(self):
        self._cap = []

    def end_capture(self):
        lst, self._cap = self._cap, None
        return lst

    def replay_interleaved(self, lists):
        its = [list(l) for l in lists]
        pos = [0] * len(its)
        left = sum(len(l) for l in its)
        while left:
            for k, l in enumerate(its):
                if pos[k] < len(l):
                    self.add(*l[pos[k]])
                    pos[k] += 1
                    left -= 1

    def add(self, stream, fn, r=(), w=(), kind="cmp"):
        if getattr(self, "_cap", None) is not None:
            self._cap.append((stream, fn, tuple(r), tuple(w), kind))
            return -1
        i = len(self.ops)
        deps = set()
        px = [k for k in r if isinstance(k, str) and k[:2] == "ps" and k[2:].isdigit()]
        if px:
            r = [k for k in r if k not in px]
            w = list(w) + [k for k in px if k not in w]
        for k in r:
            j = self.lastw.get(k)
            if j is not None:
                deps.add(j)
        for k in w:
            j = self.lastw.get(k)
            if j is not None:
                deps.add(j)
            rd = self.readers.get(k)
            if rd:
                deps.update(rd.values())
        for k in r:
            rd = self.readers.setdefault(k, {})
            rk = stream if kind == "cmp" else ("dma", i)
            rd[rk] = i
        for k in w:
            self.lastw[k] = i
            self.readers[k] = {}
        self.ops.append((stream, kind, fn, deps))
        return i

    def dma(self, out, in_, r=(), w=(), q="sp", **kw):
        return self.add(q, lambda e: e.dma_start(out=out, in_=in_, **kw), r, w, kind="dma")

    def mm(self, out, lhsT, rhs, start=True, stop=True, r=(), w=(), **kw):
        return self.add("pe", lambda e: e.matmul(out, lhsT, rhs, start=start, stop=stop, **kw), r, w)

    def tr(self, out, in_, ident, r=(), w=()):
        return self.add("pe", lambda e: e.transpose(out, in_, ident), r, w)

    def act(self, out, in_, func, r=(), w=(), eng="act", **kw):
        return self.add(eng, lambda e: e.activation(out=out, in_=in_, func=func, **kw), r, w)

    def tt(self, out, in0, in1, op, r=(), w=(), eng="dve"):
        return self.add(eng, lambda e: e.tensor_tensor(out=out, in0=in0, in1=in1, op=op), r, w)

    def ts(self, out, in0, s1, op0, s2=None, op1=None, r=(), w=(), eng="dve", **kw):
        if op1 is None:
            return self.add(eng, lambda e: e.tensor_scalar(out=out, in0=in0, scalar1=s1, scalar2=None, op0=op0, **kw), r, w)
        return self.add(eng, lambda e: e.tensor_scalar(out=out, in0=in0, scalar1=s1, scalar2=s2, op0=op0, op1=op1, **kw), r, w)

    def stt(self, out, in0, scalar, in1, op0, op1, r=(), w=(), eng="dve"):
        return self.add(eng, lambda e: e.scalar_tensor_tensor(out=out, in0=in0, scalar=scalar, in1=in1, op0=op0, op1=op1), r, w)

    def cp(self, out, in_, r=(), w=(), eng="dve"):
        return self.add(eng, lambda e: e.tensor_copy(out=out, in_=in_), r, w)

    def memset(self, ap, val, w=(), eng="pool"):
        return self.add(eng, lambda e: e.memset(ap, val), (), w)

    def barrier(self):
        n = len(self.ops)
        last = {}
        dmas = set()
        start = getattr(self, "_bar_from", 0)
        for i in range(start, n):
            st, kind = self.ops[i][0], self.ops[i][1]
            if self.ops[i][2] is None:
                continue
            if kind == "dma":
                dmas.add(i)
            else:
                last[st] = i
        deps = set(last.values()) | dmas
        for st in self.STREAMS:
            self.ops.append((st, "cmp", None, set(deps)))
        self._bar_from = len(self.ops)
        self.lastw = {}
        self.readers = {}

    def emit(self):
        nc = self.nc
        ops = self.ops
        n = len(ops)
        need = [False] * n
        for i, (st, kind, fn, deps) in enumerate(ops):
            for j in deps:
                sj, kj = ops[j][0], ops[j][1]
                if kj == "dma" or kind == "dma" or sj != st or fn is None or st != "pe":
                    need[j] = True
        sems = {}

        def newsem(name):
            return self.es.enter_context(nc.semaphore(name))

        sig = [None] * n
        guard = [None] * n
        cnt = {s: 0 for s in self.STREAMS}
        cur = {}
        epoch = {s: 0 for s in self.STREAMS}
        dcount = {s: 0 for s in self.STREAMS}
        dslots = {}
        for i, (st, kind, fn, deps) in enumerate(ops):
            if kind == "dma":
                if st not in dslots:
                    dslots[st] = [newsem(f"d_{st}_{k}") for k in range(self.NSLOT)]
                k = dcount[st]
                dcount[st] += 1
                slot = k % self.NSLOT
                gen = k // self.NSLOT
                sig[i] = (dslots[st][slot], 16 * (gen + 1))
                guard[i] = (dslots[st][slot], 16 * gen)
            elif need[i]:
                if st not in cur or cnt[st] >= self.MAXV:
                    cur[st] = newsem(f"c_{st}_{epoch[st]}")
                    epoch[st] += 1
                    cnt[st] = 0
                cnt[st] += 1
                sig[i] = (cur[st], cnt[st])
        self.dslots = dslots
        self.dcount = dcount
        by_stream = {s: [] for s in self.STREAMS}
        for i, op in enumerate(ops):
            by_stream[op[0]].append(i)

        def run(stname, e):
            waited = {}

            def wait(sem, val):
                if val <= 0:
                    return
                key = id(sem)
                if waited.get(key, 0) < val:
                    e.wait_ge(sem, val)
                    waited[key] = val

            for i in by_stream[stname]:
                st, kind, fn, deps = ops[i]
                for j in sorted(deps):
                    sj, kj = ops[j][0], ops[j][1]
                    if kj == "cmp" and kind == "cmp" and sj == st and st == "pe" and fn is not None:
                        continue
                    wait(*sig[j])
                if kind == "dma":
                    wait(*guard[i])
                if fn is None:
                    continue
                ins = fn(e)
                if kind == "dma":
                    ins.then_inc(sig[i][0], 16)
                elif need[i]:
                    ins.then_inc(sig[i][0], 1)
            if stname == "sp":
                for q, sl in dslots.items():
                    tot = dcount[q]
                    for s in range(self.NSLOT):
                        ngen = (tot - s + self.NSLOT - 1) // self.NSLOT if tot > s else 0
                        wait(sl[s], 16 * ngen)

        with nc.Block() as block:
            @block.sync
            def _(e):
                run("sp", e)

            @block.tensor
            def _(e):
                run("pe", e)

            @block.scalar
            def _(e):
                run("act", e)

            @block.vector
            def _(e):
                run("dve", e)

            @block.gpsimd
            def _(e):
                run("pool", e)

D = 1024
SEQ = 2048
NT = 16
EPS = 1e-6
TWO_PI = 2.0 * math.pi
MAGIC = 12582912.0
CW1 = 6.28125
CW2 = 0.0019350051879882812
CW3 = TWO_PI - CW1 - CW2


class Ctx:
    pass


class Arena:
    def __init__(self, c):
        self.t = c.arena
        self.n = c.arena_n
        self.off = 0

    def f32(self, shape):
        n = 1
        for d in shape:
            n *= d
        a = self.off
        self.off += n
        assert self.off <= self.n, ("arena overflow", self.off, self.n)
        ap = self.t[:, a:a + n]
        return self._shape(ap, shape)

    def bf16(self, shape):
        n = 1
        for d in shape:
            n *= d
        nw = (n + 1) // 2
        a = self.off
        self.off += nw
        assert self.off <= self.n, ("arena overflow", self.off, self.n)
        ap = self.t[:, a:a + nw].bitcast(BF16)[:, 0:n]
        return self._shape(ap, shape)

    @staticmethod
    def _shape(ap, shape):
        if len(shape) == 1:
            return ap
        if len(shape) == 2:
            return ap.rearrange("p (a b) -> p a b", a=shape[0])
        if len(shape) == 3:
            return ap.rearrange("p (a b c) -> p a b c", a=shape[0], b=shape[1])
        if len(shape) == 4:
            return ap.rearrange("p (a b c d) -> p a b c d", a=shape[0], b=shape[1], c=shape[2])
        raise ValueError(shape)


def cs(c, name):
    a, b = c.cmap[name]
    return c.cst[:, a:b]


def V(c, name, i=None, n=1):
    a, b = c.vmap[name]
    if i is None:
        return c.vec[:, a:b]
    return c.vec[:, a + i:a + i + n]


def setup_common(S_, c, nc):
    c.ps = [S_.psum(f"ps{i}", [128, 512], F32) for i in range(8)]
    c.psn = 0
    c.wn = 0
    c.cst = S_.sb("cst", [128, c.ncst], F32)
    c.vec = S_.sb("vec", [128, c.nvec], F32)
    c.sm = S_.sb("sm", [128, 64], F32)
    S_.dma(c.cst[:], c.d_cst[:, :], w=["cst"])
    S_.dma(c.vec[:], c.d_vec[:, :], w=["vec"])
    c.identf = cs(c, "ident")
    names = ["ident", "U", "ones", "blk64", "rotT", "Lst"]
    c.cbf = S_.sb("cbf", [128, len(names) * 128], BF16)
    c.b = {}
    for i, nm in enumerate(names):
        S_.cp(c.cbf[:, i * 128:(i + 1) * 128], cs(c, nm), r=["cst"], w=["cbf"])
        c.b[nm] = c.cbf[:, i * 128:(i + 1) * 128]
    c.epsc = cs(c, "eps")
    c.memT = S_.sb("memT", [128, c.NSQ, 8, 256], BF16)
    c.KnT = S_.sb("KnT", [128, c.NSQ, 4, 256], BF16)
    c.Vm = S_.sb("Vm", [128, c.NSQ, 2, 512], BF16)
    rem = nc.sbuf_bytes_remaining
    c.arena_n = (rem - 2048) // 4
    c.arena = S_.sb("arena", [128, c.arena_n], F32)


def nextps(c):
    b = c.psn % getattr(c, "nrot", 6)
    c.psn = (c.psn + 1) % getattr(c, "nrot", 6)
    return b


def rstd_from_ss(S_, c, ss_ap, out_ap, n, rkeys, wkey, tmp):
    S_.act(tmp, ss_ap, AF.Sqrt, r=list(rkeys) + ["cst"], w=[wkey + "_t"], scale=1.0 / n, bias=c.epsc)
    S_.add("dve", lambda e: e.reciprocal(out=out_ap, in_=tmp), r=[wkey + "_t"], w=[wkey])


def norm_T(S_, c, src_ap, srckey, dst, dstkey, gbase, tcol, xt2, xs2, junk):
    i = c.nt_i
    c.nt_i ^= 1
    xt, xs = xt2[:, i, :], xs2[:, i, :]
    k, ks = f"xt{i}", f"xs{i}"
    sm0 = 32 + 4 * i
    S_.dma(xt, src_ap, r=srckey, w=[k])
    S_.act(junk, xt, AF.Square, r=[k], w=["junk", f"ssa{i}"], accum_out=c.sm[:, sm0:sm0 + 1])
    rstd_from_ss(S_, c, c.sm[:, sm0:sm0 + 1], c.sm[:, sm0 + 2:sm0 + 3], 1024, [f"ssa{i}"], f"rsa{i}", c.sm[:, sm0 + 1:sm0 + 2])
    S_.act(xs, xt, AF.Copy, r=[k, f"rsa{i}"], w=[ks], scale=c.sm[:, sm0 + 2:sm0 + 3])
    for half in range(2):
        b = nextps(c)
        for j in range(4):
            kc = half * 4 + j
            S_.tr(c.ps[b][:, j * 128:(j + 1) * 128], xs[:, kc * 128:(kc + 1) * 128], c.identf,
                  r=[ks, "cst"], w=[f"ps{b}"])
        a0 = gbase + half * 4
        g = c.vec[:, a0:a0 + 4].unsqueeze(2).to_broadcast([128, 4, 128])
        S_.tt(dst[:, half * 4:half * 4 + 4, tcol:tcol + 128],
              c.ps[b][:, :].rearrange("p (j t) -> p j t", j=4), g, ALU.mult,
              r=[f"ps{b}", "vec"], w=[dstkey])


def phase_a(S_, c, A, xin, s, l):
    hT = A.bf16([8, SEQ])
    xt2 = A.f32([2, 1024])
    xs2 = A.f32([2, 1024])
    junk = A.f32([1024])
    c.nt_i = 0
    for t2 in range(0, NT, 2):
        caps = []
        for tt in (t2, t2 + 1):
            S_.capture()
            norm_T(S_, c, xin[s, tt * 128:(tt + 1) * 128, :], [("X", id(xin), s, tt)], hT, "hT",
                   c.vmap["norm_g"][0] + l * 8, tt * 128, xt2, xs2, junk)
            caps.append(S_.end_capture())
        S_.replay_interleaved(caps)
    return hT


def wtiles(c, A, n=3, kc=16):
    c.wbuf = [A.bf16([kc, 128]) for _ in range(n)]
    c.wn = 0


def load_w(S_, c, dram_blk, kc):
    i = c.wn
    c.wn = (c.wn + 1) % len(c.wbuf)
    wt = c.wbuf[i]
    S_.dma(wt[:, 0:kc, :], dram_blk, r=[], w=[f"w{i}"], q="pool")
    return wt, f"w{i}"


def proj_fm(S_, c, wt, wk, hT, tb, nk=8, hk="hT", mcols=128, ntok=512):
    b = nextps(c)
    for kc in range(nk):
        S_.mm(c.ps[b][0:mcols, 0:ntok], wt[:, kc, 0:mcols], hT[:, kc, tb * ntok:(tb + 1) * ntok],
              start=(kc == 0), stop=(kc == nk - 1), r=[wk, hk], w=[f"ps{b}"])
    return b


def mem_prologue(S_, c):
    A = Arena(c)
    xt2 = A.f32([2, 1024])
    xs2 = A.f32([2, 1024])
    junk = A.f32([1024])
    c.nt_i = 0
    for s in range(c.NSQ):
        for mt in range(2):
            norm_T(S_, c, c.d_mem[s, mt * 128:(mt + 1) * 128, :], [], c.memT[:, s], "memT",
                   c.vmap["mem_norm_g"][0], mt * 128, xt2, xs2, junk)
    S_.barrier()


def xattn_layer_prologue(S_, c, l):
    A = Arena(c)
    wkv = A.bf16([8, 1024])
    kf = A.f32([256])
    sqb = A.bf16([256])
    rsb = A.f32([256])
    tmp = A.f32([256])
    for hf in range(2):
        S_.dma(wkv[:, :, hf * 512:(hf + 1) * 512], c.d_wkv[l][:, :, hf * 512:(hf + 1) * 512], r=[], w=["wkv"], q="pool")
    import os
    XS = int(os.environ.get("XSTOP", "9"))
    for s in range(c.NSQ):
        for h in range(4):
            if XS < 1:
                break
            b = nextps(c)
            for kc in range(8 if os.environ.get("XVAR") != "b" else 0):
                S_.mm(c.ps[b][:, 0:256], wkv[:, kc, h * 128:(h + 1) * 128], c.memT[:, s, kc, :],
                      start=(kc == 0), stop=(kc == 7), r=["wkv", "memT"], w=[f"ps{b}"])
            if os.environ.get("XVAR") != "a":
                S_.cp(kf, c.ps[b][:, 0:256], r=[f"ps{b}"], w=["kf"])
            if XS < 2:
                continue
            S_.act(sqb, c.ps[b][:, 0:256], AF.Square, r=[f"ps{b}"], w=["sqb"])
            b2 = nextps(c)
            S_.mm(c.ps[b2][:, 0:256], c.b["ones"], sqb, r=["cbf", "sqb"], w=[f"ps{b2}"])
            if XS < 3:
                continue
            rstd_from_ss(S_, c, c.ps[b2][:, 0:256], rsb, 128, [f"ps{b2}"], "rsb", tmp)
            if XS < 4:
                continue
            S_.stt(c.KnT[:, s, h, :], kf, V(c, "xk_g", l), rsb, ALU.mult, ALU.mult,
                   r=["kf", "rsb", "vec"], w=["KnT"])
        for mt in range(2):
            if XS < 5:
                break
            b = nextps(c)
            for kc in range(8):
                S_.mm(c.ps[b][:, :], c.memT[:, s, kc, mt * 128:(mt + 1) * 128], wkv[:, kc, 512:1024],
                      start=(kc == 0), stop=(kc == 7), r=["wkv", "memT"], w=[f"ps{b}"])
            S_.cp(c.Vm[:, s, mt, :], c.ps[b][:, :], r=[f"ps{b}"], w=["Vm"])
    S_.barrier()


def xattn_seq(S_, c, A, l, s, hT, win, cb_zx, cb_q, G):
    sc = 128.0 ** -0.5
    XB = [dict(kf=A.f32([512]), sqb=A.bf16([512]), rsb=A.f32([512]), tmp=A.f32([512]), qn=A.bf16([512]),
               sz=A.bf16([512]), pT=A.bf16([2, 512]), gst=A.bf16([2, 512]), wq=A.bf16([8, 128]), wz=A.bf16([8, 128]))
          for _ in range(2)]
    nrot_save = getattr(c, "nrot", 6)
    for h2 in range(0, 4, 2):
        caps = []
        for si in range(2):
            h = h2 + si
            X = XB[si]
            kf, sqb, rsb, tmp, qn, sz, pT, gst, wq, wz = (X[k_] for k_ in ("kf", "sqb", "rsb", "tmp", "qn", "sz", "pT", "gst", "wq", "wz"))
            P_ = f"x{si}_"
            S_.dma(wq, win[cb_q + h], r=[], w=[P_ + "wq"], q="pool")
            S_.dma(wz, win[cb_zx + h], r=[], w=[P_ + "wz"], q="pool")
            bo, bl = (6, 7) if si == 0 else (4, 5)
            bk = (2 * si, 2 * si + 1)
            S_.capture()
            for tb in range(4):
                tsl = slice(tb * 512, (tb + 1) * 512)
                bq = bk[0]
                for kc in range(8):
                    S_.mm(c.ps[bq][:, :], wq[:, kc, :], hT[:, kc, tsl], start=(kc == 0), stop=(kc == 7), r=[P_ + "wq", "hT"], w=[f"ps{bq}"])
                S_.cp(kf, c.ps[bq][:, :], r=[f"ps{bq}"], w=[P_ + "kf"])
                S_.act(sqb, c.ps[bq][:, :], AF.Square, r=[f"ps{bq}"], w=[P_ + "sqb"])
                b2 = bk[1]
                S_.mm(c.ps[b2][:, :], c.b["ones"], sqb, r=["cbf", P_ + "sqb"], w=[f"ps{b2}"])
                rstd_from_ss(S_, c, c.ps[b2][:, :], rsb, 128, [f"ps{b2}"], P_ + "rsb", tmp)
                S_.stt(qn, kf, V(c, "xq_g", l), rsb, ALU.mult, ALU.mult, r=[P_ + "kf", P_ + "rsb", "vec"], w=[P_ + "qn"])
                bz = bk[0]
                for kc in range(8):
                    S_.mm(c.ps[bz][:, :], wz[:, kc, :], hT[:, kc, tsl], start=(kc == 0), stop=(kc == 7), r=[P_ + "wz", "hT"], w=[f"ps{bz}"])
                S_.act(sz, c.ps[bz][:, :], AF.Silu, r=[f"ps{bz}"], w=[P_ + "sz"])
                for mt in range(2):
                    bs = bk[(mt + 1) % 2]
                    S_.mm(c.ps[bs][:, :], c.KnT[:, s, h, mt * 128:(mt + 1) * 128], qn, r=["KnT", P_ + "qn"], w=[f"ps{bs}"])
                    S_.act(pT[:, mt, :], c.ps[bs][:, :], AF.Exp, r=[f"ps{bs}"], w=[P_ + f"pT{mt}"], scale=sc)
                S_.mm(c.ps[bo][:, :], c.Vm[:, s, 0, h * 128:(h + 1) * 128], pT[:, 0, :], start=True, stop=False,
                      r=["Vm", P_ + "pT0"], w=[f"ps{bo}"])
                S_.mm(c.ps[bo][:, :], c.Vm[:, s, 1, h * 128:(h + 1) * 128], pT[:, 1, :], start=False, stop=True,
                      r=["Vm", P_ + "pT1"], w=[f"ps{bo}"])
                S_.mm(c.ps[bl][:, :], c.b["ones"], pT[:, 0, :], start=True, stop=False, r=["cbf", P_ + "pT0"], w=[f"ps{bl}"])
                S_.mm(c.ps[bl][:, :], c.b["ones"], pT[:, 1, :], start=False, stop=True, r=["cbf", P_ + "pT1"], w=[f"ps{bl}"])
                S_.add("dve", lambda e, rsb=rsb, bl=bl: e.reciprocal(out=rsb, in_=c.ps[bl][:, :]), r=[f"ps{bl}"], w=[P_ + "rsb"])
                S_.tt(kf, c.ps[bo][:, :], rsb, ALU.mult, r=[f"ps{bo}", P_ + "rsb"], w=[P_ + "kf"])
                gi = tb % 2
                S_.tt(gst[:, gi, :], kf, sz, ALU.mult, r=[P_ + "kf", P_ + "sz"], w=[P_ + f"gst{gi}"])
                S_.dma(G[s, (16 + h) * 128:(17 + h) * 128, tsl], gst[:, gi, :], r=[P_ + f"gst{gi}"], w=[("G", s, 16 + h)])
            caps.append(S_.end_capture())
        S_.replay_interleaved(caps)
    c.nrot = nrot_save


def phase_c(S_, c, l, s, xin, xout, G):
    A = Arena(c)
    wo = A.bf16([20, 1024])
    gt2 = A.bf16([2, 20, 512])
    xt2 = A.f32([2, 1024])
    xo2 = A.f32([2, 1024])
    for hf in range(2):
        S_.dma(wo[:, :, hf * 512:(hf + 1) * 512], c.d_wout[l][:, :, hf * 512:(hf + 1) * 512], r=[], w=["wo"], q="pool")
    for tb in range(4):
        gi = tb % 2
        gtb, gk = gt2[:, gi], f"gt{gi}"
        S_.dma(gtb, G[s, :, tb * 512:(tb + 1) * 512].rearrange("(k p) t -> p k t", p=128),
               r=[("G", s, ch) for ch in range(20)], w=[gk])
        for t4 in range(4):
            tt = tb * 4 + t4
            i = tt % 2
            xt, k = xt2[:, i, :], f"cxt{i}"
            S_.dma(xt, xin[s, tt * 128:(tt + 1) * 128, :], r=[("X", id(xin), s, tt)], w=[k])
            xo, ko = xo2[:, i, :], f"cxo{i}"
            for hf in range(2):
                b = nextps(c)
                for kc in range(20):
                    S_.mm(c.ps[b][:, :], gtb[:, kc, t4 * 128:(t4 + 1) * 128], wo[:, kc, hf * 512:(hf + 1) * 512],
                          start=(kc == 0), stop=(kc == 19), r=[gk, "wo"], w=[f"ps{b}"])
                S_.tt(xo[:, hf * 512:(hf + 1) * 512], c.ps[b][:, :], xt[:, hf * 512:(hf + 1) * 512], ALU.add,
                      r=[f"ps{b}", k], w=[ko])
            S_.dma(xout[s, tt * 128:(tt + 1) * 128, :], xo, r=[ko], w=[("X", id(xout), s, tt)])
    S_.barrier()

S5_TB = 16


def lambda_bar(S_, c, A, lamr, lami, lstep, n, pfx):
    step = A.f32([n]); xi = A.f32([n]); xr = A.f32([n]); mag = A.f32([n])
    t = A.f32([n]); k = A.f32([n]); y = A.f32([n])
    sn = A.f32([n]); csn = A.f32([n]); lbr = A.f32([n]); lbi = A.f32([n])
    K = lambda s: pfx + s
    S_.act(step, lstep, AF.Exp, r=[K("in")], w=[K("step")])
    S_.tt(xi, lami, step, ALU.mult, r=[K("in"), K("step")], w=[K("xi")])
    S_.tt(xr, lamr, step, ALU.mult, r=[K("in"), K("step")], w=[K("xr")])
    S_.act(mag, xr, AF.Exp, r=[K("xr")], w=[K("mag")])
    for which, shift, dst in (("s", 0.0, sn), ("c", 0.5 * math.pi, csn)):
        S_.ts(t, xi, shift, ALU.add, 1.0 / TWO_PI, ALU.mult, r=[K("xi")], w=[K("t")])
        S_.ts(k, t, MAGIC, ALU.add, -MAGIC, ALU.add, r=[K("t")], w=[K("k")])
        S_.stt(y, k, -TWO_PI, xi, ALU.mult, ALU.add, r=[K("k"), K("xi")], w=[K("y")])
        S_.ts(y, y, shift, ALU.add, -3.14159, ALU.max, r=[K("y")], w=[K("y")])
        S_.ts(y, y, 3.14159, ALU.min, r=[K("y")], w=[K("y")])
        S_.act(dst, y, AF.Sin, r=[K("y")], w=[K(which)])
    S_.tt(lbr, mag, csn, ALU.mult, r=[K("mag"), K("c")], w=[K("lbr")])
    S_.tt(lbi, mag, sn, ALU.mult, r=[K("mag"), K("s")], w=[K("lbi")])
    return lbr, lbi, K("lbr"), K("lbi")


def s5_layer(S_, c, l, xin, xout):
    NSQ = c.NSQ
    TB = S5_TB
    win = c.d_win[l]
    CB_U, CB_ZM, CB_ZX, CB_Q = 0, 16, 32, 36
    U, GS, G = c.d_U, c.d_GS, c.d_G
    for s in range(NSQ):
        A = Arena(c)
        hT = phase_a(S_, c, A, xin, s, l)
        wtiles(c, A, 3, 8)
        ust = A.bf16([2, SEQ])
        for cb in range(16):
            wt, wk = load_w(S_, c, win[CB_U + cb], 8)
            i = cb % 2
            for tb in range(4):
                b = proj_fm(S_, c, wt, wk, hT, tb)
                if tb % 2 == 0:
                    S_.act(ust[:, i, tb * 512:(tb + 1) * 512], c.ps[b][:, :], AF.Copy, r=[f"ps{b}"], w=[f"ust{i}"])
                else:
                    S_.cp(ust[:, i, tb * 512:(tb + 1) * 512], c.ps[b][:, :], r=[f"ps{b}"], w=[f"ust{i}"])
            S_.dma(U[s, cb * 128:(cb + 1) * 128, :], ust[:, i, :], r=[f"ust{i}"], w=[("U", s, cb)])
        S_.barrier()
    if c.stop < 4:
        return
    A = Arena(c)
    sA = A.f32([3, 64])
    S_.dma(sA, c.d_s5A[:, :, :], w=["A_in"])
    Cc = A.bf16([2, 64, 64])
    S_.dma(Cc, c.d_s5C[:, :, :, :], w=["Cc"], q="pool")
    Bpad = A.bf16([16, 2, 2, 128])
    lbrA, lbiA, kra, kia = lambda_bar(S_, c, A, sA[:, 0, :], sA[:, 1, :], sA[:, 2, :], 64, "A_")
    mark = A.off
    sB = A.f32([5, 1024])
    S_.dma(sB, c.d_s5B[:, :, :], w=["B_in"])
    lbrB, lbiB, krb, kib = lambda_bar(S_, c, A, sB[:, 0, :], sB[:, 1, :], sB[:, 2, :], 1024, "B_")
    n = 1024
    nr = A.f32([n]); den = A.f32([n]); t1 = A.f32([n]); t2 = A.f32([n]); cor = A.f32([n]); coi = A.f32([n])
    lamr, lami, bre, bim = sB[:, 0, :], sB[:, 1, :], sB[:, 3, :], sB[:, 4, :]
    S_.ts(nr, lbrB, -1.0, ALU.add, r=[krb], w=["nr"])
    S_.tt(den, lamr, lamr, ALU.mult, r=["B_in"], w=["den"])
    S_.tt(t1, lami, lami, ALU.mult, r=["B_in"], w=["t1"])
    S_.tt(den, den, t1, ALU.add, r=["den", "t1"], w=["den"])
    S_.add("dve", lambda e: e.reciprocal(out=den, in_=den), r=["den"], w=["den"])
    S_.tt(t1, nr, lamr, ALU.mult, r=["nr", "B_in"], w=["t1"])
    S_.tt(t2, lbiB, lami, ALU.mult, r=[kib, "B_in"], w=["t2"])
    S_.tt(t1, t1, t2, ALU.add, r=["t1", "t2"], w=["t1"])
    S_.tt(cor, t1, den, ALU.mult, r=["t1", "den"], w=["cor"])
    S_.tt(t1, lbiB, lamr, ALU.mult, r=[kib, "B_in"], w=["t1"])
    S_.tt(t2, nr, lami, ALU.mult, r=["nr", "B_in"], w=["t2"])
    S_.tt(t1, t1, t2, ALU.subtract, r=["t1", "t2"], w=["t1"])
    S_.tt(coi, t1, den, ALU.mult, r=["t1", "den"], w=["coi"])
    mj = cs(c, "maskq")
    for ri, (a0, a1, op) in enumerate(((bre, bim, ALU.subtract), (bim, bre, ALU.add))):
        S_.tt(t1, cor, a0, ALU.mult, r=["cor", "B_in"], w=["t1"])
        S_.tt(t2, coi, a1, ALU.mult, r=["coi", "B_in"], w=["t2"])
        S_.tt(t1, t1, t2, op, r=["t1", "t2"], w=["t1"])
        for e_ in range(2):
            for j in range(2):
                S_.ts(Bpad[:, :, ri, e_, j * 64:(j + 1) * 64], t1.rearrange("p (k q) -> p k q", k=16),
                      mj[:, 2 * e_ + j:2 * e_ + j + 1], ALU.mult, r=["t1", "cst"], w=["Bpad"])
    S_.barrier()
    if c.stop < 5:
        return
    A.off = mark
    NTK = NSQ * TB
    Dt2 = [A.f32([TB, 2, NSQ, 64]) for _ in range(2)]
    Hc = A.f32([2, NSQ, 64])
    Hb2 = [A.bf16([2, 64, NSQ, TB]) for _ in range(2)]
    UB = 64
    ublk = A.bf16([16, NSQ, UB])
    udk2 = [A.f32([16, NSQ, TB]) for _ in range(3)]
    gst = A.bf16([16, NSQ, UB])
    TA = A.f32([2, NSQ, 64]); TBv = A.f32([2, NSQ, 64])
    lr2 = lbrA.unsqueeze(1).unsqueeze(1).to_broadcast([128, 2, NSQ, 64])
    li2 = lbiA.unsqueeze(1).unsqueeze(1).to_broadcast([128, 2, NSQ, 64])
    lr = lbrA.unsqueeze(1).to_broadcast([128, NSQ, 64])
    li = lbiA.unsqueeze(1).to_broadcast([128, NSQ, 64])
    dsk = V(c, "s5_d").unsqueeze(2).unsqueeze(3).to_broadcast([128, 16, NSQ, TB])
    S_.memset(Hc, 0.0, w=["Hc"], eng="dve")
    BPU = UB // TB
    NBLK = (SEQ // TB) if c.stop > 5 else BPU
    NPB = 512 // NTK

    def stage1(blk):
        p = blk % 2
        p3 = blk % 3
        Dt, udk = Dt2[p], udk2[p3]
        ub, tq = blk // BPU, blk % BPU
        if tq == 0:
            for s in range(NSQ):
                S_.dma(ublk[:, :, s, :], U[s, :, ub * UB:(ub + 1) * UB].rearrange("(k p) t -> p k t", p=128),
                       r=[("U", s, cb) for cb in range(16)], w=["ublk"])
        tsl = slice(tq * TB, (tq + 1) * TB)
        S_.tt(udk, ublk[:, :, :, tsl], dsk, ALU.mult, r=["ublk", "vec"], w=[f"udk{p3}"], eng="pool")
        nch = min(16, NPB // 2)
        for ri in range(2):
            for hq in range(2):
                for g2 in range(16 // nch):
                    b = nextps(c)
                    for c2 in range(nch):
                        ch = nch * g2 + c2
                        for e_ in range(2):
                            col = (c2 * 2 + e_) * NTK
                            S_.mm(c.ps[b][:, col:col + NTK], Bpad[64 * hq:64 * hq + 64, ch, ri, e_, :],
                                  ublk[64 * hq:64 * hq + 64, ch, :, tsl], r=["Bpad", "ublk"], w=[f"ps{b}"])
                    for c2 in range(nch):
                        ch = nch * g2 + c2
                        p0 = 4 * ch + 2 * hq
                        S_.act(Dt[:, :, ri, :, p0:p0 + 2].rearrange("p t s q -> p q s t"),
                               c.ps[b][:, c2 * 2 * NTK:(c2 + 1) * 2 * NTK].rearrange("p (q s t) -> p q s t", q=2, s=NSQ),
                               AF.Copy, r=[f"ps{b}"], w=[f"D{p}"])

    def stage2(blk):
        p = blk % 2
        Dt, Hb = Dt2[p], Hb2[p]
        dk = f"D{p}"
        for t in range(TB):
            X = Hc if t == 0 else Dt[:, t - 1]
            rk = [dk, "Hc", kra, kia]
            S_.tt(TA, X, lr2, ALU.mult, r=rk, w=["tm"])
            S_.tt(TBv, X, li2, ALU.mult, r=rk, w=["tm"])
            S_.tt(Dt[:, t, 0], Dt[:, t, 0], TA[:, 0], ALU.add, r=[dk, "tm"], w=[dk])
            S_.tt(Dt[:, t, 0], Dt[:, t, 0], TBv[:, 1], ALU.subtract, r=[dk, "tm"], w=[dk])
            S_.tt(Dt[:, t, 1], Dt[:, t, 1], TA[:, 1], ALU.add, r=[dk, "tm"], w=[dk])
            S_.tt(Dt[:, t, 1], Dt[:, t, 1], TBv[:, 0], ALU.add, r=[dk, "tm"], w=[dk])
        S_.cp(Hc, Dt[:, TB - 1], r=[dk], w=["Hc"])
        for ri in range(2):
            S_.act(Hb[:, ri], Dt[:, :, ri, :, :].rearrange("p t s q -> p q s t"), AF.Copy,
                   r=[dk], w=[f"Hb{p}"], scale=(1.0 if ri == 0 else -1.0))

    def stage3(blk):
        p = blk % 2
        p3 = blk % 3
        Hb, udk = Hb2[p], udk2[p3]
        ub, tq = blk // BPU, blk % BPU
        tsl = slice(tq * TB, (tq + 1) * TB)
        ncb = min(16, 512 // NTK)
        for cg in range(16 // ncb):
            b = nextps(c)
            for cc in range(ncb):
                ch = cg * ncb + cc
                for q in range(4):
                    pair = ch * 4 + q
                    hq, e_ = q // 2, q % 2
                    for ri in range(2):
                        S_.mm(c.ps[b][64 * hq:64 * hq + 64, cc * NTK:(cc + 1) * NTK], Cc[:, ri, pair, :],
                              Hb[:, ri, pair], start=(e_ == 0 and ri == 0), stop=(e_ == 1 and ri == 1),
                              r=["Cc", f"Hb{p}"], w=[f"ps{b}"])
            S_.tt(udk[:, cg * ncb:(cg + 1) * ncb], c.ps[b][:, 0:ncb * NTK].rearrange("p (k s t) -> p k s t", k=ncb, s=NSQ),
                  udk[:, cg * ncb:(cg + 1) * ncb], ALU.add, r=[f"ps{b}", f"udk{p3}"], w=[f"udk{p3}"])
        S_.act(gst[:, :, :, tsl], udk, AF.Gelu_apprx_tanh, r=[f"udk{p3}"], w=["gst"])
        if tq == BPU - 1:
            for s in range(NSQ):
                S_.dma(GS[s, :, ub * UB:(ub + 1) * UB].rearrange("(k p) t -> p k t", p=128), gst[:, :, s, :],
                       r=["gst"], w=[("GS", s, k_) for k_ in range(16)])

    stage1(0)
    for blk in range(NBLK):
        if blk + 1 < NBLK:
            stage1(blk + 1)
        stage2(blk)
        if blk >= 1:
            stage3(blk - 1)
    stage3(NBLK - 1)
    S_.barrier()
    if c.stop < 7:
        return
    for s in range(NSQ):
        A = Arena(c)
        hT = phase_a(S_, c, A, xin, s, l)
        S_.barrier()
        A.off = SEQ * 8 // 2
        wtiles(c, A, 3, 16)
        gT = A.bf16([16, SEQ])
        for hf in range(2):
            S_.dma(gT[:, hf * 8:(hf + 1) * 8, :], GS[s, hf * 1024:(hf + 1) * 1024, :].rearrange("(k p) t -> p k t", p=128),
                   r=[("GS", s, k_) for k_ in range(16)], w=["gT"])
        sg = A.f32([512]); sz = A.f32([512]); tg = A.f32([512]); gout = A.bf16([2, SEQ])
        for cb in range(16):
            wg, wgk = load_w(S_, c, c.d_wglu[cb], 16)
            wz, wzk = load_w(S_, c, win[CB_ZM + cb], 8)
            i = cb % 2
            for tb in range(4):
                bg = proj_fm(S_, c, wg, wgk, gT, tb, nk=16, hk="gT")
                S_.act(sg, c.ps[bg][:, :], AF.Sigmoid, r=[f"ps{bg}"], w=["sg"])
                bz = proj_fm(S_, c, wz, wzk, hT, tb)
                S_.act(sz, c.ps[bz][:, :], AF.Silu, r=[f"ps{bz}"], w=["sz"])
                S_.tt(tg, gT[:, cb, tb * 512:(tb + 1) * 512], sg, ALU.mult, r=["gT", "sg"], w=["tg"])
                S_.tt(gout[:, i, tb * 512:(tb + 1) * 512], tg, sz, ALU.mult, r=["tg", "sz"], w=[f"gout{i}"])
            S_.dma(G[s, cb * 128:(cb + 1) * 128, :], gout[:, i, :], r=[f"gout{i}"], w=[("G", s, cb)])
        xattn_seq(S_, c, A, l, s, hT, win, CB_ZX, CB_Q, G)
        S_.barrier()
        phase_c(S_, c, l, s, xin, xout, G)

def gla_layer(S_, c, l, xin, xout):
    win = c.d_win[l]
    CB_Q, CB_K, CB_V, CB_GK, CB_ZM, CB_ZX, CB_QM = 0, 4, 8, 24, 25, 41, 45
    G = c.d_G
    Uf, Lf = cs(c, "U"), cs(c, "Lst")
    for s in range(c.NSQ):
        A = Arena(c)
        hT = phase_a(S_, c, A, xin, s, l)
        S_.barrier()
        A.off = SEQ * 8 // 2
        wtiles(c, A, 3, 8)
        wv = A.bf16([8, 512])
        gkT = A.bf16([SEQ])
        wg2 = A.bf16([512])
        qT = A.f32([SEQ]); kT = A.f32([SEQ])
        vtok = A.bf16([16, 512])
        szT = A.bf16([4, SEQ])
        gout = A.bf16([4, SEQ])
        St = A.f32([512]); Sb = A.bf16([512])
        TP = [(A.f32([128]), A.f32([128]), A.f32([128]), A.f32([128]), A.f32([128]),
               A.bf16([128]), A.bf16([128]), A.bf16([128]), A.bf16([128])) for _ in range(2)]
        on = A.bf16([512])
        junk = A.f32([512])
        S_.memset(gkT[0:32, :], 1.0, w=["gkT"], eng="pool")
        S_.dma(wg2[0:17, :], c.d_wgk2[:, :], w=["wg2"], q="pool")
        wt, wk = load_w(S_, c, win[CB_GK], 8)
        for tb in range(4):
            b = proj_fm(S_, c, wt, wk, hT, tb, mcols=16)
            S_.cp(gkT[0:16, tb * 512:(tb + 1) * 512], c.ps[b][0:16, :], r=[f"ps{b}"], w=["gkT"])
        import os
        GS_ = int(os.environ.get("GSTOP", "99"))
        for h in range(4 if GS_ > 0 else 0):
            for nm, cb, dst in (("q", CB_Q + h, qT), ("k", CB_K + h, kT)):
                wt, wk = load_w(S_, c, win[cb], 8)
                for tb in range(4):
                    b = proj_fm(S_, c, wt, wk, hT, tb)
                    S_.act(dst[:, tb * 512:(tb + 1) * 512], c.ps[b][:, :], AF.Copy, r=[f"ps{b}"], w=[nm + "T"])
            for j in range(4):
                S_.dma(wv[:, :, j * 128:(j + 1) * 128], win[CB_V + 4 * h + j], w=["wv"], q="pool")
            for tt in range(NT):
                b = nextps(c)
                for kc in range(8):
                    S_.mm(c.ps[b][:, :], hT[:, kc, tt * 128:(tt + 1) * 128], wv[:, kc, :], start=(kc == 0), stop=(kc == 7),
                          r=["hT", "wv"], w=[f"ps{b}"])
                if tt % 2 == 0:
                    S_.cp(vtok[:, tt, :], c.ps[b][:, :], r=[f"ps{b}"], w=["vtok"])
                else:
                    S_.act(vtok[:, tt, :], c.ps[b][:, :], AF.Copy, r=[f"ps{b}"], w=["vtok"])
            for j in range(4):
                wt, wk = load_w(S_, c, win[CB_ZM + 4 * h + j], 8)
                for tb in range(4):
                    b = proj_fm(S_, c, wt, wk, hT, tb)
                    S_.act(szT[:, j, tb * 512:(tb + 1) * 512], c.ps[b][:, :], AF.Silu, r=[f"ps{b}"], w=["szT"])
            S_.memset(St, 0.0, w=["St"], eng="pool")
            S_.memset(Sb, 0.0, w=["Sb"], eng="pool")
            def front(ck):
                p = ck % 2
                e1, la, eb, enb, ebl, qt, kt, khat, attm = TP[p]
                P_ = str(p)
                bank = [3 * p + (i_ % 3) for i_ in range(5)]
                tsl = slice(ck * 128, (ck + 1) * 128)
                b = bank[0]
                S_.mm(c.ps[b][:, 0:128], gkT[0:17, tsl], wg2[0:17, h * 128:(h + 1) * 128], r=["gkT", "wg2"], w=[f"ps{b}"])
                S_.act(e1, c.ps[b][:, 0:128], AF.Exp, r=[f"ps{b}"], w=["e1" + P_], scale=-1.0)
                S_.act(e1, e1, AF.Ln, r=["e1" + P_], w=["e1" + P_], bias=1.0)
                S_.ts(la, e1, -1.0 / 16.0, ALU.mult, r=["e1" + P_], w=["la" + P_])
                b1 = bank[1]
                S_.mm(c.ps[b1][:, 0:128], la, Uf, r=["la" + P_, "cst"], w=[f"ps{b1}"])
                S_.act(eb, c.ps[b1][:, 0:128], AF.Exp, r=[f"ps{b1}"], w=["eb" + P_])
                S_.act(enb, c.ps[b1][:, 0:128], AF.Exp, r=[f"ps{b1}"], w=["enb" + P_], scale=-1.0)
                b2 = bank[2]
                S_.mm(c.ps[b2][:, 0:128], Lf, la, r=["la" + P_, "cst"], w=[f"ps{b2}"])
                S_.act(ebl, c.ps[b2][:, 0:128], AF.Exp, r=[f"ps{b2}"], w=["ebl" + P_])
                S_.stt(qt, qT[:, tsl], 128.0 ** -0.5, eb, ALU.mult, ALU.mult, r=["qT", "eb" + P_], w=["qt" + P_])
                S_.tt(kt, kT[:, tsl], enb, ALU.mult, r=["kT", "enb" + P_], w=["kt" + P_])
                b3 = bank[3]
                S_.tr(c.ps[b3][:, 0:128], kT[:, tsl], c.identf, r=["kT", "cst"], w=[f"ps{b3}"])
                S_.tt(khat, c.ps[b3][:, 0:128], ebl, ALU.mult, r=[f"ps{b3}", "ebl" + P_], w=["khat" + P_])
                b4 = bank[4]
                S_.mm(c.ps[b4][:, 0:128], kt, qt, r=["kt" + P_, "qt" + P_], w=[f"ps{b4}"])
                S_.tt(attm, c.ps[b4][:, 0:128], Uf, ALU.mult, r=[f"ps{b4}", "cst"], w=["attm" + P_])

            def tail(ck):
                p = ck % 2
                e1, la, eb, enb, ebl, qt, kt, khat, attm = TP[p]
                P_ = str(p)
                tsl = slice(ck * 128, (ck + 1) * 128)
                S_.mm(c.ps[6][:, :], attm, vtok[:, ck, :], start=True, stop=False, r=["attm" + P_, "vtok"], w=["ps6"])
                S_.mm(c.ps[6][:, :], qt, Sb, start=False, stop=True, r=["qt" + P_, "Sb"], w=["ps6"])
                S_.mm(c.ps[7][:, :], khat, vtok[:, ck, :], r=["khat" + P_, "vtok"], w=["ps7"])
                S_.stt(St, St, eb[:, 127:128], c.ps[7][:, :], ALU.mult, ALU.add, r=["St", "eb" + P_, "ps7"], w=["St"])
                S_.cp(Sb, St, r=["St"], w=["Sb"], eng="pool")
                S_.act(junk, c.ps[6][:, :], AF.Square, r=["ps6"], w=["junk", "g_ss"], accum_out=c.sm[:, 8:9])
                rstd_from_ss(S_, c, c.sm[:, 8:9], c.sm[:, 10:11], 512, ["g_ss"], "g_rs", c.sm[:, 9:10])
                S_.act(on, c.ps[6][:, :], AF.Copy, r=["ps6", "g_rs"], w=["on"], scale=c.sm[:, 10:11])
                pb = c.ps[7][:, :].bitcast(BF16)
                for j in range(4):
                    S_.tr(pb[:, j * 128:(j + 1) * 128], on[:, j * 128:(j + 1) * 128], c.b["ident"], r=["on", "cbf"], w=["ps7"])
                for j in range(4):
                    S_.stt(gout[:, j, tsl], pb[:, j * 128:(j + 1) * 128], V(c, "gla_norm_g", j), szT[:, j, tsl],
                           ALU.mult, ALU.mult, r=["ps7", "vec", "szT"], w=["gout"])

            for ck in range(0, NT, 2):
                caps = []
                for d_ in range(2):
                    S_.capture()
                    front(ck + d_)
                    caps.append(S_.end_capture())
                S_.replay_interleaved(caps)
                tail(ck)
                tail(ck + 1)
            for j in range(4):
                chn = h * 4 + j
                S_.dma(G[s, chn * 128:(chn + 1) * 128, :], gout[:, j, :], r=["gout"], w=[("G", s, chn)])
        xattn_seq(S_, c, A, l, s, hT, win, CB_ZX, CB_QM, G)
        S_.barrier()
        phase_c(S_, c, l, s, xin, xout, G)

def sincos_tables(S_, c, A, ang, n, sinT, cosT, pfx):
    t = A.f32([n]); k = A.f32([n]); y = A.f32([n])
    for which, shift, dst in (("s", 0.0, sinT), ("c", 0.5 * math.pi, cosT)):
        S_.ts(t, ang, shift, ALU.add, 1.0 / TWO_PI, ALU.mult, r=[pfx + "ang"], w=[pfx + "t"])
        S_.ts(k, t, MAGIC, ALU.add, -MAGIC, ALU.add, r=[pfx + "t"], w=[pfx + "k"])
        S_.stt(y, k, -TWO_PI, ang, ALU.mult, ALU.add, r=[pfx + "k", pfx + "ang"], w=[pfx + "y"])
        S_.ts(y, y, shift, ALU.add, -3.14159, ALU.max, r=[pfx + "y"], w=[pfx + "y"])
        S_.ts(y, y, 3.14159, ALU.min, r=[pfx + "y"], w=[pfx + "y"])
        S_.act(dst, y, AF.Sin, r=[pfx + "y"], w=[pfx + which])


def diff_layer(S_, c, l, xin, xout):
    win = c.d_win[l]
    CB_Q, CB_K, CB_V, CB_Z, CB_ZX, CB_QM = 0, 16, 32, 48, 64, 68
    G = c.d_G
    lam_init = 0.8 - 0.6 * math.exp(-0.3 * l)
    for s in range(c.NSQ):
        A = Arena(c)
        hT = phase_a(S_, c, A, xin, s, l)
        S_.barrier()
        A.off = SEQ * 8 // 2
        wtiles(c, A, 3, 8)
        c.nrot = 4
        sinT = A.f32([SEQ]); cosT = A.f32([SEQ])
        dmask = A.bf16([4 * 512])
        S_.dma(dmask, c.d_dmask[:, :], w=["dmask"], q="pool")
        mark = A.off
        posi = A.t[:, A.off:A.off + SEQ].bitcast(I32)
        A.off += SEQ
        ang = A.f32([SEQ])
        S_.dma(posi, c.d_pos[s:s + 1, :].to_broadcast([128, SEQ]), w=["posi"])
        S_.cp(ang, posi, r=["posi"], w=["r_ang"])
        S_.ts(ang, ang, cs(c, "invf"), ALU.mult, r=["r_ang", "cst"], w=["r_ang"])
        sincos_tables(S_, c, A, ang, SEQ, sinT, cosT, "r_")
        S_.barrier()
        A.off = mark
        lt = A.f32([64])
        sm = c.sm
        for i, (a_, b_) in enumerate((("diff_lq1", "diff_lk1"), ("diff_lq2", "diff_lk2"))):
            S_.tt(lt, V(c, a_), V(c, b_), ALU.mult, r=["vec"], w=["lt"])
            S_.add("dve", lambda e, i=i: e.reduce_sum(out=sm[:, 16 + i:17 + i], in_=lt, axis=AX.X), r=["lt"], w=[f"lsum{i}"])
            S_.act(sm[:, 18 + i:19 + i], sm[:, 16 + i:17 + i], AF.Exp, r=[f"lsum{i}"], w=[f"lexp{i}"])
        S_.tt(sm[:, 20:21], sm[:, 19:20], sm[:, 18:19], ALU.subtract, r=["lexp0", "lexp1"], w=["nl0"])
        S_.ts(sm[:, 21:22], sm[:, 20:21], -lam_init, ALU.add, r=["nl0"], w=["neglam"])
        S_.ts(sm[:, 22:23], V(c, "diff_subln_g"), 1.0 - lam_init, ALU.mult, r=["vec"], w=["gs"])
        neglam, gs = sm[:, 21:22], sm[:, 22:23]
        qr = A.bf16([SEQ]); kr = A.bf16([SEQ]); vtok = A.bf16([16, 128]); szT = A.bf16([SEQ]); gout = A.bf16([SEQ])
        RT = [(A.f32([512]), A.bf16([512]), A.f32([512]), A.f32([512]), A.bf16([512]), A.f32([512]), A.f32([512])) for _ in range(2)]
        sq = A.bf16([512]); rs = A.f32([512]); tmp = A.f32([512])
        pT = [[A.bf16([512]) for _ in range(2)] for _ in range(2)]
        rl = [A.f32([512]) for _ in range(2)]; on = [A.f32([512]) for _ in range(2)]; dd = A.f32([512]); g1 = A.f32([512])
        for hd in range(16):
            caps = []
            for si, (nm, cb, dst, gname) in enumerate((("q", CB_Q + hd, qr, "diff_q_g"), ("k", CB_K + hd, kr, "diff_k_g"))):
                wt, wk = load_w(S_, c, win[cb], 8)
                S_.capture()
                qf, sq_, rs_, tmp_, qn, t1, t2 = RT[si]
                P_ = str(si)
                bk = (2 * si, 2 * si + 1)
                for tb in range(4):
                    tsl = slice(tb * 512, (tb + 1) * 512)
                    b = bk[0]
                    for kc in range(8):
                        S_.mm(c.ps[b][:, :], wt[:, kc, :], hT[:, kc, tsl], start=(kc == 0), stop=(kc == 7), r=[wk, "hT"], w=[f"ps{b}"])
                    S_.cp(qf, c.ps[b][:, :], r=[f"ps{b}"], w=["qf" + P_])
                    S_.act(sq_, c.ps[b][:, :], AF.Square, r=[f"ps{b}"], w=["sq" + P_])
                    b2 = bk[1]
                    S_.mm(c.ps[b2][:, :], c.b["blk64"], sq_, r=["cbf", "sq" + P_], w=[f"ps{b2}"])
                    rstd_from_ss(S_, c, c.ps[b2][:, :], rs_, 64, [f"ps{b2}"], "rs" + P_, tmp_)
                    S_.stt(qn, qf, V(c, gname), rs_, ALU.mult, ALU.mult, r=["qf" + P_, "rs" + P_, "vec"], w=["qn" + P_])
                    b3 = bk[0]
                    S_.mm(c.ps[b3][:, :], c.b["rotT"], qn, r=["cbf", "qn" + P_], w=[f"ps{b3}"])
                    S_.tt(t1, qn, cosT[:, tsl], ALU.mult, r=["qn" + P_, "r_c"], w=["t1" + P_], eng="pool")
                    S_.tt(t2, c.ps[b3][:, :], sinT[:, tsl], ALU.mult, r=[f"ps{b3}", "r_s"], w=["t2" + P_])
                    S_.tt(dst[:, tsl], t1, t2, ALU.add, r=["t1" + P_, "t2" + P_], w=[nm + "r"])
                caps.append(S_.end_capture())
            S_.replay_interleaved(caps)
            wt, wk = load_w(S_, c, win[CB_V + hd], 8)
            for tg in range(4):
                b = nextps(c)
                for j in range(4):
                    tt = tg * 4 + j
                    for kc in range(8):
                        S_.mm(c.ps[b][:, j * 128:(j + 1) * 128], hT[:, kc, tt * 128:(tt + 1) * 128], wt[:, kc, :],
                              start=(kc == 0), stop=(kc == 7), r=["hT", wk], w=[f"ps{b}"])
                S_.cp(vtok[:, tg * 4:tg * 4 + 4, :], c.ps[b][:, :].rearrange("p (j v) -> p j v", j=4), r=[f"ps{b}"], w=["vtok"])
            wt, wk = load_w(S_, c, win[CB_Z + hd], 8)
            for tb in range(4):
                b = proj_fm(S_, c, wt, wk, hT, tb)
                S_.act(szT[:, tb * 512:(tb + 1) * 512], c.ps[b][:, :], AF.Silu, r=[f"ps{b}"], w=["szT"])
            for qb in range(4):
                qsl = slice(qb * 512, (qb + 1) * 512)
                caps = []
                for cp_ in range(2):
                    S_.capture()
                    r0 = 64 * cp_
                    nkt = 4 * qb + 4
                    bo, bl = (6, 7) if cp_ == 0 else (4, 5)

                    def st_mm(kt, cp_=cp_):
                        b_ = 2 * cp_ + (kt % 2)
                        S_.mm(c.ps[b_][:, :], kr[r0:r0 + 64, kt * 128:(kt + 1) * 128], qr[r0:r0 + 64, qsl],
                              r=["kr", "qr"], w=[f"ps{b_}"])
                        return b_
                    bnext = st_mm(0)
                    for kt in range(nkt):
                        b_ = bnext
                        if kt + 1 < nkt:
                            bnext = st_mm(kt + 1)
                        p_ = pT[cp_][kt % 2]; pk = f"pT{cp_}{kt % 2}"
                        S_.act(p_, c.ps[b_][:, :], AF.Exp, r=[f"ps{b_}"], w=[pk], scale=0.125)
                        if kt >= 4 * qb:
                            j = kt - 4 * qb
                            S_.tt(p_, p_, dmask[:, j * 512:(j + 1) * 512], ALU.mult, r=[pk, "dmask"], w=[pk], eng="pool")
                        S_.mm(c.ps[bo][:, :], vtok[:, kt, :], p_, start=(kt == 0), stop=(kt == nkt - 1), r=["vtok", pk], w=[f"ps{bo}"])
                        S_.mm(c.ps[bl][:, :], c.b["ones"], p_, start=(kt == 0), stop=(kt == nkt - 1), r=["cbf", pk], w=[f"ps{bl}"])
                    S_.add("dve", lambda e, cp_=cp_, bl=bl: e.reciprocal(out=rl[cp_], in_=c.ps[bl][:, :]), r=[f"ps{bl}"], w=[f"rl{cp_}"])
                    S_.tt(on[cp_], c.ps[bo][:, :], rl[cp_], ALU.mult, r=[f"ps{bo}", f"rl{cp_}"], w=[f"on{cp_}"])
                    caps.append(S_.end_capture())
                S_.replay_interleaved(caps)
                S_.stt(dd, on[1], neglam, on[0], ALU.mult, ALU.add, r=["on0", "on1", "neglam"], w=["dd"])
                S_.act(sq, dd, AF.Square, r=["dd"], w=["sq"])
                b2 = nextps(c)
                S_.mm(c.ps[b2][:, :], c.b["ones"], sq, r=["cbf", "sq"], w=[f"ps{b2}"])
                rstd_from_ss(S_, c, c.ps[b2][:, :], rs, 128, [f"ps{b2}"], "rs", tmp)
                S_.stt(g1, dd, gs, rs, ALU.mult, ALU.mult, r=["dd", "gs", "rs"], w=["g1"])
                S_.tt(gout[:, qsl], g1, szT[:, qsl], ALU.mult, r=["g1", "szT"], w=["gout"])
            S_.dma(G[s, hd * 128:(hd + 1) * 128, :], gout, r=["gout"], w=[("G", s, hd)])
        c.nrot = 6
        xattn_seq(S_, c, A, l, s, hT, win, CB_ZX, CB_QM, G)
        S_.barrier()
        phase_c(S_, c, l, s, xin, xout, G)

def ssd_layer(S_, c, l, xin, xout):
    win = c.d_win[l]
    CB_XS, CB_B, CB_C, CB_DT, CB_ZM, CB_ZX, CB_QM = 0, 16, 24, 32, 33, 49, 53
    G = c.d_G
    Uf, Of = cs(c, "U"), cs(c, "ones")
    for s in range(c.NSQ):
        A = Arena(c)
        hT = phase_a(S_, c, A, xin, s, l)
        S_.barrier()
        A.off = SEQ * 8 // 2
        wtiles(c, A, 3, 8)
        mark0 = A.off
        yz = A.bf16([16, SEQ])
        dt = A.f32([16, 32]); dta = A.f32([16, 32]); cumT = A.f32([16, 32]); lastT = A.f32([16, 32])
        elast = A.f32([16, 32]); ws = A.f32([16, 32]); aneg = A.f32([32])
        wt, wk = load_w(S_, c, win[CB_DT], 8)
        for ck in range(NT):
            for kc in range(8):
                S_.mm(c.ps[6][:, ck * 32:(ck + 1) * 32], hT[:, kc, ck * 128:(ck + 1) * 128], wt[:, kc, 0:32],
                      start=(kc == 0), stop=(kc == 7), r=["hT", wk], w=["ps6"])
        bias = V(c, "ssd_dt_bias").unsqueeze(1).to_broadcast([128, 16, 32])
        S_.tt(dt, c.ps[6][:, :].rearrange("p (k h) -> p k h", k=16), bias, ALU.add, r=["ps6", "vec"], w=["dt"])
        S_.act(dt, dt, AF.Exp, r=["dt"], w=["dt"])
        S_.act(dt, dt, AF.Ln, r=["dt"], w=["dt"], bias=1.0)
        S_.act(aneg, V(c, "ssd_a_log"), AF.Exp, r=["vec"], w=["aneg"])
        S_.ts(aneg, aneg, -1.0, ALU.mult, r=["aneg"], w=["aneg"])
        S_.tt(dta, dt, aneg.unsqueeze(1).to_broadcast([128, 16, 32]), ALU.mult, r=["dt", "aneg"], w=["dta"])
        for ck in range(NT):
            S_.mm(c.ps[6][:, ck * 32:(ck + 1) * 32], Uf, dta[:, ck, :], r=["cst", "dta"], w=["ps6"])
            S_.mm(c.ps[7][:, ck * 32:(ck + 1) * 32], Of, dta[:, ck, :], r=["cst", "dta"], w=["ps7"])
        S_.cp(cumT, c.ps[6][:, :].rearrange("p (k h) -> p k h", k=16), r=["ps6"], w=["cumT"])
        S_.cp(lastT, c.ps[7][:, :].rearrange("p (k h) -> p k h", k=16), r=["ps7"], w=["lastT"])
        S_.act(elast, lastT, AF.Exp, r=["lastT"], w=["elast"])
        S_.tt(ws, lastT, cumT, ALU.subtract, r=["lastT", "cumT"], w=["ws"])
        S_.act(ws, ws, AF.Exp, r=["ws"], w=["ws"])
        S_.tt(ws, ws, dt, ALU.mult, r=["ws", "dt"], w=["ws"])
        xsT = A.bf16([2, SEQ]); BT = A.bf16([SEQ]); CT = A.bf16([SEQ]); szT = A.bf16([2, SEQ])
        St = A.f32([4, 64]); Sb = A.bf16([4, 64])
        mk_ = A.off
        raw = A.f32([SEQ + 4]); acc = A.f32([SEQ])
        end1 = A.off
        A.off = mk_
        TB_ = [dict(xtok=A.bf16([256]), btok=A.bf16([128]), CBm=A.f32([128]), CBd=A.f32([4, 128]), Z=A.f32([4, 128]),
                    dm=A.f32([4, 128]), mm=A.bf16([4, 128]), ecr=A.f32([4, 128]), Cs=A.bf16([4, 128]), xw=A.bf16([4, 64]),
                    yf=A.f32([2, 128])) for _ in range(2)]
        A.off = max(A.off, end1)
        Ub = Uf.unsqueeze(1).to_broadcast([128, 4, 128])
        for g in range(8):
            S_.barrier()
            S_.memset(raw[:, 0:4], 0.0, w=["raw"], eng="pool")
            srcs = [(CB_XS + 2 * g, xsT[:, 0, :], 2 * g), (CB_XS + 2 * g + 1, xsT[:, 1, :], 2 * g + 1),
                    (CB_B + g, BT, 16 + g), (CB_C + g, CT, 24 + g)]
            for cb, dst, cch in srcs:
                wt, wk = load_w(S_, c, win[cb], 8)
                for tb in range(4):
                    b = proj_fm(S_, c, wt, wk, hT, tb)
                    S_.act(raw[:, 3 + tb * 512:3 + (tb + 1) * 512], c.ps[b][:, :], AF.Copy, r=[f"ps{b}"], w=["raw"])
                cw = lambda j: V(c, "ssd_conv_w", cch * 4 + j)
                S_.ts(acc, raw[:, 3:3 + SEQ], cw(3), ALU.mult, V(c, "ssd_conv_b", cch), ALU.add, r=["raw", "vec"], w=["acc"], eng="pool")
                for j in range(3):
                    S_.stt(acc, raw[:, j:j + SEQ], cw(j), acc, ALU.mult, ALU.add, r=["raw", "vec", "acc"], w=["acc"])
                S_.act(dst, acc, AF.Silu, r=["acc"], w=["xbc"])
            for i in range(2):
                wt, wk = load_w(S_, c, win[CB_ZM + 2 * g + i], 8)
                for tb in range(4):
                    b = proj_fm(S_, c, wt, wk, hT, tb)
                    S_.act(szT[:, i, tb * 512:(tb + 1) * 512], c.ps[b][:, :], AF.Silu, r=[f"ps{b}"], w=["szT"])
            S_.barrier()
            S_.memset(St, 0.0, w=["St"], eng="pool")
            S_.memset(Sb, 0.0, w=["Sb"], eng="pool")
            h0 = 4 * g
            def front(ck):
                    tsl = slice(ck * 128, (ck + 1) * 128)
                    T_ = TB_[ck % 2]
                    xtok, btok, CBm, CBd, Z, dm, mm_, ecr, Cs, xw, yf = (T_[k_] for k_ in
                                                                         ("xtok", "btok", "CBm", "CBd", "Z", "dm", "mm", "ecr", "Cs", "xw", "yf"))
                    P_ = str(ck % 2)
                    b = nextps(c)
                    pb = c.ps[b][:, :].bitcast(BF16)
                    for i in range(2):
                        S_.tr(pb[:, i * 128:(i + 1) * 128], xsT[:, i, tsl], c.b["ident"], r=["xbc", "cbf"], w=[f"ps{b}"])
                    S_.tr(pb[:, 256:384], BT[:, tsl], c.b["ident"], r=["xbc", "cbf"], w=[f"ps{b}"])
                    S_.act(xtok, pb[:, 0:256], AF.Copy, r=[f"ps{b}"], w=["xtok" + P_])
                    S_.act(btok, pb[:, 256:384], AF.Copy, r=[f"ps{b}"], w=["btok" + P_])
                    b1 = nextps(c)
                    S_.mm(c.ps[b1][:, 0:128], BT[:, tsl], CT[:, tsl], r=["xbc"], w=[f"ps{b1}"])
                    S_.tt(CBm, c.ps[b1][:, 0:128], Uf, ALU.mult, r=[f"ps{b1}", "cst"], w=["CBm" + P_])
                    dtb = dt[:, ck, h0:h0 + 4].unsqueeze(2).to_broadcast([128, 4, 128])
                    S_.tt(CBd, CBm.unsqueeze(1).to_broadcast([128, 4, 128]), dtb, ALU.mult, r=["CBm" + P_, "dt"], w=["CBd" + P_], eng="pool")
                    S_.tt(Z, Ub, dta[:, ck, h0:h0 + 4].unsqueeze(2).to_broadcast([128, 4, 128]), ALU.mult,
                          r=["cst", "dta"], w=["Z" + P_], eng="pool")
                    b2 = nextps(c)
                    S_.mm(c.ps[b2][:, :], Of, Z.rearrange("p r t -> p (r t)"), r=["cst", "Z" + P_], w=[f"ps{b2}"])
                    psv = c.ps[b2][:, :].rearrange("p (r t) -> p r t", r=4)
                    S_.act(ecr, psv, AF.Exp, r=[f"ps{b2}"], w=["ecr" + P_])
                    S_.tt(dm, psv, cumT[:, ck, h0:h0 + 4].unsqueeze(2).to_broadcast([128, 4, 128]), ALU.subtract,
                          r=[f"ps{b2}", "cumT"], w=["dm" + P_])
                    S_.ts(dm, dm, 0.0, ALU.min, r=["dm" + P_], w=["dm" + P_])
                    S_.act(dm, dm, AF.Exp, r=["dm" + P_], w=["dm" + P_])
                    S_.tt(mm_, dm, CBd, ALU.mult, r=["dm" + P_, "CBd" + P_], w=["mm" + P_])
                    S_.tt(Cs, CT[:, tsl].unsqueeze(1).to_broadcast([128, 4, 128]), ecr, ALU.mult, r=["xbc", "ecr" + P_], w=["Cs" + P_], eng="pool")
                    S_.tt(xw, xtok.rearrange("p (r q) -> p r q", r=4), ws[:, ck, h0:h0 + 4].unsqueeze(2).to_broadcast([128, 4, 64]),
                          ALU.mult, r=["xtok" + P_, "ws"], w=["xw" + P_], eng="pool")
            def tail(ck):
                    tsl = slice(ck * 128, (ck + 1) * 128)
                    T_ = TB_[ck % 2]
                    xtok, btok, CBm, CBd, Z, dm, mm_, ecr, Cs, xw, yf = (T_[k_] for k_ in
                                                                         ("xtok", "btok", "CBm", "CBd", "Z", "dm", "mm", "ecr", "Cs", "xw", "yf"))
                    P_ = str(ck % 2)
                    by = 7
                    b3 = 6
                    for r_ in range(4):
                        i, po = r_ // 2, 64 * (r_ % 2)
                        S_.mm(c.ps[by][po:po + 64, i * 128:(i + 1) * 128], xtok[:, r_ * 64:(r_ + 1) * 64], mm_[:, r_, :], start=True, stop=False,
                              r=["xtok" + P_, "mm" + P_], w=[f"ps{by}"])
                        S_.mm(c.ps[by][po:po + 64, i * 128:(i + 1) * 128], Sb[:, r_, :], Cs[:, r_, :], start=False, stop=True,
                              r=["Sb", "Cs" + P_], w=[f"ps{by}"])
                    for r_ in range(4):
                        S_.mm(c.ps[b3][:, r_ * 64:(r_ + 1) * 64], btok, xw[:, r_, :], r=["btok" + P_, "xw" + P_], w=[f"ps{b3}"])
                    S_.tt(St, St, elast[:, ck, h0:h0 + 4].unsqueeze(2).to_broadcast([128, 4, 64]), ALU.mult, r=["St", "elast"], w=["St"])
                    S_.tt(St, St, c.ps[b3][:, 0:256].rearrange("p (r q) -> p r q", r=4), ALU.add, r=["St", f"ps{b3}"], w=["St"])
                    S_.act(Sb, St, AF.Copy, r=["St"], w=["Sb"])
                    for i in range(2):
                        S_.stt(yf[:, i, :], xsT[:, i, tsl], V(c, "ssd_d", 2 * g + i), c.ps[by][:, i * 128:(i + 1) * 128], ALU.mult, ALU.add,
                               r=["xbc", "vec", f"ps{by}"], w=["yf" + P_])
                        S_.tt(yz[:, 2 * g + i, tsl], yf[:, i, :], szT[:, i, tsl], ALU.mult, r=["yf" + P_, "szT"], w=["yz"])
            for ck in range(0, NT, 2):
                caps = []
                for d_ in range(2):
                    S_.capture()
                    front(ck + d_)
                    caps.append(S_.end_capture())
                S_.replay_interleaved(caps)
                tail(ck)
                tail(ck + 1)
        S_.barrier()
        A.off = mark0 + 16 * SEQ // 2
        sq = A.bf16([512]); rs = A.f32([512]); tmp = A.f32([512]); gst = A.bf16([2, 512])
        for tb in range(4):
            tsl = slice(tb * 512, (tb + 1) * 512)
            for cc in range(16):
                S_.act(sq, yz[:, cc, tsl], AF.Square, r=["yz"], w=["sq"])
                S_.mm(c.ps[6][:, :], c.b["ones"], sq, start=(cc == 0), stop=(cc == 15), r=["cbf", "sq"], w=["ps6"])
            rstd_from_ss(S_, c, c.ps[6][:, :], rs, 2048, ["ps6"], "rs", tmp)
            for cc in range(16):
                i = cc % 2
                S_.stt(gst[:, i, :], yz[:, cc, tsl], V(c, "ssd_norm_g", cc), rs, ALU.mult, ALU.mult, r=["yz", "vec", "rs"], w=[f"gst{i}"])
                S_.dma(G[s, cc * 128:(cc + 1) * 128, tsl], gst[:, i, :], r=[f"gst{i}"], w=[("G", s, cc)])
        xattn_seq(S_, c, A, l, s, hT, win, CB_ZX, CB_QM, G)
        S_.barrier()
        phase_c(S_, c, l, s, xin, xout, G)

LAYER_FN = {}


def build_nc(layers, NSQ, meta):
    nc = bass.Bass("TRN2", target_bir_lowering=False)
    c = Ctx()
    c.NSQ = NSQ
    c.cmap, c.ncst = meta["cmap"], meta["ncst"]
    c.vmap, c.nvec = meta["vmap"], meta["nvec"]

    def din(name, shape, dt=F32):
        return nc.dram_tensor(name, list(shape), dt, kind="ExternalInput").ap()

    def dscr(name, shape, dt):
        return nc.dram_tensor(name, list(shape), dt, kind="Internal").ap()

    x = din("x", [NSQ, SEQ, D])
    c.d_mem = din("mem", [NSQ, 256, D])
    c.d_pos = din("pos", [NSQ, SEQ], I32)
    c.d_cst = din("cst", [128, c.ncst])
    c.d_vec = din("vec", [128, c.nvec])
    c.d_win, c.d_wout, c.d_wkv = {}, {}, {}
    for l in layers:
        c.d_win[l] = din(f"win{l}", [meta["ncb"][l], 128, 8, 128])
        c.d_wout[l] = din(f"wout{l}", [128, 20, 1024])
        c.d_wkv[l] = din(f"wkv{l}", [128, 8, 1024])
    for nm, shp in meta["extra"].items():
        if int(nm[1]) in layers or nm[0] != "L":
            setattr(c, "d_" + nm[3:], din(nm, shp))
    y = nc.dram_tensor("y", [NSQ, SEQ, D], F32, kind="ExternalOutput").ap()
    c.d_G = dscr("G", [NSQ, 2560, SEQ], BF16)
    if 0 in layers:
        c.d_U = dscr("U", [NSQ, 2048, SEQ], BF16)
        c.d_GS = dscr("GS", [NSQ, 2048, SEQ], BF16)
    xs = [x]
    for i in range(len(layers) - 1):
        xs.append(dscr(f"xs{i}", [NSQ, SEQ, D], F32))
    xs.append(y)
    with ExitStack() as es:
        S_ = Sched(nc, es)
        setup_common(S_, c, nc)
        c.stop = meta.get("stop", 99)
        mem_prologue(S_, c)
        for i, l in enumerate(layers):
            if c.stop < 2:
                break
            xattn_layer_prologue(S_, c, l)
            if c.stop < 3:
                break
            LAYER_FN[l](S_, c, l, xs[i], xs[i + 1])
        print("ops recorded:", len(S_.ops), "arena words:", c.arena_n, flush=True)
        S_.emit()
    return nc


def _blk(W, c0, ncols):
    nb = (ncols + 127) // 128
    K = W.shape[0] // 128
    Wp = np.zeros((W.shape[0], nb * 128), np.float32)
    Wp[:, :ncols] = W[:, c0:c0 + ncols]
    return np.ascontiguousarray(Wp.reshape(K, 128, nb, 128).transpose(2, 1, 0, 3))


def host_prep(inp):
    f = lambda a: np.asarray(a, np.float32)
    meta = {"ncb": {}, "extra": {}}
    shared = {}
    cols = []
    cmap = {}

    def addc(name, arr):
        a = sum(x.shape[1] for x in cols)
        cols.append(arr.astype(np.float32))
        cmap[name] = (a, a + arr.shape[1])

    i128 = np.arange(128)
    addc("ident", np.eye(128))
    addc("U", (i128[:, None] <= i128[None, :]).astype(np.float32))
    addc("ones", np.ones((128, 128)))
    blk = np.zeros((128, 128)); blk[:64, :64] = 1; blk[64:, 64:] = 1
    addc("blk64", blk)
    rot = np.zeros((128, 128))
    for cpt in range(2):
        for d in range(64):
            if d < 32:
                rot[cpt * 64 + d + 32, cpt * 64 + d] = -1.0
            else:
                rot[cpt * 64 + d - 32, cpt * 64 + d] = 1.0
    addc("rotT", rot)
    addc("Lst", (i128[:, None] > i128[None, :]).astype(np.float32))
    addc("eps", np.full((128, 1), EPS))
    addc("maskq", np.stack([((i128 // 16) % 4 == j) for j in range(4)], 1).astype(np.float32))
    inv = 10000.0 ** (-np.arange(0, 64, 2, dtype=np.float32) / 64)
    addc("invf", inv[(i128 % 64) % 32][:, None])
    shared["cst"] = np.ascontiguousarray(np.concatenate(cols, 1))
    meta["cmap"], meta["ncst"] = cmap, shared["cst"].shape[1]
    vcols = []
    vmap = {}

    def addv(name, arr):
        a = sum(x.shape[1] for x in vcols)
        vcols.append(np.asarray(arr, np.float32))
        vmap[name] = (a, a + arr.shape[1])

    addv("norm_g", f(inp["norm_g"]).reshape(4, 8, 128).transpose(2, 0, 1).reshape(128, 32))
    addv("mem_norm_g", f(inp["mem_norm_g"]).reshape(8, 128).T)
    addv("xq_g", f(inp["xq_g"]).T)
    addv("xk_g", f(inp["xk_g"]).T)
    addv("s5_d", f(inp["s5_d"])[0].reshape(16, 128).T)
    addv("gla_norm_g", f(inp["gla_norm_g"])[0].reshape(4, 128).T)
    addv("diff_q_g", np.tile(f(inp["diff_q_g"])[0], 2)[:, None])
    addv("diff_k_g", np.tile(f(inp["diff_k_g"])[0], 2)[:, None])
    addv("diff_subln_g", f(inp["diff_subln_g"])[0][:, None])
    for nm in ("diff_lq1", "diff_lk1", "diff_lq2", "diff_lk2"):
        addv(nm, np.tile(f(inp[nm])[0][None, :], (128, 1)))
    addv("ssd_conv_w", f(inp["ssd_conv_w"])[0].reshape(4, 32, 128).transpose(2, 1, 0).reshape(128, 128))
    addv("ssd_conv_b", f(inp["ssd_conv_b"])[0].reshape(32, 128).T)
    addv("ssd_d", np.repeat(f(inp["ssd_d"])[0], 64).reshape(16, 128).T)
    addv("ssd_norm_g", f(inp["ssd_norm_g"])[0].reshape(16, 128).T)
    addv("ssd_dt_bias", np.tile(f(inp["ssd_dt_bias"])[0][None, :], (128, 1)))
    addv("ssd_a_log", np.tile(f(inp["ssd_a_log"])[0][None, :], (128, 1)))
    shared["vec"] = np.ascontiguousarray(np.concatenate(vcols, 1))
    meta["vmap"], meta["nvec"] = vmap, shared["vec"].shape[1]
    wins = {0: f(inp["s5_w_in"])[0], 1: f(inp["gla_w_in"])[0], 2: f(inp["diff_w_in"])[0], 3: f(inp["ssd_w_in"])[0]}
    shared["win0"] = _blk(wins[0], 0, 5120)
    W = wins[1]
    shared["win1"] = np.concatenate([_blk(W, 0, 3072), _blk(W, 3072, 16), _blk(W, 3088, 3072)], 0)
    shared["win2"] = _blk(wins[2], 0, 9216)
    W = wins[3]
    shared["win3"] = np.concatenate([_blk(W, 0, 4096), _blk(W, 4096, 32), _blk(W, 4128, 3072)], 0)
    for l in range(4):
        meta["ncb"][l] = shared[f"win{l}"].shape[0]
        shared[f"wout{l}"] = np.ascontiguousarray(f(inp["w_out"])[l].reshape(20, 128, 1024).transpose(1, 0, 2))
        shared[f"wkv{l}"] = np.ascontiguousarray(f(inp["w_mem_kv"])[l].reshape(8, 128, 1024).transpose(1, 0, 2))
    ex = {}
    ex["L0_wglu"] = np.ascontiguousarray(f(inp["s5_w_glu"])[0].reshape(16, 128, 16, 128).transpose(2, 1, 0, 3))
    lam_re, lam_im, ls = f(inp["s5_lam_re"])[0], f(inp["s5_lam_im"])[0], f(inp["s5_log_step"])[0]
    toA = lambda a: a.reshape(64, 2, 64).transpose(1, 2, 0).reshape(128, 64)
    lsf = np.repeat(ls[:, None], 64, 1)
    ex["L0_s5A"] = np.ascontiguousarray(np.stack([toA(lam_re), toA(lam_im), toA(lsf)], 1))
    toBl = lambda a: np.repeat(a.reshape(16, 8, 1, 64), 16, 2).transpose(1, 2, 0, 3).reshape(128, 1024)
    toBb = lambda b: b.reshape(16, 8, 64, 16).transpose(1, 3, 0, 2).reshape(128, 1024)
    ex["L0_s5B"] = np.ascontiguousarray(np.stack([toBl(lam_re), toBl(lam_im), toBl(lsf),
                                                    toBb(f(inp["s5_b_re"])[0]), toBb(f(inp["s5_b_im"])[0])], 1))
    Cc = np.zeros((2, 64, 2, 32, 2, 2, 2, 16), np.float32)
    for ri, nm in enumerate(("s5_c_re", "s5_c_im")):
        Cm = f(inp[nm])[0].reshape(32, 2, 2, 16, 64)
        for j in range(2):
            for e_ in range(2):
                Cc[j, :, ri, :, e_, e_, j, :] = Cm[:, e_, j].transpose(2, 0, 1)
    ex["L0_s5C"] = np.ascontiguousarray(Cc.reshape(128, 2, 64, 64))
    wg2 = np.zeros((17, 512), np.float32)
    wg2[:16] = f(inp["gla_w_gk2"])[0]
    wg2[16] = f(inp["gla_b_gk2"])[0]
    ex["L1_wgk2"] = wg2
    qi = np.arange(512)[None, :]
    ex["L2_dmask"] = np.concatenate([((128 * j + np.arange(128)[:, None]) <= qi).astype(np.float32) for j in range(4)], 1)
    for k_, v_ in ex.items():
        meta["extra"][k_] = list(v_.shape)
        shared[k_] = v_
    return shared, meta


def run_layers(inp, layers, NSQ, ncores, xin=None, stop=99, trace=False):
    shared, meta = host_prep(inp)
    meta["stop"] = stop
    nc = build_nc(layers, NSQ, meta)
    x = np.asarray(inp["x"], np.float32) if xin is None else xin
    mem = np.asarray(inp["mem"], np.float32)
    pos = np.asarray(inp["positions"], np.int32)
    names = ["cst", "vec"] + [f"{p}{l}" for l in layers for p in ("win", "wout", "wkv")]
    names += [k_ for k_ in meta["extra"] if int(k_[1]) in layers]
    in_maps = []
    for ci in range(ncores):
        sl = slice(ci * NSQ, (ci + 1) * NSQ)
        m = {"x": np.ascontiguousarray(x[sl]), "mem": np.ascontiguousarray(mem[sl]), "pos": np.ascontiguousarray(pos[sl])}
        for nm in names:
            m[nm] = shared[nm]
        in_maps.append(m)
    res = run_bass_kernel_spmd(nc, in_maps, core_ids=list(range(ncores)), **({"trace": True} if trace else {}))
    if trace:
        print("EXEC_NS", layers, res.exec_time_ns, flush=True)
    return np.concatenate([r["y"] for r in res.results], 0)


def kernel(**inputs):
    return run_layers(inputs, [0, 1, 2, 3], 4, 8)

LAYER_FN[0] = s5_layer
LAYER_FN[1] = gla_layer
LAYER_FN[2] = diff_layer
LAYER_FN[3] = ssd_layer
```
